# Optimizing a Trainium2 kernel written in Bass

```python
import jax, jax.numpy as jnp
from jax import lax
import numpy as np

D_MODEL = 1024
BATCH = 32
SEQ = 2048
DEPTH = 1

PLE_DIM = 256
EPS = 1e-6
GMLP_WIDTH = 1024
GMLP_GROUPS = 4
GMLP_GROUP_DIM = GMLP_WIDTH // GMLP_GROUPS
GMLP_CHUNK = 128
HGRN_HEADS = 8
HGRN_KEY_DIM = 128
HGRN_VAL_DIM = 128
HGRN_FORGET_WIDTH = HGRN_HEADS * HGRN_KEY_DIM
HGRN_WIDTH = HGRN_HEADS * HGRN_VAL_DIM
HGRN_CHUNK = 32
COL_SIZES = (GMLP_WIDTH, GMLP_WIDTH, GMLP_WIDTH, HGRN_FORGET_WIDTH, HGRN_FORGET_WIDTH, HGRN_WIDTH, HGRN_WIDTH, D_MODEL, D_MODEL)
IN_COLS = sum(COL_SIZES)
SPLIT_POINTS = tuple(int(c) for c in np.cumsum(COL_SIZES)[:-1])

kernel_name = "hybrid_gmlp_hgrn2_gated_block"


def rms_norm(x, g):
    xf = x.astype(jnp.float32)
    y = xf * lax.rsqrt(jnp.mean(xf * xf, axis=-1, keepdims=True) + EPS)
    return (y * g.astype(jnp.float32)).astype(x.dtype)


def layer_norm(x, g, b):
    xf = x.astype(jnp.float32)
    mu = jnp.mean(xf, axis=-1, keepdims=True)
    xc = xf - mu
    y = xc * lax.rsqrt(jnp.mean(xc * xc, axis=-1, keepdims=True) + EPS)
    return (y * g.astype(jnp.float32) + b.astype(jnp.float32)).astype(x.dtype)


def gmlp_spatial_gating(u, v, ln_g, ln_b, w_s, b_s):
    bsz, t, _ = u.shape
    n = t // GMLP_CHUNK
    v = layer_norm(v, ln_g, ln_b)
    v = v.reshape(bsz, n, GMLP_CHUNK, GMLP_GROUPS, GMLP_GROUP_DIM)
    causal = jnp.tril(jnp.ones((GMLP_CHUNK, GMLP_CHUNK), dtype=bool))
    w = jnp.where(causal[None], w_s, jnp.zeros_like(w_s))
    v = jnp.einsum('gts,bnsgc->bntgc', w, v) + b_s.T[:, :, None]
    return u * v.reshape(bsz, t, GMLP_WIDTH)


def hgrn2_chunked(q, k, g, v):
    bsz, t = q.shape[:2]
    n = t // HGRN_CHUNK

    def to_chunks(a):
        return a.reshape(bsz, n, HGRN_CHUNK, HGRN_HEADS, a.shape[-1]).transpose(1, 0, 3, 2, 4)

    qc, kc, gc, vc = to_chunks(q), to_chunks(k), to_chunks(g), to_chunks(v)
    causal = jnp.tril(jnp.ones((HGRN_CHUNK, HGRN_CHUNK), dtype=bool))[:, :, None]

    def step(state, inp):
        qb, kb, gb, vb = inp
        cum = jnp.cumsum(gb, axis=2)
        o_inter = jnp.einsum('bhtd,bhdv->bhtv', qb * jnp.exp(cum), state)
        diff = cum[:, :, :, None, :] - cum[:, :, None, :, :]
        decay = jnp.exp(jnp.where(causal, diff, -jnp.inf))
        scores = jnp.einsum('bhtd,bhsd,bhtsd->bhts', qb, kb, decay)
        o = o_inter + jnp.einsum('bhts,bhsv->bhtv', scores, vb)
        last = cum[:, :, -1:, :]
        state = jnp.exp(last[:, :, 0, :])[..., None] * state + jnp.einsum('bhsd,bhsv->bhdv', kb * jnp.exp(last - cum), vb)
        return state, o

    s0 = jnp.zeros((bsz, HGRN_HEADS, HGRN_KEY_DIM, HGRN_VAL_DIM), jnp.float32)
    _, o = lax.scan(step, s0, (qc, kc, gc, vc))
    return o.transpose(1, 0, 3, 2, 4).reshape(bsz, t, HGRN_HEADS, HGRN_VAL_DIM)


def setup_inputs(seed: int = 0) -> dict:
    key = jax.random.key(seed)
    ks = jax.random.split(key, 20)
    f32 = jnp.float32

    def nrm(k, shape, scale):
        return jax.random.normal(k, shape, f32) * scale

    return {
        "x": nrm(ks[0], (BATCH, SEQ, D_MODEL), 1.0),
        "p": nrm(ks[1], (DEPTH, BATCH, SEQ, PLE_DIM), 1.0),
        "w_in": nrm(ks[2], (DEPTH, D_MODEL, IN_COLS), D_MODEL ** -0.5),
        "gmlp_ln_g": 1.0 + nrm(ks[3], (DEPTH, GMLP_WIDTH), 0.05),
        "gmlp_ln_b": nrm(ks[4], (DEPTH, GMLP_WIDTH), 0.02),
        "gmlp_w_s": nrm(ks[5], (DEPTH, GMLP_GROUPS, GMLP_CHUNK, GMLP_CHUNK), 0.5 * GMLP_CHUNK ** -0.5),
        "gmlp_b_s": 1.0 + nrm(ks[6], (DEPTH, GMLP_GROUPS, GMLP_CHUNK), 0.05),
        "hgrn_lb_logits": nrm(ks[7], (DEPTH + 1, HGRN_FORGET_WIDTH), 0.5),
        "hgrn_norm_g": 1.0 + nrm(ks[8], (DEPTH, HGRN_HEADS, HGRN_VAL_DIM), 0.05),
        "w_branch_a": nrm(ks[9], (DEPTH, GMLP_WIDTH, D_MODEL), GMLP_WIDTH ** -0.5),
        "w_branch_b": nrm(ks[10], (DEPTH, HGRN_WIDTH, D_MODEL), HGRN_WIDTH ** -0.5),
        "w_out": nrm(ks[11], (DEPTH, D_MODEL, D_MODEL), D_MODEL ** -0.5),
        "g_pre": 1.0 + nrm(ks[12], (DEPTH, D_MODEL), 0.05),
        "g_post": 1.0 + nrm(ks[13], (DEPTH, D_MODEL), 0.05),
        "w_ple": nrm(ks[14], (DEPTH, PLE_DIM, D_MODEL), PLE_DIM ** -0.5),
        "w_ple_gate": nrm(ks[15], (DEPTH, D_MODEL, D_MODEL), D_MODEL ** -0.5),
        "b_ple_gate": nrm(ks[16], (DEPTH, D_MODEL), 0.02),
        "g_ple": 1.0 + nrm(ks[17], (DEPTH, D_MODEL), 0.05),
    }


def reference(x, p, w_in, gmlp_ln_g, gmlp_ln_b, gmlp_w_s, gmlp_b_s, hgrn_lb_logits, hgrn_norm_g,
              w_branch_a, w_branch_b, w_out, g_pre, g_post, w_ple, w_ple_gate, b_ple_gate, g_ple):
    bsz, t, _ = x.shape
    lower_bounds = jnp.cumsum(jax.nn.softmax(hgrn_lb_logits.astype(jnp.float32), axis=0), axis=0)
    for i in range(DEPTH):
        h = rms_norm(x, g_pre[i])
        proj = h @ w_in[i]
        u, v, z_a, q, f, inp, z_b, a_a, a_b = jnp.split(proj, SPLIT_POINTS, axis=-1)

        y_a = gmlp_spatial_gating(jax.nn.gelu(u, approximate=False), jax.nn.gelu(v, approximate=False),
                                  gmlp_ln_g[i], gmlp_ln_b[i], gmlp_w_s[i], gmlp_b_s[i])
        y_a = (y_a * jax.nn.silu(z_a)) @ w_branch_a[i]

        lb = lower_bounds[i]
        forget = lb + (1.0 - lb) * jax.nn.sigmoid(f.astype(jnp.float32))
        k_in = (1.0 - forget).reshape(bsz, t, HGRN_HEADS, HGRN_KEY_DIM)
        log_f = jnp.log(forget).reshape(bsz, t, HGRN_HEADS, HGRN_KEY_DIM)
        q_h = (jax.nn.silu(q.astype(jnp.float32)) * (HGRN_KEY_DIM ** -0.5)).reshape(bsz, t, HGRN_HEADS, HGRN_KEY_DIM)
        v_h = inp.astype(jnp.float32).reshape(bsz, t, HGRN_HEADS, HGRN_VAL_DIM)
        o = hgrn2_chunked(q_h, k_in, log_f, v_h)
        o = rms_norm(o, hgrn_norm_g[i]).reshape(bsz, t, HGRN_WIDTH).astype(x.dtype)
        y_b = (o * jax.nn.silu(z_b)) @ w_branch_b[i]

        merged = jax.nn.sigmoid(a_a) * y_a + jax.nn.sigmoid(a_b) * y_b
        x = x + rms_norm(merged @ w_out[i], g_post[i])

        e = p[i].astype(x.dtype) @ w_ple[i]
        gate = jax.nn.sigmoid(x @ w_ple_gate[i] + b_ple_gate[i])
        x = x + rms_norm(e * gate, g_ple[i])
    return x
```

```python
import math
from contextlib import ExitStack

import numpy as np
import ml_dtypes

import concourse.bass as bass
import concourse.mybir as mybir
from concourse.bass_utils import run_bass_kernel_spmd

F32 = mybir.dt.float32
BF16 = mybir.dt.bfloat16
AF = mybir.ActivationFunctionType
ALU = mybir.AluOpType
AX = mybir.AxisListType

D = 1024
PLE = 256
EPS = 1e-6
NCORES = 8
TT = 512
NB = TT // 128
CH = 64
NBLK = 26
NSLOT = 3
LAZY_CAST = True

G_U, G_V, G_ZA, G_Q, G_F, G_INP, G_ZB, G_AA, G_AB = range(9)
BLK_WA, BLK_WB, BLK_WO, BLK_WG = 18, 20, 22, 24


class _Op:
    __slots__ = ("eng", "fn", "deps", "is_dma", "sig", "count", "sem", "val", "gid")

    def __init__(self, eng, fn, is_dma):
        self.eng = eng
        self.fn = fn
        self.deps = set()
        self.is_dma = is_dma
        self.sig = False
        self.count = 0
        self.sem = None
        self.val = 0
        self.gid = 0


class Prog:
    ENGS = ("pe", "act", "dve", "pool", "sp")

    def __init__(self, nc):
        self.nc = nc
        self.ops = []
        self.last_w = {}
        self.readers = {}

    def add(self, eng, fn, reads=(), writes=(), dma=False):
        op = _Op(eng, fn, dma)
        op.gid = len(self.ops)
        deps = set()
        for k in reads:
            w = self.last_w.get(k)
            if w is not None:
                deps.add(w)
        for k in writes:
            w = self.last_w.get(k)
            if w is not None:
                deps.add(w)
            deps |= self.readers.get(k, set())
        deps.discard(op)
        op.deps = deps
        for k in reads:
            self.readers.setdefault(k, set()).add(op)
        for k in writes:
            self.last_w[k] = op
            self.readers[k] = set()
        self.ops.append(op)
        return op

    def emit(self, es, n_dma_sems=24):
        nc = self.nc
        engobj = {"pe": nc.tensor, "act": nc.scalar, "dve": nc.vector, "pool": nc.gpsimd, "sp": nc.sync}
        for op in self.ops:
            best = {}
            keep = set()
            for d in op.deps:
                if d.is_dma:
                    keep.add(d)
                    continue
                if d.eng == "pe" and op.eng == "pe":
                    continue
                if d.eng not in best or best[d.eng].gid < d.gid:
                    best[d.eng] = d
            for d in best.values():
                d.sig = True
                keep.add(d)
            op.deps = keep
        sems = {e: es.enter_context(nc.semaphore("sem_" + e)) for e in ("pe", "act", "dve", "pool")}
        dsems = [es.enter_context(nc.semaphore("dsem%d" % i)) for i in range(n_dma_sems)]
        dcount = [0] * n_dma_sems
        dprev = [None] * n_dma_sems
        cnt = {e: 0 for e in sems}
        ndma = 0
        for op in self.ops:
            if op.is_dma:
                i = ndma % n_dma_sems
                ndma += 1
                op.sem = dsems[i]
                dcount[i] += 16
                op.val = dcount[i]
                if dprev[i] is not None:
                    op.deps.add(dprev[i])
                dprev[i] = op
            elif op.sig:
                cnt[op.eng] += 1
                op.count = cnt[op.eng]
                op.sem = sems[op.eng]
                op.val = op.count
        per_eng = {e: [o for o in self.ops if o.eng == e] for e in self.ENGS}
        block = es.enter_context(nc.Block())

        def run(e):
            eo = engobj[e]
            waited = {}
            for op in per_eng[e]:
                need = {}
                for d in op.deps:
                    if (not d.is_dma) and d.eng == "pe" and e == "pe":
                        continue
                    key = id(d.sem)
                    if waited.get(key, 0) >= d.val:
                        continue
                    if key not in need or need[key][1] < d.val:
                        need[key] = (d.sem, d.val)
                for key, (s, v) in need.items():
                    eo.wait_ge(s, v)
                    waited[key] = v
                inst = op.fn()
                if op.is_dma:
                    inst.then_inc(op.sem, 16)
                elif op.sig:
                    inst.then_inc(op.sem, 1)
            return eo

        @block.tensor
        def _(eng):
            run("pe")

        @block.scalar
        def _(eng):
            run("act")

        @block.vector
        def _(eng):
            run("dve")

        @block.gpsimd
        def _(eng):
            run("pool")

        @block.sync
        def _(eng):
            eo = run("sp")
            for i in range(n_dma_sems):
                if dcount[i]:
                    eo.wait_ge(dsems[i], dcount[i])


def build(nseq, T, debug=False):
    assert T % TT == 0
    ntok = nseq * T
    ntile_seq = T // TT
    nc = bass.Bass("TRN2", target_bir_lowering=False)
    es = ExitStack()
    P = Prog(nc)

    def din(name, shape, dt=F32):
        return nc.dram_tensor(name, list(shape), dt, kind="ExternalInput").ap()

    x_d = din("x", [ntok, D])
    p_d = din("p", [ntok, PLE])
    wblk_d = din("wblk", [NBLK, 128, 8 * 512])
    wp_d = din("wp", [128, 2 * D])
    gpreF_d = din("gpreF", [128, 8])
    hgF_d = din("hgF", [128, 8])
    lngF_d = din("lngF", [128, 8])
    lnb_d = din("lnb", [1, D])
    lngr_d = din("lng_row", [1, D])
    bs_d = din("bs", [1, 512])
    wsT_d = din("wsT", [128, 512])
    lbl_d = din("lbl", [128, 16])
    gpost_d = din("gpost_b", [128, D])
    gple_d = din("gple_b", [128, D])
    bg_d = din("bg", [1, D])
    ident_d = din("ident", [128, 128], BF16)
    triu_d = din("triu", [128, 128])
    mbd_d = din("mbd", [128, 128], BF16)
    rm_d = din("rm", [128, TT])
    y_d = nc.dram_tensor("y", [ntok, D], F32, kind="ExternalOutput").ap()
    wsc_d = nc.dram_tensor("wsc", [NBLK, 128, 8 * 512], BF16, kind="Internal").ap()
    dbg_d = {}

    def sb(name, shape, dt=F32):
        return es.enter_context(nc.sbuf_tensor("s_" + name, list(shape), dt))

    def ps(name, shape, dt=F32):
        return es.enter_context(nc.psum_tensor("ps_" + name, list(shape), dt))

    ident = sb("ident", [128, 128], BF16)
    mbd = sb("mbd", [128, 128], BF16)
    rm = sb("rm", [128, TT])
    wsT = sb("wsT", [128, 4, 128], BF16)
    L2p = sb("L2p", [2, D])
    gpost_b = sb("gpost_b", [128, D])
    gple_b = sb("gple_b", [128, D])
    wp = sb("wp", [128, 2, D], BF16)
    small = sb("small", [128, 128])
    ones_row = sb("ones_row", [1, 128], BF16)
    bg_row = sb("bg_row", [1, D], BF16)
    neghalf = sb("neghalf", [128, 32])
    C_GPRE, C_HG, C_LNG, C_LSC, C_LBI, C_LHO = 0, 8, 16, 24, 32, 40
    gpreF = small[:, C_GPRE:C_GPRE + 8]
    hgF = small[:, C_HG:C_HG + 8]
    lngF = small[:, C_LNG:C_LNG + 8]
    lscF = small[:, C_LSC:C_LSC + 8]
    lbiF = small[:, C_LBI:C_LBI + 8]
    lhoF = small[:, C_LHO:C_LHO + 8]
    tmpc = small[:, 48:80]

    wslot = [sb("wslot%d" % i, [128, 8, 512], BF16) for i in range(NSLOT)]
    FB = [sb("FB%d" % i, [128, 8 * TT], BF16) for i in range(8)]
    F32B = [sb("F32B%d" % i, [128, 8 * TT]) for i in range(1)]
    QF = sb("QF", [128, 8, NB, 3, CH], BF16)
    xt = sb("xt", [128, NB, D])
    hn = [sb("hn%d" % i, [128, D], BF16) for i in range(2)]
    pT = sb("pT", [128, 2, TT], BF16)
    hb = [sb("hb%d" % i, [128, D], BF16) for i in range(2)]
    junk = sb("junk", [128, D], BF16)
    gvt = [sb("gvt%d" % i, [128, D]) for i in range(2)]
    ft = [sb("ft%d" % i, [128, TT]) for i in range(6)]
    fe = [sb("fe%d" % i, [128, TT], BF16) for i in range(4)]
    Sst = sb("Sst", [128, D])
    Sbf = [sb("Sbf%d" % i, [128, D], BF16) for i in range(3)]
    Am = [sb("Am%d" % i, [128, 8, 128], BF16) for i in range(2)]
    el = sb("el", [128, 8, 8])
    stat = sb("stat", [128, 256])

    PB = [ps("PB%d" % i, [128, 512]) for i in range(6)]
    TB = [ps("TB%d" % i, [128, 1024], BF16) for i in range(2)]

    st = {"pb": 0, "tb": 0, "dve_pool": 0, "stat": 0, "gv": 0, "ft": 0}

    def next_pb(n=6):
        i = st["pb"] % n
        st["pb"] += 1
        return i

    def next_tb():
        i = st["tb"] % 2
        st["tb"] += 1
        return i

    def statcol(n):
        c = st["stat"]
        if c + n > 256:
            c = 0
        st["stat"] = c + n
        return c

    def dma(out, in_, reads, writes):
        return P.add("sp", lambda: nc.sync.dma_start(out=out, in_=in_), reads, writes, dma=True)

    def act(out, in_, func, reads, writes, scale=None, bias=None, accum_out=None):
        kw = {}
        if scale is not None:
            kw["scale"] = scale
        if bias is not None:
            kw["bias"] = bias
        if accum_out is not None:
            kw["accum_out"] = accum_out
        return P.add("act", lambda: nc.scalar.activation(out=out, in_=in_, func=func, **kw), reads, writes)

    def veng(e):
        return nc.vector if e == "dve" else nc.gpsimd

    def tt(e, out, in0, in1, op, reads, writes):
        return P.add(e, lambda: veng(e).tensor_tensor(out=out, in0=in0, in1=in1, op=op), reads, writes)

    def ts(e, out, in0, s1, s2, op0, op1, reads, writes):
        if op1 is None:
            return P.add(e, lambda: veng(e).tensor_scalar(out=out, in0=in0, scalar1=s1, scalar2=None, op0=op0),
                         reads, writes)
        return P.add(e, lambda: veng(e).tensor_scalar(out=out, in0=in0, scalar1=s1, scalar2=s2, op0=op0, op1=op1),
                     reads, writes)

    def stt(out, in0, scalar, in1, op0, op1, reads, writes):
        return P.add("dve", lambda: nc.vector.scalar_tensor_tensor(out=out, in0=in0, scalar=scalar, in1=in1,
                                                                   op0=op0, op1=op1), reads, writes)

    def cp(e, out, in_, reads, writes):
        if e == "act":
            return P.add("act", lambda: nc.scalar.copy(out=out, in_=in_), reads, writes)
        return P.add(e, lambda: veng(e).tensor_copy(out=out, in_=in_), reads, writes)

    def mm(out, lhsT, rhs, start, stop, reads, writes):
        return P.add("pe", lambda: nc.tensor.matmul(out, lhsT, rhs, start=start, stop=stop), reads, writes)

    def tr(out, in_, reads, writes):
        return P.add("pe", lambda: nc.tensor.transpose(out, in_, ident[:, :]), reads + ("ident",), writes)

    def rsqrt_cols(c_in, n, scale, eps, e="dve"):
        c_ms = statcol(n)
        c_out = statcol(n)
        ts(e, stat[:, c_ms:c_ms + n], stat[:, c_in:c_in + n], scale, eps, ALU.mult, ALU.add,
           ("stat%d" % c_in,), ("stat%d" % c_ms,))
        tt("pool", stat[:, c_out:c_out + n], stat[:, c_ms:c_ms + n], neghalf[:, 0:n], ALU.pow,
           ("stat%d" % c_ms, "neghalf"), ("stat%d" % c_out,))
        return c_out

    stgA = F32B[0][:, :]
    stgB = xt[:, :, :].rearrange("p a b -> p (a b)")
    def XKp(k):
        return tuple("%s.%d" % (k, i) for i in range(NB))

    EAGER = (BLK_WO, BLK_WO + 1, BLK_WG, BLK_WG + 1) if LAZY_CAST else tuple(range(NBLK))
    for b in EAGER:
        stg, sk = ((stgA, "F32B0"), (stgB, "xt"))[b % 2]
        cb, ck = ((FB[4], "FB4"), (FB[5], "FB5"))[b % 2]
        dma(stg, wblk_d[b], XKp(sk), XKp(sk))
        fold = gpreF if b < 18 else (hgF if b in (BLK_WB, BLK_WB + 1) else None)
        if fold is None:
            cp("dve", cb[:, 0:2048], stg[:, 0:2048], XKp(sk), (ck,))
            cp("act", cb[:, 2048:4096], stg[:, 2048:4096], XKp(sk), (ck,))
        else:
            for kc in range(8):
                if kc % 2 == 0:
                    ts("dve", cb[:, kc * 512:(kc + 1) * 512], stg[:, kc * 512:(kc + 1) * 512], fold[:, kc:kc + 1],
                       None, ALU.mult, None, XKp(sk) + ("small",), (ck,))
                else:
                    act(cb[:, kc * 512:(kc + 1) * 512], stg[:, kc * 512:(kc + 1) * 512], AF.Copy,
                        XKp(sk) + ("small",), (ck,), scale=fold[:, kc:kc + 1])
        dma(wsc_d[b], cb[:, :], (ck,), ("wsc%d" % b,))

    dma(ident[:, :], ident_d, (), ("ident",))
    dma(mbd[:, :], mbd_d, (), ("mbd",))
    dma(rm[:, :], rm_d, (), ("rm",))
    dma(gpost_b[:, :], gpost_d, (), ("gpost_b",))
    dma(gple_b[:, :], gple_d, (), ("gple_b",))
    dma(small[:, C_GPRE:C_GPRE + 8], gpreF_d, (), ("small",))
    dma(small[:, C_HG:C_HG + 8], hgF_d, (), ("small",))
    dma(small[:, C_LNG:C_LNG + 8], lngF_d, (), ("small",))
    P.add("dve", lambda: nc.vector.memset(neghalf[:, :], -0.5), (), ("neghalf",))
    P.add("dve", lambda: nc.vector.memset(QF[:, :, :, :, :], 0.0), (), ("QF",))
    P.add("pool", lambda: nc.gpsimd.memset(ones_row[:, :], 1.0), (), ("ones_row",))
    lbl = ft[0]
    dma(lbl[:, 0:16], lbl_d, (), ("ft0",))
    tt("dve", tmpc[:, 0:8], lbl[:, 0:8], lbl[:, 8:16], ALU.subtract, ("ft0",), ("tmpc",))
    act(tmpc[:, 8:16], tmpc[:, 0:8], AF.Sigmoid, ("tmpc",), ("tmpc",), scale=-1.0)
    ts("dve", small[:, C_LSC:C_LSC + 8], tmpc[:, 8:16], -0.5, None, ALU.mult, None, ("tmpc",), ("small",))
    ts("dve", small[:, C_LBI:C_LBI + 8], tmpc[:, 8:16], -0.5, 1.0, ALU.mult, ALU.add, ("tmpc",), ("small",))
    act(small[:, C_LHO:C_LHO + 8], tmpc[:, 8:16], AF.Ln, ("tmpc",), ("small",), scale=0.5)
    bgf = ft[1]
    dma(bgf[0:1, 0:512], bg_d[:, 0:512], (), ("ft1",))
    cp("dve", bg_row[0:1, 0:512], bgf[0:1, 0:512], ("ft1",), ("bg_row",))
    dma(bgf[0:1, 0:512], bg_d[:, 512:1024], ("ft1",), ("ft1",))
    cp("dve", bg_row[0:1, 512:1024], bgf[0:1, 0:512], ("ft1",), ("bg_row",))
    wpf = gvt[0]
    for h in range(2):
        dma(wpf[:, :], wp_d[:, h * D:(h + 1) * D], ("gvt0",), ("gvt0",))
        cp("dve", wp[:, h, :], wpf[:, :], ("gvt0",), ("wp",))
    wsf = ft[2]
    triu = ft[3]
    dma(wsf[:, :], wsT_d, (), ("ft2",))
    dma(triu[:, 0:128], triu_d, (), ("ft3",))
    tt("dve", wsT[:, :, :], wsf[:, :].rearrange("p (g t) -> p g t", g=4),
       triu[:, 0:128].unsqueeze(1).broadcast_to([128, 4, 128]), ALU.mult, ("ft2", "ft3"), ("wsT",))
    onescol = sb("onescol", [128, 1], BF16)
    P.add("dve", lambda: nc.vector.memset(onescol[:, :], 1.0), (), ("onescol",))
    L2 = gvt[1][0:2, :]
    R2 = ft[5][0:2, :]
    G2 = gvt[0][0:2, :]
    P.add("dve", lambda: nc.vector.memset(L2[:, :], 1.0), (), ("gvt1",))
    dma(L2[0:1, :], lnb_d, ("gvt1",), ("gvt1",))
    dma(G2[0:1, :], lngr_d, (), ("gvt0",))
    dma(G2[1:2, :], lngr_d, (), ("gvt0",))
    P.add("dve", lambda: nc.vector.reciprocal(out=G2[:, :], in_=G2[:, :]), ("gvt0",), ("gvt0",))
    tt("dve", L2p[:, :], L2[:, :], G2[:, :], ALU.mult, ("gvt1", "gvt0"), ("L2p",))
    mm(PB[0][0:1, :], onescol[:, 0:1], wsT[:, :, :].rearrange("p g t -> p (g t)"), True, True,
       ("onescol", "wsT"), ("PB0",))
    cp("dve", R2[0:1, :], PB[0][0:1, :], ("PB0",), ("ft5",))
    dma(R2[1:2, :], bs_d, ("ft5",), ("ft5",))

    ring = {"n": 0}
    lazy = set(range(NBLK)) - set(EAGER)

    def issue_block(blk):
        s = ring["n"] % NSLOT
        ring["n"] += 1
        sk_ = "wslot%d" % s
        if blk in lazy:
            lazy.discard(blk)
            fold = gpreF if blk < 18 else (hgF if blk in (BLK_WB, BLK_WB + 1) else None)
            for hf in range(2):
                stg = xt[:, 2 * hf:2 * hf + 2, :].rearrange("p a b -> p (a b)")
                lk = "xtL%d" % hf
                dma(stg, wblk_d[blk][:, hf * 2048:(hf + 1) * 2048], (), (lk,) + XKp("xt"))
                if fold is None:
                    cp("pool", wslot[s][:, 4 * hf:4 * hf + 4, :].rearrange("p a b -> p (a b)"), stg, (lk,), (sk_,))
                else:
                    for k4 in range(4):
                        kc = 4 * hf + k4
                        ts("pool", wslot[s][:, kc, :], stg[:, k4 * 512:(k4 + 1) * 512], fold[:, kc:kc + 1], 0.0,
                           ALU.mult, ALU.add, (lk, "small"), (sk_,))
            dma(wsc_d[blk], wslot[s][:, :, :].rearrange("p a b -> p (a b)"), (sk_,), ("wsc%d" % blk,))
        else:
            dma(wslot[s][:, :, :].rearrange("p a b -> p (a b)"), wsc_d[blk], ("wsc%d" % blk, sk_), (sk_,))
        return s

    total_tiles = nseq * ntile_seq

    def gb(*gs):
        out = []
        for g in gs:
            out += [2 * g, 2 * g + 1]
        return out

    stream = []
    for g_ in range(total_tiles):
        if g_ == 0:
            stream += gb(G_U, G_ZA, G_AA)
        stream += (gb(G_V) if g_ == 0 else []) + gb(G_Q, G_F) + [BLK_WA, BLK_WA + 1] + gb(G_ZB, G_AB, G_INP)
        if g_ + 1 < total_tiles:
            stream += gb(G_ZA)
        stream += [BLK_WB, BLK_WB + 1, BLK_WO, BLK_WO + 1]
        if g_ + 1 < total_tiles:
            stream += gb(G_U, G_AA)
        stream += [BLK_WG, BLK_WG + 1]
    sp_ = {"issued": 0, "used": 0}
    slot_of = {}

    def prefetch():
        while sp_["issued"] < len(stream) and sp_["issued"] - sp_["used"] < NSLOT:
            i = sp_["issued"]
            slot_of[i] = issue_block(stream[i])
            sp_["issued"] += 1

    def take_block(expect, off=0):
        i = sp_["used"] + off
        assert stream[i] == expect, (stream[i], expect)
        assert i < sp_["issued"]
        return i, slot_of[i]

    def release_block(n=1):
        sp_["used"] += n
        prefetch()

    prefetch()

    kchunk = {"k": 0}
    XBUF = [(xt[:, :, :].rearrange("p a b -> p (a b)"), "xt"), (F32B[0][:, :], "F32B0")]

    def XK(k, tb=None):
        if tb is None:
            return tuple("%s.%d" % (k, i) for i in range(NB))
        return ("%s.%d" % (k, tb),)

    HBUF = [(FB[0], "FB0"), (FB[7], "FB7")]
    s1 = {}

    def F3(i):
        return FB[i][:, :].rearrange("p (a b) -> p a b", a=8)

    def T3(i):
        return FB[i][:, :].rearrange("p (a b) -> p a b", a=NB)

    def stage1_load(g):
        X, Xk = XBUF[g % 2]
        r0 = g * TT
        wk = XK(Xk) + (("xtL0", "xtL1") if Xk == "xt" else ())
        dma(X.rearrange("p (a b) -> p a b", a=NB), x_d[r0:r0 + TT, :].rearrange("(b p) d -> p b d", p=128),
            XK(Xk), wk)

    def stage1_stats(g):
        X, Xk = XBUF[g % 2]
        X3 = X.rearrange("p (a b) -> p a b", a=NB)
        c_ss = statcol(NB)
        for tb in range(NB):
            act(junk[:, :], X3[:, tb, :], AF.Square, XK(Xk, tb), ("stat%d" % c_ss,),
                accum_out=stat[:, c_ss + tb:c_ss + tb + 1])
        s1[g] = rsqrt_cols(c_ss, NB, 1.0 / D, EPS)

    def stage1_scale(g, tb):
        X, Xk = XBUF[g % 2]
        X3 = X.rearrange("p (a b) -> p a b", a=NB)
        c_r = s1[g]
        h, hk = hn[tb % 2], "hn%d" % (tb % 2)
        ts("dve", h[:, :], X3[:, tb, :], stat[:, c_r + tb:c_r + tb + 1], None, ALU.mult, None,
           XK(Xk, tb) + ("stat%d" % c_r,), (hk,))

    def stage1_tr(g, tb):
        hTn, hTnk = HBUF[g % 2]
        hTn3 = hTn[:, :].rearrange("p (a b) -> p a b", a=8)
        h, hk = hn[tb % 2], "hn%d" % (tb % 2)
        tbi = next_tb()
        for kc in range(8):
            tr(TB[tbi][:, kc * 128:(kc + 1) * 128], h[:, kc * 128:(kc + 1) * 128], (hk,), ("TB%d" % tbi,))
        cp("act", hTn3[:, :, tb * 128:(tb + 1) * 128], TB[tbi][:, :].rearrange("p (a b) -> p a b", a=8),
           ("TB%d" % tbi,), (hTnk,))

    def tile_body(g):
        r0 = g * TT
        X, Xk = XBUF[g % 2]
        S, Sk = XBUF[(g + 1) % 2]
        X3 = X.rearrange("p (a b) -> p a b", a=NB)
        hT, hTk = HBUF[g % 2]
        hT3 = hT[:, :].rearrange("p (a b) -> p a b", a=8)
        has_next = g + 1 < total_tiles

        if g == 0:
            stage1_load(0)
            stage1_stats(0)
            for tb in range(NB):
                stage1_scale(0, tb)
                stage1_tr(0, tb)
        if g % ntile_seq == 0:
            P.add("pool", lambda: nc.gpsimd.memset(Sst[:, :], 0.0), (), ("Sst0", "Sst1"))
            k0 = kchunk["k"] % 3
            P.add("pool", lambda: nc.gpsimd.memset(Sbf[k0][:, :], 0.0), (), ("Sbf%d" % k0,))

        pt3 = gvt[1][:, :].rearrange("p (a b) -> p a b", a=NB)
        dma(pt3, p_d[r0:r0 + TT, :].rearrange("(b p) d -> p b d", p=128), ("gvt1",), ("gvt1",))
        ptb = hb[0][:, :].rearrange("p (a b) -> p a b", a=NB)
        cp("pool", ptb[:, :, :], pt3, ("gvt1",), ("hb0",))

        def proj_F(gg, evac, src=None):
            h3_, hk_ = src if src is not None else (hT3, hTk)
            for half in range(2):
                _, s = take_block(2 * gg + half)
                for cb_ in range(4):
                    b = next_pb()
                    for kc in range(8):
                        mm(PB[b][:, :], wslot[s][:, kc, cb_ * 128:(cb_ + 1) * 128], h3_[:, kc, :], kc == 0, kc == 7,
                           ("wslot%d" % s, hk_), ("PB%d" % b,))
                    evac(half * 4 + cb_, PB[b], "PB%d" % b)
                release_block()

        def proj_T(gg, evac, wsrc=None):
            for half in range(2):
                if wsrc is None:
                    _, s = take_block(2 * gg + half)
                    w3, wk_ = wslot[s], "wslot%d" % s
                else:
                    w3, wk_ = wsrc[half]
                for tb in range(NB):
                    b = next_pb()
                    for kc in range(8):
                        mm(PB[b][:, :], hT3[:, kc, tb * 128:(tb + 1) * 128], w3[:, kc, :], kc == 0, kc == 7,
                           (wk_, hTk), ("PB%d" % b,))
                    evac(tb, half, PB[b], "PB%d" % b)
                if wsrc is None:
                    release_block()

        guT, szaT, taaT = F3(5), F3(6), F3(4)

        def proj_uza(which, src=None):
            if which == 0:
                proj_F(G_U, lambda j, bk, k: act(guT[:, j, :], bk[:, :], AF.Gelu, (k,), ("FB5",)), src)
            elif which == 1:
                proj_F(G_ZA, lambda j, bk, k: act(szaT[:, j, :], bk[:, :], AF.Silu, (k,), ("FB6",)), src)
            else:
                proj_F(G_AA, lambda j, bk, k: act(taaT[:, j, :], bk[:, :], AF.Tanh, (k,), ("FB4",), scale=0.5), src)

        def make_gz():
            for j in range(8):
                tt("pool", szaT[:, j, :], szaT[:, j, :], guT[:, j, :], ALU.mult, ("FB6", "FB5"), ("FB6",))

        if g == 0:
            for w_ in range(3):
                proj_uza(w_)
            make_gz()

        vn = T3(2)
        gv = S.rearrange("p (a b) -> p a b", a=NB)
        vsrc = None
        if g > 0:
            hprev, hprevk = HBUF[(g - 1) % 2]
            vsrc = [(FB[2][:, :].rearrange("p (a b) -> p a b", a=8), "FB2"),
                    (hprev[:, :].rearrange("p (a b) -> p a b", a=8), hprevk)]
        proj_T(G_V, lambda tb, half, bk, k: act(gv[:, tb, half * 512:(half + 1) * 512], bk[:, :], AF.Gelu, (k,),
                                               XK(Sk, tb)), vsrc)
        for tb in range(NB):
            c_bs = statcol(12)
            for hh in range(2):
                P.add("dve", (lambda tb=tb, hh=hh, c=c_bs: nc.vector.bn_stats(
                    out=stat[:, c + 6 * hh:c + 6 * hh + 6], in_=gv[:, tb, hh * 512:(hh + 1) * 512])),
                    XK(Sk, tb), ("stat%d" % c_bs,))
            c_mv = statcol(2)
            P.add("dve", (lambda c=c_bs, m=c_mv: nc.vector.bn_aggr(out=stat[:, m:m + 2], in_=stat[:, c:c + 12])),
                  ("stat%d" % c_bs,), ("stat%d" % c_mv,))
            c_var = statcol(1)
            cp("dve", stat[:, c_var:c_var + 1], stat[:, c_mv + 1:c_mv + 2], ("stat%d" % c_mv,), ("stat%d" % c_var,))
            c_rs = rsqrt_cols(c_var, 1, 1.0, EPS)
            c_nm = statcol(1)
            stt(stat[:, c_nm:c_nm + 1], stat[:, c_mv:c_mv + 1], -1.0, stat[:, c_rs:c_rs + 1], ALU.mult, ALU.mult,
                ("stat%d" % c_mv, "stat%d" % c_rs), ("stat%d" % c_nm,))
            ts("dve", vn[:, tb, :], gv[:, tb, :], stat[:, c_rs:c_rs + 1], stat[:, c_nm:c_nm + 1], ALU.mult, ALU.add,
               XK(Sk, tb) + ("stat%d" % c_rs, "stat%d" % c_nm), ("FB2",))

        sqF = F3(1)
        tfall = S.rearrange("p (a b) -> p a b", a=8)
        proj_F(G_Q, lambda j, bk, k: act(sqF[:, j, :], bk[:, :], AF.Silu, (k,), ("FB1",)))
        proj_F(G_F, lambda j, bk, k: act(tfall[:, j, :], bk[:, :], AF.Tanh, (k,), XK(Sk, j // 2), scale=-0.5))

        KF, KFk = F3(3), "FB3"
        lnsc = math.log(128.0 ** -0.5)

        def f_ln(j):
            lf, lfk = ft[j % 3], "ft%d" % (j % 3)
            act(lf[:, :], tfall[:, j, :], AF.Ln, XK(Sk, j // 2) + ("small",), (lfk,), scale=lscF[:, j:j + 1],
                bias=lbiF[:, j:j + 1])

        def f_scan(j):
            lf, lfk = ft[j % 3], "ft%d" % (j % 3)
            cm, cmk = ft[3 + (j % 2)], "ft%d" % (3 + j % 2)
            P.add("dve", (lambda cm=cm, lf=lf: nc.vector.tensor_tensor_scan(
                out=cm[:, :], data0=rm[:, :], data1=lf[:, :], initial=0.0, op0=ALU.mult, op1=ALU.add)),
                (lfk, "rm"), (cmk,))

        def f_rest(j):
            cm, cmk = ft[3 + (j % 2)], "ft%d" % (3 + j % 2)
            E, Ek = fe[j % 2], "fe%d" % (j % 2)
            Ei, Eik = fe[2 + j % 2], "fe%d" % (2 + j % 2)
            act(E[:, :], cm[:, :], AF.Exp, (cmk,), (Ek,), bias=float(lnsc))
            act(Ei[:, :], cm[:, :], AF.Exp, (cmk, "small"), (Eik,), scale=-1.0, bias=lhoF[:, j:j + 1])
            act(el[:, j, :], cm[:, :].rearrange("p (c i) -> p c i", i=CH)[:, :, CH - 1], AF.Exp, (cmk,), ("el",))
            tt("pool", QF[:, j, :, 0:3:2, :], sqF[:, j, :].rearrange("p (a b c) -> p a b c", a=NB, b=2),
               E[:, :].rearrange("p (a b c) -> p a b c", a=NB, b=2), ALU.mult, ("FB1", Ek), ("QF",))
            stt(KF[:, j, :], tfall[:, j, :], 1.0, Ei[:, :], ALU.add, ALU.mult, XK(Sk, j // 2) + (Eik,), (KFk,))

        tbi = next_tb()
        for tb in range(NB):
            for kc in range(2):
                tr(TB[tbi][:, (tb * 2 + kc) * 128:(tb * 2 + kc + 1) * 128], ptb[:, tb, kc * 128:(kc + 1) * 128],
                   ("hb0",), ("TB%d" % tbi,))
        cp("dve", pT[:, :, :].rearrange("p k (b t) -> p b k t", b=NB),
           TB[tbi][:, :].rearrange("p (b k t) -> p b k t", b=NB, k=2), ("TB%d" % tbi,), ("pT",))

        yapreT = guT

        def spatial(j):
            gi_ = j // 2
            b = 3 + (j % 2)
            for tb in range(NB):
                mm(PB[b][:, tb * 128:(tb + 1) * 128], vn[:, tb, j * 128:(j + 1) * 128], wsT[:, gi_, :], True, False,
                   ("FB2", "wsT"), ("PB%d" % b,))
                mm(PB[b][:, tb * 128:(tb + 1) * 128], L2p[0:2, j * 128:(j + 1) * 128],
                   ft[5][0:2, gi_ * 128:(gi_ + 1) * 128], False, True, ("L2p", "ft5"), ("PB%d" % b,))
            stt(yapreT[:, j, :], PB[b][:, :], lngF[:, j:j + 1], szaT[:, j, :], ALU.mult, ALU.mult,
                ("PB%d" % b, "small", "FB6"), ("FB5",))

        f_ln(0)
        f_ln(1)
        f_scan(0)
        for j in range(8):
            if j + 2 < 8:
                f_ln(j + 2)
            if j + 1 < 8:
                f_scan(j + 1)
            f_rest(j)
            spatial(j)

        if has_next:
            stage1_load(g + 1)

        maT = taaT
        for half in range(2):
            _, s = take_block(BLK_WA + half)
            for cb_ in range(4):
                jo = half * 4 + cb_
                b = next_pb()
                for kc in range(8):
                    mm(PB[b][:, :], wslot[s][:, kc, cb_ * 128:(cb_ + 1) * 128], yapreT[:, kc, :], kc == 0, kc == 7,
                       ("wslot%d" % s, "FB5"), ("PB%d" % b,))
                stt(maT[:, jo, :], taaT[:, jo, :], 1.0, PB[b][:, :], ALU.add, ALU.mult, ("FB4", "PB%d" % b), ("FB4",))
            release_block()

        if has_next:
            stage1_stats(g + 1)
        KT, KTk = T3(1), "FB1"
        for tb in range(NB):
            tbi = next_tb()
            for j in range(8):
                tr(TB[tbi][:, j * 128:(j + 1) * 128], KF[:, j, tb * 128:(tb + 1) * 128], (KFk,), ("TB%d" % tbi,))
            cp("act", KT[:, tb, :], TB[tbi][:, :], ("TB%d" % tbi,), (KTk,))
        szbT, tabT, V_T = T3(2), F3(5), T3(6)
        proj_T(G_ZB, lambda tb, half, bk, k: act(szbT[:, tb, half * 512:(half + 1) * 512], bk[:, :], AF.Silu, (k,),
                                                ("FB2",)))
        proj_F(G_AB, lambda j, bk, k: act(tabT[:, j, :], bk[:, :], AF.Tanh, (k,), ("FB5",), scale=0.5))
        proj_T(G_INP, lambda tb, half, bk, k: cp("dve", V_T[:, tb, half * 512:(half + 1) * 512], bk[:, :], (k,),
                                                 ("FB6",)))

        obT, obk = hT3, hTk
        pend = []

        def T_out(tb):
            on_, onk = hb[tb % 2], "hb%d" % (tb % 2)
            tbi = next_tb()
            for j in range(8):
                tr(TB[tbi][:, j * 128:(j + 1) * 128], on_[:, j * 128:(j + 1) * 128], (onk,), ("TB%d" % tbi,))
            cp("act", obT[:, :, tb * 128:(tb + 1) * 128], TB[tbi][:, :].rearrange("p (a b) -> p a b", a=8),
               ("TB%d" % tbi,), (obk,))

        for tb in range(NB):
            kA = kchunk["k"]
            kB = kA + 1
            kchunk["k"] += 2
            cA, cB = 2 * tb, 2 * tb + 1
            am, amk = Am[tb % 2], "Am%d" % (tb % 2)
            if has_next:
                stage1_scale(g + 1, tb)
            for hb_ in range(2):
                for jj in range(4):
                    j = hb_ * 4 + jj
                    mm(PB[hb_][:, jj * 128:(jj + 1) * 128], KF[:, j, tb * 128:(tb + 1) * 128], QF[:, j, tb, 0:3:2, :],
                       True, True, (KFk, "QF"), ("PB%d" % hb_,))
                tt("dve", am[:, hb_ * 4:(hb_ + 1) * 4, :], PB[hb_][:, :].rearrange("p (a b) -> p a b", a=4),
                   mbd[:, :].unsqueeze(1).broadcast_to([128, 4, 128]), ALU.mult, ("PB%d" % hb_, "mbd"), (amk,))
            for (c, k, lo) in ((cA, kA, 0), (cB, kB, 64)):
                for hb_ in range(2):
                    b = 4 + hb_
                    for jj in range(4):
                        j = hb_ * 4 + jj
                        mm(PB[b][:, jj * 128:(jj + 1) * 128], KT[lo:lo + 64, tb, j * 128:(j + 1) * 128],
                           V_T[lo:lo + 64, tb, j * 128:(j + 1) * 128], True, True, (KTk, "FB6"), ("PB%d" % b,))
                    sl = slice(hb_ * 512, (hb_ + 1) * 512)
                    sk_ = "Sst%d" % hb_
                    tt("dve", Sst[:, sl], PB[b][:, :], Sst[:, sl], ALU.add, ("PB%d" % b, sk_), (sk_,))
                    tt("dve", Sst[:, sl].rearrange("p (a b) -> p a b", a=4),
                       Sst[:, sl].rearrange("p (a b) -> p a b", a=4),
                       el[:, hb_ * 4:(hb_ + 1) * 4, c:c + 1].broadcast_to([128, 4, 128]), ALU.mult,
                       (sk_, "el"), (sk_,))
                kn = (k + 1) % 3
                cp("act", Sbf[kn][:, :], Sst[:, :], ("Sst0", "Sst1"), ("Sbf%d" % kn,))
                if lo == 0 and has_next:
                    stage1_tr(g + 1, tb)
            sA, sB = kA % 3, kB % 3
            for hb_ in range(2):
                b = 2 + hb_
                for jj in range(4):
                    j = hb_ * 4 + jj
                    o_ = PB[b][:, jj * 128:(jj + 1) * 128]
                    mm(o_, am[:, j, :], V_T[:, tb, j * 128:(j + 1) * 128], True, False, (amk, "FB6"), ("PB%d" % b,))
                    mm(o_, QF[:, j, tb, 0:2, :], Sbf[sA][:, j * 128:(j + 1) * 128], False, False,
                       ("QF", "Sbf%d" % sA), ("PB%d" % b,))
                    mm(o_, QF[:, j, tb, 1:3, :], Sbf[sB][:, j * 128:(j + 1) * 128], False, True,
                       ("QF", "Sbf%d" % sB), ("PB%d" % b,))
            oraw, ork = gvt[tb % 2], "gvt%d" % (tb % 2)
            for hb_ in range(2):
                cp("act", oraw[:, hb_ * 512:(hb_ + 1) * 512], PB[2 + hb_][:, :], ("PB%d" % (2 + hb_),), (ork,))
            c_s8 = statcol(8)
            for j in range(8):
                act(junk[:, 0:128], oraw[:, j * 128:(j + 1) * 128], AF.Square, (ork,), ("stat%d" % c_s8,),
                    accum_out=stat[:, c_s8 + j:c_s8 + j + 1])
            c_r8 = rsqrt_cols(c_s8, 8, 1.0 / 128.0, EPS, e="pool")
            tt("pool", oraw[:, :].rearrange("p (a b) -> p a b", a=8), oraw[:, :].rearrange("p (a b) -> p a b", a=8),
               stat[:, c_r8:c_r8 + 8].unsqueeze(2).broadcast_to([128, 8, 128]), ALU.mult,
               (ork, "stat%d" % c_r8), (ork,))
            on_, onk = hb[tb % 2], "hb%d" % (tb % 2)
            tt("pool", on_[:, :], oraw[:, :], szbT[:, tb, :], ALU.mult, (ork, "FB2"), (onk,))
            if tb == NB - 1 and has_next:
                hTn_, hTnk_ = HBUF[(g + 1) % 2]
                nsrc = (hTn_[:, :].rearrange("p (a b) -> p a b", a=8), hTnk_)
                proj_uza(1, nsrc)
            if pend:
                T_out(pend.pop())
            pend.append(tb)
        T_out(pend.pop())

        mgT, mgk = tabT, "FB5"
        for half in range(2):
            _, s = take_block(BLK_WB + half)
            for cb_ in range(4):
                jo = half * 4 + cb_
                b = next_pb()
                for kc in range(8):
                    mm(PB[b][:, :], wslot[s][:, kc, cb_ * 128:(cb_ + 1) * 128], obT[:, kc, :], kc == 0, kc == 7,
                       ("wslot%d" % s, obk), ("PB%d" % b,))
                f_, fk = ft[jo % 2], "ft%d" % (jo % 2)
                stt(f_[:, :], tabT[:, jo, :], 1.0, PB[b][:, :], ALU.add, ALU.mult, ("FB5", "PB%d" % b), (fk,))
                tt("dve", mgT[:, jo, :], f_[:, :], maT[:, jo, :], ALU.add, (fk, "FB4"), (mgk,))
            release_block()

        if has_next:
            dma(FB[2][:, :], wsc_d[2 * G_V], ("wsc%d" % (2 * G_V), "FB2"), ("FB2",))
            dma(hT[:, :], wsc_d[2 * G_V + 1], ("wsc%d" % (2 * G_V + 1), hTk), (hTk,))

        if g == 0 and LAZY_CAST:
            assert not lazy
            stage1_load(0)
        _, s0 = take_block(BLK_WO, 0)
        _, s1_ = take_block(BLK_WO + 1, 1)
        x1b, x1bk = T3(3), "FB3"
        for tb in range(NB):
            banks = []
            c_ss2 = statcol(2)
            for half, s in ((0, s0), (1, s1_)):
                b = next_pb(6)
                banks.append(b)
                for kc in range(8):
                    mm(PB[b][:, :], mgT[:, kc, tb * 128:(tb + 1) * 128], wslot[s][:, kc, :], kc == 0, kc == 7,
                       ("wslot%d" % s, mgk), ("PB%d" % b,))
                act(junk[:, 0:512], PB[b][:, :], AF.Square, ("PB%d" % b,), ("stat%d" % c_ss2,),
                    accum_out=stat[:, c_ss2 + half:c_ss2 + half + 1])
            c_s1 = statcol(1)
            tt("dve", stat[:, c_s1:c_s1 + 1], stat[:, c_ss2:c_ss2 + 1], stat[:, c_ss2 + 1:c_ss2 + 2], ALU.add,
               ("stat%d" % c_ss2,), ("stat%d" % c_s1,))
            c_r1 = rsqrt_cols(c_s1, 1, 1.0 / D, 4.0 * EPS)
            for half in range(2):
                b = banks[half]
                f_, fk = ft[half], "ft%d" % half
                sl = slice(half * 512, (half + 1) * 512)
                stt(f_[:, :], PB[b][:, :], stat[:, c_r1:c_r1 + 1], gpost_b[:, sl], ALU.mult, ALU.mult,
                    ("PB%d" % b, "stat%d" % c_r1, "gpost_b"), (fk,))
                tt(("dve", "pool")[half], X3[:, tb, sl], X3[:, tb, sl], f_[:, :], ALU.add, XK(Xk, tb) + (fk,),
                   XK(Xk, tb))
            cp("act", x1b[:, tb, :], X3[:, tb, :], XK(Xk, tb), (x1bk,))
        release_block(2)
        if has_next:
            proj_uza(0, nsrc)
            make_gz()

        x1T, x1Tk = F3(1), "FB1"
        for tb in range(NB):
            tbi = next_tb()
            for kc in range(8):
                tr(TB[tbi][:, kc * 128:(kc + 1) * 128], x1b[:, tb, kc * 128:(kc + 1) * 128], (x1bk,),
                   ("TB%d" % tbi,))
            cp("act", x1T[:, :, tb * 128:(tb + 1) * 128], TB[tbi][:, :].rearrange("p (a b) -> p a b", a=8),
               ("TB%d" % tbi,), (x1Tk,))

        if has_next:
            proj_uza(2, nsrc)

        _, s0 = take_block(BLK_WG, 0)
        _, s1_ = take_block(BLK_WG + 1, 1)
        ple_c = {}

        def ple_head(tb):
            eg, egk = gvt[tb % 2], "gvt%d" % (tb % 2)
            c_ss2 = statcol(2)
            ple_c[tb] = c_ss2
            bg_, be_ = [], []
            for half, s in ((0, s0), (1, s1_)):
                sl = slice(half * 512, (half + 1) * 512)
                b = next_pb(6)
                bg_.append(b)
                for kc in range(8):
                    mm(PB[b][:, :], x1T[:, kc, tb * 128:(tb + 1) * 128], wslot[s][:, kc, :], kc == 0, False,
                       ("wslot%d" % s, x1Tk), ("PB%d" % b,))
                mm(PB[b][:, :], ones_row[0:1, :], bg_row[0:1, sl], False, True, ("ones_row", "bg_row"),
                   ("PB%d" % b,))
            for half in range(2):
                sl = slice(half * 512, (half + 1) * 512)
                b2 = next_pb(6)
                be_.append(b2)
                for kc in range(2):
                    mm(PB[b2][:, :], pT[:, kc, tb * 128:(tb + 1) * 128], wp[:, kc, sl], kc == 0, kc == 1,
                       ("pT", "wp"), ("PB%d" % b2,))
            for half in range(2):
                f_, fk = ft[half], "ft%d" % half
                act(f_[:, :], PB[bg_[half]][:, :], AF.Tanh, ("PB%d" % bg_[half],), (fk,), scale=0.5)
            for half in range(2):
                sl = slice(half * 512, (half + 1) * 512)
                f_, fk = ft[half], "ft%d" % half
                stt(eg[:, sl], f_[:, :], 1.0, PB[be_[half]][:, :], ALU.add, ALU.mult,
                    (fk, "PB%d" % be_[half], egk), (egk + "h%d" % half,))
            for half in range(2):
                sl = slice(half * 512, (half + 1) * 512)
                act(junk[:, 0:512], eg[:, sl], AF.Square, (egk + "h%d" % half,), ("stat%d" % c_ss2,),
                    accum_out=stat[:, c_ss2 + half:c_ss2 + half + 1])

        def ple_tail(tb):
            eg, egk = gvt[tb % 2], "gvt%d" % (tb % 2)
            c_ss2 = ple_c[tb]
            c_s1 = statcol(1)
            tt("dve", stat[:, c_s1:c_s1 + 1], stat[:, c_ss2:c_ss2 + 1], stat[:, c_ss2 + 1:c_ss2 + 2], ALU.add,
               ("stat%d" % c_ss2,), ("stat%d" % c_s1,))
            c_r1 = rsqrt_cols(c_s1, 1, 1.0 / D, 4.0 * EPS)
            stt(eg[:, :], eg[:, :], stat[:, c_r1:c_r1 + 1], gple_b[:, :], ALU.mult, ALU.mult,
                (egk + "h0", egk + "h1", "stat%d" % c_r1, "gple_b"), (egk + "h0", egk + "h1"))
            tt("dve", eg[:, :], eg[:, :], X3[:, tb, :], ALU.add, (egk + "h0", egk + "h1") + XK(Xk, tb),
               (egk + "h0", egk + "h1", egk))
            dma(y_d[r0 + tb * 128:r0 + (tb + 1) * 128, :], eg[:, :], (egk, egk + "h0", egk + "h1"), ("y",))

        ple_head(0)
        for tb in range(1, NB):
            ple_head(tb)
            ple_tail(tb - 1)
        ple_tail(NB - 1)
        release_block(2)

    for g in range(total_tiles):
        tile_body(g)

    P.emit(es)
    es.close()
    return nc


def _host_inputs(x, p, w_in, gmlp_ln_g, gmlp_ln_b, gmlp_w_s, gmlp_b_s, hgrn_lb_logits, hgrn_norm_g,
                 w_branch_a, w_branch_b, w_out, g_pre, g_post, w_ple, w_ple_gate, b_ple_gate, g_ple):
    f32 = np.float32

    def blocks(w):
        out = []
        for c0 in range(0, w.shape[1], 512):
            out.append(np.ascontiguousarray(w[:, c0:c0 + 512].reshape(8, 128, 512).transpose(1, 0, 2)).reshape(128, 4096))
        return out

    wb = blocks(np.asarray(w_in[0], f32)) + blocks(np.asarray(w_branch_a[0], f32)) + \
        blocks(np.asarray(w_branch_b[0], f32)) + blocks(np.asarray(w_out[0], f32)) + \
        blocks(np.asarray(w_ple_gate[0], f32))
    wblk = np.stack(wb, 0).astype(f32)

    def colF(v):
        return np.ascontiguousarray(np.asarray(v, f32).reshape(8, 128).T)

    tt_ = np.arange(128)
    triu = (tt_[:, None] <= tt_[None, :]).astype(f32)
    mbd = (triu * ((tt_[:, None] // CH) == (tt_[None, :] // CH))).astype(ml_dtypes.bfloat16)
    rm = np.ones((128, TT), f32)
    rm[:, ::CH] = 0.0
    common = {
        "wblk": wblk,
        "wp": np.ascontiguousarray(np.asarray(w_ple[0], f32).reshape(2, 128, D).transpose(1, 0, 2)).reshape(128, 2 * D),
        "gpreF": colF(g_pre[0]),
        "hgF": colF(np.asarray(hgrn_norm_g[0]).reshape(-1)),
        "lngF": colF(gmlp_ln_g[0]),
        "lnb": np.asarray(gmlp_ln_b[0], f32).reshape(1, D),
        "lng_row": np.asarray(gmlp_ln_g[0], f32).reshape(1, D),
        "bs": np.asarray(gmlp_b_s[0], f32).reshape(1, 512),
        "wsT": np.ascontiguousarray(np.asarray(gmlp_w_s[0], f32).transpose(2, 0, 1)).reshape(128, 512),
        "lbl": np.ascontiguousarray(np.asarray(hgrn_lb_logits, f32).reshape(2, 8, 128).transpose(2, 0, 1)).reshape(128, 16),
        "gpost_b": np.ascontiguousarray(np.broadcast_to(np.asarray(g_post[0], f32), (128, D))),
        "gple_b": np.ascontiguousarray(np.broadcast_to(np.asarray(g_ple[0], f32), (128, D))),
        "bg": np.asarray(b_ple_gate[0], f32).reshape(1, D),
        "ident": np.eye(128, dtype=f32).astype(ml_dtypes.bfloat16),
        "triu": triu,
        "mbd": mbd,
        "rm": rm,
    }
    return common


_CACHE = {}


def run(inputs, nseq, T, ncores, x_full, p_full):
    common = _host_inputs(**inputs)
    key = (nseq, T)
    nc = build(nseq, T)
    in_maps = []
    for c in range(ncores):
        m = dict(common)
        m["x"] = np.ascontiguousarray(x_full[c * nseq:(c + 1) * nseq].reshape(nseq * T, D))
        m["p"] = np.ascontiguousarray(p_full[c * nseq:(c + 1) * nseq].reshape(nseq * T, PLE))
        in_maps.append(m)
    res = run_bass_kernel_spmd(nc, in_maps, core_ids=list(range(ncores)))
    outs = [np.asarray(r["y"]).reshape(nseq, T, D) for r in res.results]
    return np.concatenate(outs, 0).astype(np.float32)


def kernel(**inputs):
    x = np.asarray(inputs["x"], np.float32)
    p = np.asarray(inputs["p"], np.float32)[0]
    B, T, _ = x.shape
    nseq = B // NCORES
    return run(inputs, nseq, T, NCORES, x, p)
```

```python
import math
from contextlib import ExitStack

import numpy as np
import ml_dtypes

import concourse.bass as bass
import concourse.mybir as mybir
from concourse.bass_utils import run_bass_kernel_spmd

F32 = mybir.dt.float32
BF16 = mybir.dt.bfloat16
AF = mybir.ActivationFunctionType
ALU = mybir.AluOpType
AX = mybir.AxisListType

D = 1024
PLE = 256
EPS = 1e-6
NCORES = 8
TT = 512
NB = TT // 128
CH = 64
NBLK = 26
NSLOT = 3
LAZY_CAST = True

G_U, G_V, G_ZA, G_Q, G_F, G_INP, G_ZB, G_AA, G_AB = range(9)
BLK_WA, BLK_WB, BLK_WO, BLK_WG = 18, 20, 22, 24


class _Op:
    __slots__ = ("eng", "fn", "deps", "is_dma", "sig", "count", "sem", "val", "gid")

    def __init__(self, eng, fn, is_dma):
        self.eng = eng
        self.fn = fn
        self.deps = set()
        self.is_dma = is_dma
        self.sig = False
        self.count = 0
        self.sem = None
        self.val = 0
        self.gid = 0


class Prog:
    ENGS = ("pe", "act", "dve", "pool", "sp")

    def __init__(self, nc):
        self.nc = nc
        self.ops = []
        self.last_w = {}
        self.readers = {}

    def add(self, eng, fn, reads=(), writes=(), dma=False):
        op = _Op(eng, fn, dma)
        op.gid = len(self.ops)
        deps = set()
        for k in reads:
            w = self.last_w.get(k)
            if w is not None:
                deps.add(w)
        for k in writes:
            w = self.last_w.get(k)
            if w is not None:
                deps.add(w)
            deps |= self.readers.get(k, set())
        deps.discard(op)
        op.deps = deps
        for k in reads:
            self.readers.setdefault(k, set()).add(op)
        for k in writes:
            self.last_w[k] = op
            self.readers[k] = set()
        self.ops.append(op)
        return op

    def emit(self, es, n_dma_sems=24):
        nc = self.nc
        engobj = {"pe": nc.tensor, "act": nc.scalar, "dve": nc.vector, "pool": nc.gpsimd, "sp": nc.sync}
        for op in self.ops:
            best = {}
            keep = set()
            for d in op.deps:
                if d.is_dma:
                    keep.add(d)
                    continue
                if d.eng == "pe" and op.eng == "pe":
                    continue
                if d.eng not in best or best[d.eng].gid < d.gid:
                    best[d.eng] = d
            for d in best.values():
                d.sig = True
                keep.add(d)
            op.deps = keep
        sems = {e: es.enter_context(nc.semaphore("sem_" + e)) for e in ("pe", "act", "dve", "pool")}
        dsems = [es.enter_context(nc.semaphore("dsem%d" % i)) for i in range(n_dma_sems)]
        dcount = [0] * n_dma_sems
        dprev = [None] * n_dma_sems
        cnt = {e: 0 for e in sems}
        ndma = 0
        for op in self.ops:
            if op.is_dma:
                i = ndma % n_dma_sems
                ndma += 1
                op.sem = dsems[i]
                dcount[i] += 16
                op.val = dcount[i]
                if dprev[i] is not None:
                    op.deps.add(dprev[i])
                dprev[i] = op
            elif op.sig:
                cnt[op.eng] += 1
                op.count = cnt[op.eng]
                op.sem = sems[op.eng]
                op.val = op.count
        per_eng = {e: [o for o in self.ops if o.eng == e] for e in self.ENGS}
        block = es.enter_context(nc.Block())

        def run(e):
            eo = engobj[e]
            waited = {}
            for op in per_eng[e]:
                need = {}
                for d in op.deps:
                    if (not d.is_dma) and d.eng == "pe" and e == "pe":
                        continue
                    key = id(d.sem)
                    if waited.get(key, 0) >= d.val:
                        continue
                    if key not in need or need[key][1] < d.val:
                        need[key] = (d.sem, d.val)
                for key, (s, v) in need.items():
                    eo.wait_ge(s, v)
                    waited[key] = v
                inst = op.fn()
                if op.is_dma:
                    inst.then_inc(op.sem, 16)
                elif op.sig:
                    inst.then_inc(op.sem, 1)
            return eo

        @block.tensor
        def _(eng):
            run("pe")

        @block.scalar
        def _(eng):
            run("act")

        @block.vector
        def _(eng):
            run("dve")

        @block.gpsimd
        def _(eng):
            run("pool")

        @block.sync
        def _(eng):
            eo = run("sp")
            for i in range(n_dma_sems):
                if dcount[i]:
                    eo.wait_ge(dsems[i], dcount[i])


def build(nseq, T, debug=False):
    assert T % TT == 0
    ntok = nseq * T
    ntile_seq = T // TT
    nc = bass.Bass("TRN2", target_bir_lowering=False)
    es = ExitStack()
    P = Prog(nc)

    def din(name, shape, dt=F32):
        return nc.dram_tensor(name, list(shape), dt, kind="ExternalInput").ap()

    x_d = din("x", [ntok, D])
    p_d = din("p", [ntok, PLE])
    wblk_d = din("wblk", [NBLK, 128, 8 * 512])
    wp_d = din("wp", [128, 2 * D])
    gpreF_d = din("gpreF", [128, 8])
    hgF_d = din("hgF", [128, 8])
    lngF_d = din("lngF", [128, 8])
    lnb_d = din("lnb", [1, D])
    lngr_d = din("lng_row", [1, D])
    bs_d = din("bs", [1, 512])
    wsT_d = din("wsT", [128, 512])
    lbl_d = din("lbl", [128, 16])
    gpost_d = din("gpost_b", [128, D])
    gple_d = din("gple_b", [128, D])
    bg_d = din("bg", [1, D])
    ident_d = din("ident", [128, 128], BF16)
    triu_d = din("triu", [128, 128])
    mbd_d = din("mbd", [128, 128], BF16)
    rm_d = din("rm", [128, TT])
    y_d = nc.dram_tensor("y", [ntok, D], F32, kind="ExternalOutput").ap()
    wsc_d = nc.dram_tensor("wsc", [NBLK, 128, 8 * 512], BF16, kind="Internal").ap()
    dbg_d = {}

    def sb(name, shape, dt=F32):
        return es.enter_context(nc.sbuf_tensor("s_" + name, list(shape), dt))

    def ps(name, shape, dt=F32):
        return es.enter_context(nc.psum_tensor("ps_" + name, list(shape), dt))

    ident = sb("ident", [128, 128], BF16)
    mbd = sb("mbd", [128, 128], BF16)
    rm = sb("rm", [128, TT])
    wsT = sb("wsT", [128, 4, 128], BF16)
    L2p = sb("L2p", [2, D])
    gpost_b = sb("gpost_b", [128, D])
    gple_b = sb("gple_b", [128, D])
    wp = sb("wp", [128, 2, D], BF16)
    small = sb("small", [128, 128])
    ones_row = sb("ones_row", [1, 128], BF16)
    bg_row = sb("bg_row", [1, D], BF16)
    neghalf = sb("neghalf", [128, 32])
    C_GPRE, C_HG, C_LNG, C_LSC, C_LBI, C_LHO = 0, 8, 16, 24, 32, 40
    gpreF = small[:, C_GPRE:C_GPRE + 8]
    hgF = small[:, C_HG:C_HG + 8]
    lngF = small[:, C_LNG:C_LNG + 8]
    lscF = small[:, C_LSC:C_LSC + 8]
    lbiF = small[:, C_LBI:C_LBI + 8]
    lhoF = small[:, C_LHO:C_LHO + 8]
    tmpc = small[:, 48:80]

    wslot = [sb("wslot%d" % i, [128, 8, 512], BF16) for i in range(NSLOT)]
    FB = [sb("FB%d" % i, [128, 8 * TT], BF16) for i in range(8)]
    F32B = [sb("F32B%d" % i, [128, 8 * TT]) for i in range(1)]
    QF = sb("QF", [128, 8, NB, 3, CH], BF16)
    xt = sb("xt", [128, NB, D])
    hn = [sb("hn%d" % i, [128, D], BF16) for i in range(2)]
    pT = sb("pT", [128, 2, TT], BF16)
    hb = [sb("hb%d" % i, [128, D], BF16) for i in range(2)]
    junk = sb("junk", [128, D], BF16)
    gvt = [sb("gvt%d" % i, [128, D]) for i in range(2)]
    ft = [sb("ft%d" % i, [128, TT]) for i in range(6)]
    fe = [sb("fe%d" % i, [128, TT], BF16) for i in range(4)]
    Sst = sb("Sst", [128, D])
    Sbf = [sb("Sbf%d" % i, [128, D], BF16) for i in range(3)]
    Am = [sb("Am%d" % i, [128, 8, 128], BF16) for i in range(2)]
    el = sb("el", [128, 8, 8])
    stat = sb("stat", [128, 256])

    PB = [ps("PB%d" % i, [128, 512]) for i in range(6)]
    TB = [ps("TB%d" % i, [128, 1024], BF16) for i in range(2)]

    st = {"pb": 0, "tb": 0, "dve_pool": 0, "stat": 0, "gv": 0, "ft": 0}

    def next_pb(n=6):
        i = st["pb"] % n
        st["pb"] += 1
        return i

    def next_tb():
        i = st["tb"] % 2
        st["tb"] += 1
        return i

    def statcol(n):
        c = st["stat"]
        if c + n > 256:
            c = 0
        st["stat"] = c + n
        return c

    def dma(out, in_, reads, writes):
        return P.add("sp", lambda: nc.sync.dma_start(out=out, in_=in_), reads, writes, dma=True)

    def act(out, in_, func, reads, writes, scale=None, bias=None, accum_out=None):
        kw = {}
        if scale is not None:
            kw["scale"] = scale
        if bias is not None:
            kw["bias"] = bias
        if accum_out is not None:
            kw["accum_out"] = accum_out
        return P.add("act", lambda: nc.scalar.activation(out=out, in_=in_, func=func, **kw), reads, writes)

    def veng(e):
        return nc.vector if e == "dve" else nc.gpsimd

    def tt(e, out, in0, in1, op, reads, writes):
        return P.add(e, lambda: veng(e).tensor_tensor(out=out, in0=in0, in1=in1, op=op), reads, writes)

    def ts(e, out, in0, s1, s2, op0, op1, reads, writes):
        if op1 is None:
            return P.add(e, lambda: veng(e).tensor_scalar(out=out, in0=in0, scalar1=s1, scalar2=None, op0=op0),
                         reads, writes)
        return P.add(e, lambda: veng(e).tensor_scalar(out=out, in0=in0, scalar1=s1, scalar2=s2, op0=op0, op1=op1),
                     reads, writes)

    def stt(out, in0, scalar, in1, op0, op1, reads, writes):
        return P.add("dve", lambda: nc.vector.scalar_tensor_tensor(out=out, in0=in0, scalar=scalar, in1=in1,
                                                                   op0=op0, op1=op1), reads, writes)

    def cp(e, out, in_, reads, writes):
        if e == "act":
            return P.add("act", lambda: nc.scalar.copy(out=out, in_=in_), reads, writes)
        return P.add(e, lambda: veng(e).tensor_copy(out=out, in_=in_), reads, writes)

    def mm(out, lhsT, rhs, start, stop, reads, writes):
        return P.add("pe", lambda: nc.tensor.matmul(out, lhsT, rhs, start=start, stop=stop), reads, writes)

    def tr(out, in_, reads, writes):
        return P.add("pe", lambda: nc.tensor.transpose(out, in_, ident[:, :]), reads + ("ident",), writes)

    def rsqrt_cols(c_in, n, scale, eps, e="dve"):
        c_ms = statcol(n)
        c_out = statcol(n)
        ts(e, stat[:, c_ms:c_ms + n], stat[:, c_in:c_in + n], scale, eps, ALU.mult, ALU.add,
           ("stat%d" % c_in,), ("stat%d" % c_ms,))
        tt("pool", stat[:, c_out:c_out + n], stat[:, c_ms:c_ms + n], neghalf[:, 0:n], ALU.pow,
           ("stat%d" % c_ms, "neghalf"), ("stat%d" % c_out,))
        return c_out

    stgA = F32B[0][:, :]
    stgB = xt[:, :, :].rearrange("p a b -> p (a b)")
    def XKp(k):
        return tuple("%s.%d" % (k, i) for i in range(NB))

    EAGER = (BLK_WO, BLK_WO + 1, BLK_WG, BLK_WG + 1) if LAZY_CAST else tuple(range(NBLK))
    for b in EAGER:
        stg, sk = ((stgA, "F32B0"), (stgB, "xt"))[b % 2]
        cb, ck = ((FB[4], "FB4"), (FB[5], "FB5"))[b % 2]
        dma(stg, wblk_d[b], XKp(sk), XKp(sk))
        fold = gpreF if b < 18 else (hgF if b in (BLK_WB, BLK_WB + 1) else None)
        if fold is None:
            cp("dve", cb[:, 0:2048], stg[:, 0:2048], XKp(sk), (ck,))
            cp("act", cb[:, 2048:4096], stg[:, 2048:4096], XKp(sk), (ck,))
        else:
            for kc in range(8):
                if kc % 2 == 0:
                    ts("dve", cb[:, kc * 512:(kc + 1) * 512], stg[:, kc * 512:(kc + 1) * 512], fold[:, kc:kc + 1],
                       None, ALU.mult, None, XKp(sk) + ("small",), (ck,))
                else:
                    act(cb[:, kc * 512:(kc + 1) * 512], stg[:, kc * 512:(kc + 1) * 512], AF.Copy,
                        XKp(sk) + ("small",), (ck,), scale=fold[:, kc:kc + 1])
        dma(wsc_d[b], cb[:, :], (ck,), ("wsc%d" % b,))

    dma(ident[:, :], ident_d, (), ("ident",))
    dma(mbd[:, :], mbd_d, (), ("mbd",))
    dma(rm[:, :], rm_d, (), ("rm",))
    dma(gpost_b[:, :], gpost_d, (), ("gpost_b",))
    dma(gple_b[:, :], gple_d, (), ("gple_b",))
    dma(small[:, C_GPRE:C_GPRE + 8], gpreF_d, (), ("small",))
    dma(small[:, C_HG:C_HG + 8], hgF_d, (), ("small",))
    dma(small[:, C_LNG:C_LNG + 8], lngF_d, (), ("small",))
    P.add("dve", lambda: nc.vector.memset(neghalf[:, :], -0.5), (), ("neghalf",))
    P.add("dve", lambda: nc.vector.memset(QF[:, :, :, :, :], 0.0), (), ("QF",))
    P.add("pool", lambda: nc.gpsimd.memset(ones_row[:, :], 1.0), (), ("ones_row",))
    lbl = ft[0]
    dma(lbl[:, 0:16], lbl_d, (), ("ft0",))
    tt("dve", tmpc[:, 0:8], lbl[:, 0:8], lbl[:, 8:16], ALU.subtract, ("ft0",), ("tmpc",))
    act(tmpc[:, 8:16], tmpc[:, 0:8], AF.Sigmoid, ("tmpc",), ("tmpc",), scale=-1.0)
    ts("dve", small[:, C_LSC:C_LSC + 8], tmpc[:, 8:16], -0.5, None, ALU.mult, None, ("tmpc",), ("small",))
    ts("dve", small[:, C_LBI:C_LBI + 8], tmpc[:, 8:16], -0.5, 1.0, ALU.mult, ALU.add, ("tmpc",), ("small",))
    act(small[:, C_LHO:C_LHO + 8], tmpc[:, 8:16], AF.Ln, ("tmpc",), ("small",), scale=0.5)
    bgf = ft[1]
    dma(bgf[0:1, 0:512], bg_d[:, 0:512], (), ("ft1",))
    cp("dve", bg_row[0:1, 0:512], bgf[0:1, 0:512], ("ft1",), ("bg_row",))
    dma(bgf[0:1, 0:512], bg_d[:, 512:1024], ("ft1",), ("ft1",))
    cp("dve", bg_row[0:1, 512:1024], bgf[0:1, 0:512], ("ft1",), ("bg_row",))
    wpf = gvt[0]
    for h in range(2):
        dma(wpf[:, :], wp_d[:, h * D:(h + 1) * D], ("gvt0",), ("gvt0",))
        cp("dve", wp[:, h, :], wpf[:, :], ("gvt0",), ("wp",))
    wsf = ft[2]
    triu = ft[3]
    dma(wsf[:, :], wsT_d, (), ("ft2",))
    dma(triu[:, 0:128], triu_d, (), ("ft3",))
    tt("dve", wsT[:, :, :], wsf[:, :].rearrange("p (g t) -> p g t", g=4),
       triu[:, 0:128].unsqueeze(1).broadcast_to([128, 4, 128]), ALU.mult, ("ft2", "ft3"), ("wsT",))
    onescol = sb("onescol", [128, 1], BF16)
    P.add("dve", lambda: nc.vector.memset(onescol[:, :], 1.0), (), ("onescol",))
    L2 = gvt[1][0:2, :]
    R2 = ft[5][0:2, :]
    G2 = gvt[0][0:2, :]
    P.add("dve", lambda: nc.vector.memset(L2[:, :], 1.0), (), ("gvt1",))
    dma(L2[0:1, :], lnb_d, ("gvt1",), ("gvt1",))
    dma(G2[0:1, :], lngr_d, (), ("gvt0",))
    dma(G2[1:2, :], lngr_d, (), ("gvt0",))
    P.add("dve", lambda: nc.vector.reciprocal(out=G2[:, :], in_=G2[:, :]), ("gvt0",), ("gvt0",))
    tt("dve", L2p[:, :], L2[:, :], G2[:, :], ALU.mult, ("gvt1", "gvt0"), ("L2p",))
    mm(PB[0][0:1, :], onescol[:, 0:1], wsT[:, :, :].rearrange("p g t -> p (g t)"), True, True,
       ("onescol", "wsT"), ("PB0",))
    cp("dve", R2[0:1, :], PB[0][0:1, :], ("PB0",), ("ft5",))
    dma(R2[1:2, :], bs_d, ("ft5",), ("ft5",))

    ring = {"n": 0}
    lazy = set(range(NBLK)) - set(EAGER)

    def issue_block(blk):
        s = ring["n"] % NSLOT
        ring["n"] += 1
        sk_ = "wslot%d" % s
        if blk in lazy:
            lazy.discard(blk)
            fold = gpreF if blk < 18 else (hgF if blk in (BLK_WB, BLK_WB + 1) else None)
            for hf in range(2):
                stg = xt[:, 2 * hf:2 * hf + 2, :].rearrange("p a b -> p (a b)")
                lk = "xtL%d" % hf
                dma(stg, wblk_d[blk][:, hf * 2048:(hf + 1) * 2048], (), (lk,) + XKp("xt"))
                if fold is None:
                    cp("pool", wslot[s][:, 4 * hf:4 * hf + 4, :].rearrange("p a b -> p (a b)"), stg, (lk,), (sk_,))
                else:
                    for k4 in range(4):
                        kc = 4 * hf + k4
                        ts("pool", wslot[s][:, kc, :], stg[:, k4 * 512:(k4 + 1) * 512], fold[:, kc:kc + 1], 0.0,
                           ALU.mult, ALU.add, (lk, "small"), (sk_,))
            dma(wsc_d[blk], wslot[s][:, :, :].rearrange("p a b -> p (a b)"), (sk_,), ("wsc%d" % blk,))
        else:
            dma(wslot[s][:, :, :].rearrange("p a b -> p (a b)"), wsc_d[blk], ("wsc%d" % blk, sk_), (sk_,))
        return s

    total_tiles = nseq * ntile_seq

    def gb(*gs):
        out = []
        for g in gs:
            out += [2 * g, 2 * g + 1]
        return out

    stream = []
    for g_ in range(total_tiles):
        if g_ == 0:
            stream += gb(G_U, G_ZA, G_AA)
        stream += (gb(G_V) if g_ == 0 else []) + gb(G_Q, G_F) + [BLK_WA, BLK_WA + 1] + gb(G_ZB, G_AB, G_INP)
        if g_ + 1 < total_tiles:
            stream += gb(G_ZA)
        stream += [BLK_WB, BLK_WB + 1, BLK_WO, BLK_WO + 1]
        if g_ + 1 < total_tiles:
            stream += [2 * G_U] + gb(G_AA)
        stream += [BLK_WG, BLK_WG + 1]
    sp_ = {"issued": 0, "used": 0}
    slot_of = {}

    def prefetch():
        while sp_["issued"] < len(stream) and sp_["issued"] - sp_["used"] < NSLOT:
            i = sp_["issued"]
            slot_of[i] = issue_block(stream[i])
            sp_["issued"] += 1

    def take_block(expect, off=0):
        i = sp_["used"] + off
        assert stream[i] == expect, (stream[i], expect)
        assert i < sp_["issued"]
        return i, slot_of[i]

    def release_block(n=1):
        sp_["used"] += n
        prefetch()

    prefetch()

    kchunk = {"k": 0}
    XBUF = [(xt[:, :, :].rearrange("p a b -> p (a b)"), "xt"), (F32B[0][:, :], "F32B0")]

    def XK(k, tb=None):
        if tb is None:
            return tuple("%s.%d" % (k, i) for i in range(NB))
        return ("%s.%d" % (k, tb),)

    HBUF = [(FB[0], "FB0"), (FB[7], "FB7")]
    s1 = {}

    def F3(i):
        return FB[i][:, :].rearrange("p (a b) -> p a b", a=8)

    def T3(i):
        return FB[i][:, :].rearrange("p (a b) -> p a b", a=NB)

    def stage1_load(g):
        X, Xk = XBUF[g % 2]
        r0 = g * TT
        wk = XK(Xk) + (("xtL0", "xtL1") if Xk == "xt" else ())
        dma(X.rearrange("p (a b) -> p a b", a=NB), x_d[r0:r0 + TT, :].rearrange("(b p) d -> p b d", p=128),
            XK(Xk), wk)

    def stage1_stats(g):
        X, Xk = XBUF[g % 2]
        X3 = X.rearrange("p (a b) -> p a b", a=NB)
        c_ss = statcol(NB)
        for tb in range(NB):
            act(junk[:, :], X3[:, tb, :], AF.Square, XK(Xk, tb), ("stat%d" % c_ss,),
                accum_out=stat[:, c_ss + tb:c_ss + tb + 1])
        s1[g] = rsqrt_cols(c_ss, NB, 1.0 / D, EPS)

    def stage1_scale(g, tb):
        X, Xk = XBUF[g % 2]
        X3 = X.rearrange("p (a b) -> p a b", a=NB)
        c_r = s1[g]
        h, hk = hn[tb % 2], "hn%d" % (tb % 2)
        ts("dve", h[:, :], X3[:, tb, :], stat[:, c_r + tb:c_r + tb + 1], None, ALU.mult, None,
           XK(Xk, tb) + ("stat%d" % c_r,), (hk,))

    def stage1_tr(g, tb):
        hTn, hTnk = HBUF[g % 2]
        hTn3 = hTn[:, :].rearrange("p (a b) -> p a b", a=8)
        h, hk = hn[tb % 2], "hn%d" % (tb % 2)
        tbi = next_tb()
        for kc in range(8):
            tr(TB[tbi][:, kc * 128:(kc + 1) * 128], h[:, kc * 128:(kc + 1) * 128], (hk,), ("TB%d" % tbi,))
        cp("act", hTn3[:, :, tb * 128:(tb + 1) * 128], TB[tbi][:, :].rearrange("p (a b) -> p a b", a=8),
           ("TB%d" % tbi,), (hTnk,))

    def tile_body(g):
        r0 = g * TT
        X, Xk = XBUF[g % 2]
        S, Sk = XBUF[(g + 1) % 2]
        X3 = X.rearrange("p (a b) -> p a b", a=NB)
        hT, hTk = HBUF[g % 2]
        hT3 = hT[:, :].rearrange("p (a b) -> p a b", a=8)
        has_next = g + 1 < total_tiles

        if g == 0:
            stage1_load(0)
            stage1_stats(0)
            for tb in range(NB):
                stage1_scale(0, tb)
                stage1_tr(0, tb)
        if g % ntile_seq == 0:
            P.add("pool", lambda: nc.gpsimd.memset(Sst[:, :], 0.0), (), ("Sst0", "Sst1"))
            k0 = kchunk["k"] % 3
            P.add("pool", lambda: nc.gpsimd.memset(Sbf[k0][:, :], 0.0), (), ("Sbf%d" % k0,))

        pt3 = gvt[1][:, :].rearrange("p (a b) -> p a b", a=NB)
        dma(pt3, p_d[r0:r0 + TT, :].rearrange("(b p) d -> p b d", p=128), ("gvt1",), ("gvt1",))
        ptb = hb[0][:, :].rearrange("p (a b) -> p a b", a=NB)
        cp("pool", ptb[:, :, :], pt3, ("gvt1",), ("hb0",))

        def proj_F(gg, evac, src=None, wsrc=None):
            h3_, hk_ = src if src is not None else (hT3, hTk)
            for half in range(2):
                if wsrc is not None and half in wsrc:
                    w3, wk_ = wsrc[half]
                    ring_ = False
                else:
                    _, s = take_block(2 * gg + half)
                    w3, wk_ = wslot[s], "wslot%d" % s
                    ring_ = True
                for cb_ in range(4):
                    b = next_pb()
                    for kc in range(8):
                        mm(PB[b][:, :], w3[:, kc, cb_ * 128:(cb_ + 1) * 128], h3_[:, kc, :], kc == 0, kc == 7,
                           (wk_, hk_), ("PB%d" % b,))
                    evac(half * 4 + cb_, PB[b], "PB%d" % b)
                if ring_:
                    release_block()

        def proj_T(gg, evac, wsrc=None):
            for half in range(2):
                if wsrc is None:
                    _, s = take_block(2 * gg + half)
                    w3, wk_ = wslot[s], "wslot%d" % s
                else:
                    w3, wk_ = wsrc[half]
                for tb in range(NB):
                    b = next_pb()
                    for kc in range(8):
                        mm(PB[b][:, :], hT3[:, kc, tb * 128:(tb + 1) * 128], w3[:, kc, :], kc == 0, kc == 7,
                           (wk_, hTk), ("PB%d" % b,))
                    evac(tb, half, PB[b], "PB%d" % b)
                if wsrc is None:
                    release_block()

        guT, szaT, taaT = F3(5), F3(6), F3(4)

        def proj_uza(which, src=None):
            if which == 0:
                wsrc_ = None
                if src is not None:
                    wsrc_ = {1: (FB[4][:, :].rearrange("p (a b) -> p a b", a=8), "FB4")}
                proj_F(G_U, lambda j, bk, k: act(guT[:, j, :], bk[:, :], AF.Gelu, (k,), ("FB5",)), src, wsrc_)
            elif which == 1:
                proj_F(G_ZA, lambda j, bk, k: act(szaT[:, j, :], bk[:, :], AF.Silu, (k,), ("FB6",)), src)
            else:
                proj_F(G_AA, lambda j, bk, k: act(taaT[:, j, :], bk[:, :], AF.Tanh, (k,), ("FB4",), scale=0.5), src)

        def make_gz():
            for j in range(8):
                tt("pool", szaT[:, j, :], szaT[:, j, :], guT[:, j, :], ALU.mult, ("FB6", "FB5"), ("FB6",))

        if g == 0:
            for w_ in range(3):
                proj_uza(w_)
            make_gz()

        vn = T3(2)
        gv = S.rearrange("p (a b) -> p a b", a=NB)
        vsrc = None
        if g > 0:
            hprev, hprevk = HBUF[(g - 1) % 2]
            vsrc = [(FB[2][:, :].rearrange("p (a b) -> p a b", a=8), "FB2"),
                    (hprev[:, :].rearrange("p (a b) -> p a b", a=8), hprevk)]
        proj_T(G_V, lambda tb, half, bk, k: act(gv[:, tb, half * 512:(half + 1) * 512], bk[:, :], AF.Gelu, (k,),
                                               XK(Sk, tb)), vsrc)
        for tb in range(NB):
            c_bs = statcol(12)
            for hh in range(2):
                P.add("dve", (lambda tb=tb, hh=hh, c=c_bs: nc.vector.bn_stats(
                    out=stat[:, c + 6 * hh:c + 6 * hh + 6], in_=gv[:, tb, hh * 512:(hh + 1) * 512])),
                    XK(Sk, tb), ("stat%d" % c_bs,))
            c_mv = statcol(2)
            P.add("dve", (lambda c=c_bs, m=c_mv: nc.vector.bn_aggr(out=stat[:, m:m + 2], in_=stat[:, c:c + 12])),
                  ("stat%d" % c_bs,), ("stat%d" % c_mv,))
            c_var = statcol(1)
            cp("dve", stat[:, c_var:c_var + 1], stat[:, c_mv + 1:c_mv + 2], ("stat%d" % c_mv,), ("stat%d" % c_var,))
            c_rs = rsqrt_cols(c_var, 1, 1.0, EPS)
            c_nm = statcol(1)
            stt(stat[:, c_nm:c_nm + 1], stat[:, c_mv:c_mv + 1], -1.0, stat[:, c_rs:c_rs + 1], ALU.mult, ALU.mult,
                ("stat%d" % c_mv, "stat%d" % c_rs), ("stat%d" % c_nm,))
            ts("dve", vn[:, tb, :], gv[:, tb, :], stat[:, c_rs:c_rs + 1], stat[:, c_nm:c_nm + 1], ALU.mult, ALU.add,
               XK(Sk, tb) + ("stat%d" % c_rs, "stat%d" % c_nm), ("FB2",))

        sqF = F3(1)
        tfall = S.rearrange("p (a b) -> p a b", a=8)
        proj_F(G_Q, lambda j, bk, k: act(sqF[:, j, :], bk[:, :], AF.Silu, (k,), ("FB1",)))
        proj_F(G_F, lambda j, bk, k: act(tfall[:, j, :], bk[:, :], AF.Tanh, (k,), XK(Sk, j // 2), scale=-0.5))

        KF, KFk = F3(3), "FB3"
        lnsc = math.log(128.0 ** -0.5)

        def f_ln(j):
            lf, lfk = ft[j % 3], "ft%d" % (j % 3)
            act(lf[:, :], tfall[:, j, :], AF.Ln, XK(Sk, j // 2) + ("small",), (lfk,), scale=lscF[:, j:j + 1],
                bias=lbiF[:, j:j + 1])

        def f_scan(j):
            lf, lfk = ft[j % 3], "ft%d" % (j % 3)
            cm, cmk = ft[3 + (j % 2)], "ft%d" % (3 + j % 2)
            P.add("dve", (lambda cm=cm, lf=lf: nc.vector.tensor_tensor_scan(
                out=cm[:, :], data0=rm[:, :], data1=lf[:, :], initial=0.0, op0=ALU.mult, op1=ALU.add)),
                (lfk, "rm"), (cmk,))

        def f_rest(j):
            cm, cmk = ft[3 + (j % 2)], "ft%d" % (3 + j % 2)
            E, Ek = fe[j % 2], "fe%d" % (j % 2)
            Ei, Eik = fe[2 + j % 2], "fe%d" % (2 + j % 2)
            act(E[:, :], cm[:, :], AF.Exp, (cmk,), (Ek,), bias=float(lnsc))
            act(Ei[:, :], cm[:, :], AF.Exp, (cmk, "small"), (Eik,), scale=-1.0, bias=lhoF[:, j:j + 1])
            act(el[:, j, :], cm[:, :].rearrange("p (c i) -> p c i", i=CH)[:, :, CH - 1], AF.Exp, (cmk,), ("el",))
            tt("pool", QF[:, j, :, 0:3:2, :], sqF[:, j, :].rearrange("p (a b c) -> p a b c", a=NB, b=2),
               E[:, :].rearrange("p (a b c) -> p a b c", a=NB, b=2), ALU.mult, ("FB1", Ek), ("QF",))
            stt(KF[:, j, :], tfall[:, j, :], 1.0, Ei[:, :], ALU.add, ALU.mult, XK(Sk, j // 2) + (Eik,), (KFk,))

        tbi = next_tb()
        for tb in range(NB):
            for kc in range(2):
                tr(TB[tbi][:, (tb * 2 + kc) * 128:(tb * 2 + kc + 1) * 128], ptb[:, tb, kc * 128:(kc + 1) * 128],
                   ("hb0",), ("TB%d" % tbi,))
        cp("dve", pT[:, :, :].rearrange("p k (b t) -> p b k t", b=NB),
           TB[tbi][:, :].rearrange("p (b k t) -> p b k t", b=NB, k=2), ("TB%d" % tbi,), ("pT",))

        yapreT = guT

        def spatial(j):
            gi_ = j // 2
            b = 3 + (j % 2)
            for tb in range(NB):
                mm(PB[b][:, tb * 128:(tb + 1) * 128], vn[:, tb, j * 128:(j + 1) * 128], wsT[:, gi_, :], True, False,
                   ("FB2", "wsT"), ("PB%d" % b,))
                mm(PB[b][:, tb * 128:(tb + 1) * 128], L2p[0:2, j * 128:(j + 1) * 128],
                   ft[5][0:2, gi_ * 128:(gi_ + 1) * 128], False, True, ("L2p", "ft5"), ("PB%d" % b,))
            stt(yapreT[:, j, :], PB[b][:, :], lngF[:, j:j + 1], szaT[:, j, :], ALU.mult, ALU.mult,
                ("PB%d" % b, "small", "FB6"), ("FB5",))

        f_ln(0)
        f_ln(1)
        f_scan(0)
        for j in range(8):
            if j + 2 < 8:
                f_ln(j + 2)
            if j + 1 < 8:
                f_scan(j + 1)
            f_rest(j)
            spatial(j)

        if has_next:
            stage1_load(g + 1)

        maT = taaT
        for half in range(2):
            _, s = take_block(BLK_WA + half)
            for cb_ in range(4):
                jo = half * 4 + cb_
                b = next_pb()
                for kc in range(8):
                    mm(PB[b][:, :], wslot[s][:, kc, cb_ * 128:(cb_ + 1) * 128], yapreT[:, kc, :], kc == 0, kc == 7,
                       ("wslot%d" % s, "FB5"), ("PB%d" % b,))
                stt(maT[:, jo, :], taaT[:, jo, :], 1.0, PB[b][:, :], ALU.add, ALU.mult, ("FB4", "PB%d" % b), ("FB4",))
            release_block()

        if has_next:
            stage1_stats(g + 1)
        KT, KTk = T3(1), "FB1"
        for tb in range(NB):
            tbi = next_tb()
            for j in range(8):
                tr(TB[tbi][:, j * 128:(j + 1) * 128], KF[:, j, tb * 128:(tb + 1) * 128], (KFk,), ("TB%d" % tbi,))
            cp("act", KT[:, tb, :], TB[tbi][:, :], ("TB%d" % tbi,), (KTk,))
        szbT, tabT, V_T = T3(2), F3(5), T3(6)
        proj_T(G_ZB, lambda tb, half, bk, k: act(szbT[:, tb, half * 512:(half + 1) * 512], bk[:, :], AF.Silu, (k,),
                                                ("FB2",)))
        proj_F(G_AB, lambda j, bk, k: act(tabT[:, j, :], bk[:, :], AF.Tanh, (k,), ("FB5",), scale=0.5))
        proj_T(G_INP, lambda tb, half, bk, k: cp("dve", V_T[:, tb, half * 512:(half + 1) * 512], bk[:, :], (k,),
                                                 ("FB6",)))

        obT, obk = hT3, hTk
        pend = []

        def T_out(tb):
            on_, onk = hb[tb % 2], "hb%d" % (tb % 2)
            tbi = next_tb()
            for j in range(8):
                tr(TB[tbi][:, j * 128:(j + 1) * 128], on_[:, j * 128:(j + 1) * 128], (onk,), ("TB%d" % tbi,))
            cp("act", obT[:, :, tb * 128:(tb + 1) * 128], TB[tbi][:, :].rearrange("p (a b) -> p a b", a=8),
               ("TB%d" % tbi,), (obk,))

        for tb in range(NB):
            kA = kchunk["k"]
            kB = kA + 1
            kchunk["k"] += 2
            cA, cB = 2 * tb, 2 * tb + 1
            am, amk = Am[tb % 2], "Am%d" % (tb % 2)
            if has_next:
                stage1_scale(g + 1, tb)
            for hb_ in range(2):
                for jj in range(4):
                    j = hb_ * 4 + jj
                    mm(PB[hb_][:, jj * 128:(jj + 1) * 128], KF[:, j, tb * 128:(tb + 1) * 128], QF[:, j, tb, 0:3:2, :],
                       True, True, (KFk, "QF"), ("PB%d" % hb_,))
                tt("dve", am[:, hb_ * 4:(hb_ + 1) * 4, :], PB[hb_][:, :].rearrange("p (a b) -> p a b", a=4),
                   mbd[:, :].unsqueeze(1).broadcast_to([128, 4, 128]), ALU.mult, ("PB%d" % hb_, "mbd"), (amk,))
            for (c, k, lo) in ((cA, kA, 0), (cB, kB, 64)):
                for hb_ in range(2):
                    b = 4 + hb_
                    for jj in range(4):
                        j = hb_ * 4 + jj
                        mm(PB[b][:, jj * 128:(jj + 1) * 128], KT[lo:lo + 64, tb, j * 128:(j + 1) * 128],
                           V_T[lo:lo + 64, tb, j * 128:(j + 1) * 128], True, True, (KTk, "FB6"), ("PB%d" % b,))
                    sl = slice(hb_ * 512, (hb_ + 1) * 512)
                    sk_ = "Sst%d" % hb_
                    tt("dve", Sst[:, sl], PB[b][:, :], Sst[:, sl], ALU.add, ("PB%d" % b, sk_), (sk_,))
                    tt("dve", Sst[:, sl].rearrange("p (a b) -> p a b", a=4),
                       Sst[:, sl].rearrange("p (a b) -> p a b", a=4),
                       el[:, hb_ * 4:(hb_ + 1) * 4, c:c + 1].broadcast_to([128, 4, 128]), ALU.mult,
                       (sk_, "el"), (sk_,))
                kn = (k + 1) % 3
                cp("act", Sbf[kn][:, :], Sst[:, :], ("Sst0", "Sst1"), ("Sbf%d" % kn,))
                if lo == 0 and has_next:
                    stage1_tr(g + 1, tb)
            sA, sB = kA % 3, kB % 3
            for hb_ in range(2):
                b = 2 + hb_
                for jj in range(4):
                    j = hb_ * 4 + jj
                    o_ = PB[b][:, jj * 128:(jj + 1) * 128]
                    mm(o_, am[:, j, :], V_T[:, tb, j * 128:(j + 1) * 128], True, False, (amk, "FB6"), ("PB%d" % b,))
                    mm(o_, QF[:, j, tb, 0:2, :], Sbf[sA][:, j * 128:(j + 1) * 128], False, False,
                       ("QF", "Sbf%d" % sA), ("PB%d" % b,))
                    mm(o_, QF[:, j, tb, 1:3, :], Sbf[sB][:, j * 128:(j + 1) * 128], False, True,
                       ("QF", "Sbf%d" % sB), ("PB%d" % b,))
            oraw, ork = gvt[tb % 2], "gvt%d" % (tb % 2)
            for hb_ in range(2):
                cp("act", oraw[:, hb_ * 512:(hb_ + 1) * 512], PB[2 + hb_][:, :], ("PB%d" % (2 + hb_),), (ork,))
            c_s8 = statcol(8)
            for j in range(8):
                act(junk[:, 0:128], oraw[:, j * 128:(j + 1) * 128], AF.Square, (ork,), ("stat%d" % c_s8,),
                    accum_out=stat[:, c_s8 + j:c_s8 + j + 1])
            c_r8 = rsqrt_cols(c_s8, 8, 1.0 / 128.0, EPS, e="pool")
            tt("pool", oraw[:, :].rearrange("p (a b) -> p a b", a=8), oraw[:, :].rearrange("p (a b) -> p a b", a=8),
               stat[:, c_r8:c_r8 + 8].unsqueeze(2).broadcast_to([128, 8, 128]), ALU.mult,
               (ork, "stat%d" % c_r8), (ork,))
            on_, onk = hb[tb % 2], "hb%d" % (tb % 2)
            tt("pool", on_[:, :], oraw[:, :], szbT[:, tb, :], ALU.mult, (ork, "FB2"), (onk,))
            if tb == NB - 1 and has_next:
                hTn_, hTnk_ = HBUF[(g + 1) % 2]
                nsrc = (hTn_[:, :].rearrange("p (a b) -> p a b", a=8), hTnk_)
                proj_uza(1, nsrc)
            if pend:
                T_out(pend.pop())
            pend.append(tb)
        T_out(pend.pop())

        mgT, mgk = tabT, "FB5"
        for half in range(2):
            _, s = take_block(BLK_WB + half)
            for cb_ in range(4):
                jo = half * 4 + cb_
                b = next_pb()
                for kc in range(8):
                    mm(PB[b][:, :], wslot[s][:, kc, cb_ * 128:(cb_ + 1) * 128], obT[:, kc, :], kc == 0, kc == 7,
                       ("wslot%d" % s, obk), ("PB%d" % b,))
                f_, fk = ft[jo % 2], "ft%d" % (jo % 2)
                stt(f_[:, :], tabT[:, jo, :], 1.0, PB[b][:, :], ALU.add, ALU.mult, ("FB5", "PB%d" % b), (fk,))
                tt("dve", mgT[:, jo, :], f_[:, :], maT[:, jo, :], ALU.add, (fk, "FB4"), (mgk,))
            release_block()

        if has_next:
            dma(FB[2][:, :], wsc_d[2 * G_V], ("wsc%d" % (2 * G_V), "FB2"), ("FB2",))
            dma(hT[:, :], wsc_d[2 * G_V + 1], ("wsc%d" % (2 * G_V + 1), hTk), (hTk,))
            dma(FB[4][:, :], wsc_d[2 * G_U + 1], ("wsc%d" % (2 * G_U + 1), "FB4"), ("FB4",))

        if g == 0 and LAZY_CAST:
            assert not lazy
            stage1_load(0)
        _, s0 = take_block(BLK_WO, 0)
        _, s1_ = take_block(BLK_WO + 1, 1)
        x1b, x1bk = T3(3), "FB3"
        for tb in range(NB):
            banks = []
            c_ss2 = statcol(2)
            for half, s in ((0, s0), (1, s1_)):
                b = next_pb(6)
                banks.append(b)
                for kc in range(8):
                    mm(PB[b][:, :], mgT[:, kc, tb * 128:(tb + 1) * 128], wslot[s][:, kc, :], kc == 0, kc == 7,
                       ("wslot%d" % s, mgk), ("PB%d" % b,))
                act(junk[:, 0:512], PB[b][:, :], AF.Square, ("PB%d" % b,), ("stat%d" % c_ss2,),
                    accum_out=stat[:, c_ss2 + half:c_ss2 + half + 1])
            c_s1 = statcol(1)
            tt("dve", stat[:, c_s1:c_s1 + 1], stat[:, c_ss2:c_ss2 + 1], stat[:, c_ss2 + 1:c_ss2 + 2], ALU.add,
               ("stat%d" % c_ss2,), ("stat%d" % c_s1,))
            c_r1 = rsqrt_cols(c_s1, 1, 1.0 / D, 4.0 * EPS)
            for half in range(2):
                b = banks[half]
                f_, fk = ft[half], "ft%d" % half
                sl = slice(half * 512, (half + 1) * 512)
                stt(f_[:, :], PB[b][:, :], stat[:, c_r1:c_r1 + 1], gpost_b[:, sl], ALU.mult, ALU.mult,
                    ("PB%d" % b, "stat%d" % c_r1, "gpost_b"), (fk,))
                tt(("dve", "pool")[half], X3[:, tb, sl], X3[:, tb, sl], f_[:, :], ALU.add, XK(Xk, tb) + (fk,),
                   XK(Xk, tb))
            cp("act", x1b[:, tb, :], X3[:, tb, :], XK(Xk, tb), (x1bk,))
        release_block(2)
        if has_next:
            proj_uza(0, nsrc)
            make_gz()

        x1T, x1Tk = F3(1), "FB1"
        for tb in range(NB):
            tbi = next_tb()
            for kc in range(8):
                tr(TB[tbi][:, kc * 128:(kc + 1) * 128], x1b[:, tb, kc * 128:(kc + 1) * 128], (x1bk,),
                   ("TB%d" % tbi,))
            cp("act", x1T[:, :, tb * 128:(tb + 1) * 128], TB[tbi][:, :].rearrange("p (a b) -> p a b", a=8),
               ("TB%d" % tbi,), (x1Tk,))

        if has_next:
            proj_uza(2, nsrc)

        _, s0 = take_block(BLK_WG, 0)
        _, s1_ = take_block(BLK_WG + 1, 1)
        ple_c = {}

        def ple_head(tb):
            eg, egk = gvt[tb % 2], "gvt%d" % (tb % 2)
            c_ss2 = statcol(2)
            ple_c[tb] = c_ss2
            bg_, be_ = [], []
            for half, s in ((0, s0), (1, s1_)):
                sl = slice(half * 512, (half + 1) * 512)
                b = next_pb(6)
                bg_.append(b)
                for kc in range(8):
                    mm(PB[b][:, :], x1T[:, kc, tb * 128:(tb + 1) * 128], wslot[s][:, kc, :], kc == 0, False,
                       ("wslot%d" % s, x1Tk), ("PB%d" % b,))
                mm(PB[b][:, :], ones_row[0:1, :], bg_row[0:1, sl], False, True, ("ones_row", "bg_row"),
                   ("PB%d" % b,))
            for half in range(2):
                sl = slice(half * 512, (half + 1) * 512)
                b2 = next_pb(6)
                be_.append(b2)
                for kc in range(2):
                    mm(PB[b2][:, :], pT[:, kc, tb * 128:(tb + 1) * 128], wp[:, kc, sl], kc == 0, kc == 1,
                       ("pT", "wp"), ("PB%d" % b2,))
            for half in range(2):
                f_, fk = ft[half], "ft%d" % half
                act(f_[:, :], PB[bg_[half]][:, :], AF.Tanh, ("PB%d" % bg_[half],), (fk,), scale=0.5)
            for half in range(2):
                sl = slice(half * 512, (half + 1) * 512)
                f_, fk = ft[half], "ft%d" % half
                stt(eg[:, sl], f_[:, :], 1.0, PB[be_[half]][:, :], ALU.add, ALU.mult,
                    (fk, "PB%d" % be_[half], egk), (egk + "h%d" % half,))
            for half in range(2):
                sl = slice(half * 512, (half + 1) * 512)
                act(junk[:, 0:512], eg[:, sl], AF.Square, (egk + "h%d" % half,), ("stat%d" % c_ss2,),
                    accum_out=stat[:, c_ss2 + half:c_ss2 + half + 1])

        def ple_tail(tb):
            eg, egk = gvt[tb % 2], "gvt%d" % (tb % 2)
            c_ss2 = ple_c[tb]
            c_s1 = statcol(1)
            tt("dve", stat[:, c_s1:c_s1 + 1], stat[:, c_ss2:c_ss2 + 1], stat[:, c_ss2 + 1:c_ss2 + 2], ALU.add,
               ("stat%d" % c_ss2,), ("stat%d" % c_s1,))
            c_r1 = rsqrt_cols(c_s1, 1, 1.0 / D, 4.0 * EPS)
            stt(eg[:, :], eg[:, :], stat[:, c_r1:c_r1 + 1], gple_b[:, :], ALU.mult, ALU.mult,
                (egk + "h0", egk + "h1", "stat%d" % c_r1, "gple_b"), (egk + "h0", egk + "h1"))
            tt("dve", eg[:, :], eg[:, :], X3[:, tb, :], ALU.add, (egk + "h0", egk + "h1") + XK(Xk, tb),
               (egk + "h0", egk + "h1", egk))
            dma(y_d[r0 + tb * 128:r0 + (tb + 1) * 128, :], eg[:, :], (egk, egk + "h0", egk + "h1"), ("y",))

        ple_head(0)
        for tb in range(1, NB):
            ple_head(tb)
            ple_tail(tb - 1)
        ple_tail(NB - 1)
        release_block(2)

    for g in range(total_tiles):
        tile_body(g)

    P.emit(es)
    es.close()
    return nc


def _host_inputs(x, p, w_in, gmlp_ln_g, gmlp_ln_b, gmlp_w_s, gmlp_b_s, hgrn_lb_logits, hgrn_norm_g,
                 w_branch_a, w_branch_b, w_out, g_pre, g_post, w_ple, w_ple_gate, b_ple_gate, g_ple):
    f32 = np.float32

    def blocks(w):
        out = []
        for c0 in range(0, w.shape[1], 512):
            out.append(np.ascontiguousarray(w[:, c0:c0 + 512].reshape(8, 128, 512).transpose(1, 0, 2)).reshape(128, 4096))
        return out

    wb = blocks(np.asarray(w_in[0], f32)) + blocks(np.asarray(w_branch_a[0], f32)) + \
        blocks(np.asarray(w_branch_b[0], f32)) + blocks(np.asarray(w_out[0], f32)) + \
        blocks(np.asarray(w_ple_gate[0], f32))
    wblk = np.stack(wb, 0).astype(f32)

    def colF(v):
        return np.ascontiguousarray(np.asarray(v, f32).reshape(8, 128).T)

    tt_ = np.arange(128)
    triu = (tt_[:, None] <= tt_[None, :]).astype(f32)
    mbd = (triu * ((tt_[:, None] // CH) == (tt_[None, :] // CH))).astype(ml_dtypes.bfloat16)
    rm = np.ones((128, TT), f32)
    rm[:, ::CH] = 0.0
    common = {
        "wblk": wblk,
        "wp": np.ascontiguousarray(np.asarray(w_ple[0], f32).reshape(2, 128, D).transpose(1, 0, 2)).reshape(128, 2 * D),
        "gpreF": colF(g_pre[0]),
        "hgF": colF(np.asarray(hgrn_norm_g[0]).reshape(-1)),
        "lngF": colF(gmlp_ln_g[0]),
        "lnb": np.asarray(gmlp_ln_b[0], f32).reshape(1, D),
        "lng_row": np.asarray(gmlp_ln_g[0], f32).reshape(1, D),
        "bs": np.asarray(gmlp_b_s[0], f32).reshape(1, 512),
        "wsT": np.ascontiguousarray(np.asarray(gmlp_w_s[0], f32).transpose(2, 0, 1)).reshape(128, 512),
        "lbl": np.ascontiguousarray(np.asarray(hgrn_lb_logits, f32).reshape(2, 8, 128).transpose(2, 0, 1)).reshape(128, 16),
        "gpost_b": np.ascontiguousarray(np.broadcast_to(np.asarray(g_post[0], f32), (128, D))),
        "gple_b": np.ascontiguousarray(np.broadcast_to(np.asarray(g_ple[0], f32), (128, D))),
        "bg": np.asarray(b_ple_gate[0], f32).reshape(1, D),
        "ident": np.eye(128, dtype=f32).astype(ml_dtypes.bfloat16),
        "triu": triu,
        "mbd": mbd,
        "rm": rm,
    }
    return common


_CACHE = {}


def run(inputs, nseq, T, ncores, x_full, p_full):
    common = _host_inputs(**inputs)
    key = (nseq, T)
    nc = build(nseq, T)
    in_maps = []
    for c in range(ncores):
        m = dict(common)
        m["x"] = np.ascontiguousarray(x_full[c * nseq:(c + 1) * nseq].reshape(nseq * T, D))
        m["p"] = np.ascontiguousarray(p_full[c * nseq:(c + 1) * nseq].reshape(nseq * T, PLE))
        in_maps.append(m)
    res = run_bass_kernel_spmd(nc, in_maps, core_ids=list(range(ncores)))
    outs = [np.asarray(r["y"]).reshape(nseq, T, D) for r in res.results]
    return np.concatenate(outs, 0).astype(np.float32)


def kernel(**inputs):
    x = np.asarray(inputs["x"], np.float32)
    p = np.asarray(inputs["p"], np.float32)[0]
    B, T, _ = x.shape
    nseq = B // NCORES
    return run(inputs, nseq, T, NCORES, x, p)
```

```python
import math
from contextlib import ExitStack

import numpy as np
import ml_dtypes

import concourse.bass as bass
import concourse.mybir as mybir
from concourse.bass_utils import run_bass_kernel_spmd

F32 = mybir.dt.float32
BF16 = mybir.dt.bfloat16
AF = mybir.ActivationFunctionType
ALU = mybir.AluOpType
AX = mybir.AxisListType

D = 1024
PLE = 256
EPS = 1e-6
NCORES = 8
TT = 512
NB = TT // 128
CH = 64
NBLK = 26
NSLOT = 3
LAZY_CAST = True

G_U, G_V, G_ZA, G_Q, G_F, G_INP, G_ZB, G_AA, G_AB = range(9)
BLK_WA, BLK_WB, BLK_WO, BLK_WG = 18, 20, 22, 24


class _Op:
    __slots__ = ("eng", "fn", "deps", "is_dma", "sig", "count", "sem", "val", "gid")

    def __init__(self, eng, fn, is_dma):
        self.eng = eng
        self.fn = fn
        self.deps = set()
        self.is_dma = is_dma
        self.sig = False
        self.count = 0
        self.sem = None
        self.val = 0
        self.gid = 0


class Prog:
    ENGS = ("pe", "act", "dve", "pool", "sp")

    def __init__(self, nc):
        self.nc = nc
        self.ops = []
        self.last_w = {}
        self.readers = {}

    def add(self, eng, fn, reads=(), writes=(), dma=False):
        op = _Op(eng, fn, dma)
        op.gid = len(self.ops)
        deps = set()
        for k in reads:
            w = self.last_w.get(k)
            if w is not None:
                deps.add(w)
        for k in writes:
            w = self.last_w.get(k)
            if w is not None:
                deps.add(w)
            deps |= self.readers.get(k, set())
        deps.discard(op)
        op.deps = deps
        for k in reads:
            self.readers.setdefault(k, set()).add(op)
        for k in writes:
            self.last_w[k] = op
            self.readers[k] = set()
        self.ops.append(op)
        return op

    def emit(self, es, n_dma_sems=24):
        nc = self.nc
        engobj = {"pe": nc.tensor, "act": nc.scalar, "dve": nc.vector, "pool": nc.gpsimd, "sp": nc.sync}
        for op in self.ops:
            best = {}
            keep = set()
            for d in op.deps:
                if d.is_dma:
                    keep.add(d)
                    continue
                if d.eng == "pe" and op.eng == "pe":
                    continue
                if d.eng not in best or best[d.eng].gid < d.gid:
                    best[d.eng] = d
            for d in best.values():
                d.sig = True
                keep.add(d)
            op.deps = keep
        sems = {e: es.enter_context(nc.semaphore("sem_" + e)) for e in ("pe", "act", "dve", "pool")}
        dsems = [es.enter_context(nc.semaphore("dsem%d" % i)) for i in range(n_dma_sems)]
        dcount = [0] * n_dma_sems
        dprev = [None] * n_dma_sems
        cnt = {e: 0 for e in sems}
        ndma = 0
        for op in self.ops:
            if op.is_dma:
                i = ndma % n_dma_sems
                ndma += 1
                op.sem = dsems[i]
                dcount[i] += 16
                op.val = dcount[i]
                if dprev[i] is not None:
                    op.deps.add(dprev[i])
                dprev[i] = op
            elif op.sig:
                cnt[op.eng] += 1
                op.count = cnt[op.eng]
                op.sem = sems[op.eng]
                op.val = op.count
        per_eng = {e: [o for o in self.ops if o.eng == e] for e in self.ENGS}
        block = es.enter_context(nc.Block())

        def run(e):
            eo = engobj[e]
            waited = {}
            for op in per_eng[e]:
                need = {}
                for d in op.deps:
                    if (not d.is_dma) and d.eng == "pe" and e == "pe":
                        continue
                    key = id(d.sem)
                    if waited.get(key, 0) >= d.val:
                        continue
                    if key not in need or need[key][1] < d.val:
                        need[key] = (d.sem, d.val)
                for key, (s, v) in need.items():
                    eo.wait_ge(s, v)
                    waited[key] = v
                inst = op.fn()
                if op.is_dma:
                    inst.then_inc(op.sem, 16)
                elif op.sig:
                    inst.then_inc(op.sem, 1)
            return eo

        @block.tensor
        def _(eng):
            run("pe")

        @block.scalar
        def _(eng):
            run("act")

        @block.vector
        def _(eng):
            run("dve")

        @block.gpsimd
        def _(eng):
            run("pool")

        @block.sync
        def _(eng):
            eo = run("sp")
            for i in range(n_dma_sems):
                if dcount[i]:
                    eo.wait_ge(dsems[i], dcount[i])


def build(nseq, T, debug=False):
    assert T % TT == 0
    ntok = nseq * T
    ntile_seq = T // TT
    nc = bass.Bass("TRN2", target_bir_lowering=False)
    es = ExitStack()
    P = Prog(nc)

    def din(name, shape, dt=F32):
        return nc.dram_tensor(name, list(shape), dt, kind="ExternalInput").ap()

    x_d = din("x", [ntok, D])
    p_d = din("p", [ntok, PLE])
    wblk_d = din("wblk", [NBLK, 128, 8 * 512])
    wp_d = din("wp", [128, 2 * D])
    gpreF_d = din("gpreF", [128, 8])
    hgF_d = din("hgF", [128, 8])
    lngF_d = din("lngF", [128, 8])
    lnb_d = din("lnb", [1, D])
    lngr_d = din("lng_row", [1, D])
    bs_d = din("bs", [1, 512])
    wsT_d = din("wsT", [128, 512])
    lbl_d = din("lbl", [128, 16])
    gpost_d = din("gpost_b", [128, D])
    gple_d = din("gple_b", [128, D])
    bg_d = din("bg", [1, D])
    ident_d = din("ident", [128, 128], BF16)
    triu_d = din("triu", [128, 128])
    mbd_d = din("mbd", [128, 128], BF16)
    rm_d = din("rm", [128, TT])
    y_d = nc.dram_tensor("y", [ntok, D], F32, kind="ExternalOutput").ap()
    wsc_d = nc.dram_tensor("wsc", [NBLK, 128, 8 * 512], BF16, kind="Internal").ap()
    dbg_d = {}

    def sb(name, shape, dt=F32):
        return es.enter_context(nc.sbuf_tensor("s_" + name, list(shape), dt))

    def ps(name, shape, dt=F32):
        return es.enter_context(nc.psum_tensor("ps_" + name, list(shape), dt))

    ident = sb("ident", [128, 128], BF16)
    mbd = sb("mbd", [128, 128], BF16)
    rm = sb("rm", [128, TT])
    wsT = sb("wsT", [128, 4, 128], BF16)
    L2p = sb("L2p", [2, D])
    gpost_b = sb("gpost_b", [128, D])
    gple_b = sb("gple_b", [128, D])
    wp = sb("wp", [128, 2, D], BF16)
    small = sb("small", [128, 128])
    ones_row = sb("ones_row", [1, 128], BF16)
    bg_row = sb("bg_row", [1, D], BF16)
    neghalf = sb("neghalf", [128, 32])
    C_GPRE, C_HG, C_LNG, C_LSC, C_LBI, C_LHO = 0, 8, 16, 24, 32, 40
    gpreF = small[:, C_GPRE:C_GPRE + 8]
    hgF = small[:, C_HG:C_HG + 8]
    lngF = small[:, C_LNG:C_LNG + 8]
    lscF = small[:, C_LSC:C_LSC + 8]
    lbiF = small[:, C_LBI:C_LBI + 8]
    lhoF = small[:, C_LHO:C_LHO + 8]
    tmpc = small[:, 48:80]

    wslot = [sb("wslot%d" % i, [128, 8, 512], BF16) for i in range(NSLOT)]
    FB = [sb("FB%d" % i, [128, 8 * TT], BF16) for i in range(8)]
    F32B = [sb("F32B%d" % i, [128, 8 * TT]) for i in range(1)]
    QF = sb("QF", [128, 8, NB, 3, CH], BF16)
    xt = sb("xt", [128, NB, D])
    hn = [sb("hn%d" % i, [128, D], BF16) for i in range(2)]
    pT = sb("pT", [128, 2, TT], BF16)
    hb = [sb("hb%d" % i, [128, D], BF16) for i in range(2)]
    junk = sb("junk", [128, D], BF16)
    gvt = [sb("gvt%d" % i, [128, D]) for i in range(2)]
    ft = [sb("ft%d" % i, [128, TT]) for i in range(6)]
    fe = [sb("fe%d" % i, [128, TT], BF16) for i in range(4)]
    Sst = sb("Sst", [128, D])
    Sbf = [sb("Sbf%d" % i, [128, D], BF16) for i in range(3)]
    Am = [sb("Am%d" % i, [128, 8, 128], BF16) for i in range(2)]
    el = sb("el", [128, 8, 8])
    stat = sb("stat", [128, 256])

    PB = [ps("PB%d" % i, [128, 512]) for i in range(6)]
    TB = [ps("TB%d" % i, [128, 1024], BF16) for i in range(2)]

    st = {"pb": 0, "tb": 0, "dve_pool": 0, "stat": 0, "gv": 0, "ft": 0}

    def next_pb(n=6):
        i = st["pb"] % n
        st["pb"] += 1
        return i

    def next_tb():
        i = st["tb"] % 2
        st["tb"] += 1
        return i

    def statcol(n):
        c = st["stat"]
        if c + n > 256:
            c = 0
        st["stat"] = c + n
        return c

    def dma(out, in_, reads, writes):
        return P.add("sp", lambda: nc.sync.dma_start(out=out, in_=in_), reads, writes, dma=True)

    def act(out, in_, func, reads, writes, scale=None, bias=None, accum_out=None):
        kw = {}
        if scale is not None:
            kw["scale"] = scale
        if bias is not None:
            kw["bias"] = bias
        if accum_out is not None:
            kw["accum_out"] = accum_out
        return P.add("act", lambda: nc.scalar.activation(out=out, in_=in_, func=func, **kw), reads, writes)

    def veng(e):
        return nc.vector if e == "dve" else nc.gpsimd

    def tt(e, out, in0, in1, op, reads, writes):
        return P.add(e, lambda: veng(e).tensor_tensor(out=out, in0=in0, in1=in1, op=op), reads, writes)

    def ts(e, out, in0, s1, s2, op0, op1, reads, writes):
        if op1 is None:
            return P.add(e, lambda: veng(e).tensor_scalar(out=out, in0=in0, scalar1=s1, scalar2=None, op0=op0),
                         reads, writes)
        return P.add(e, lambda: veng(e).tensor_scalar(out=out, in0=in0, scalar1=s1, scalar2=s2, op0=op0, op1=op1),
                     reads, writes)

    def stt(out, in0, scalar, in1, op0, op1, reads, writes):
        return P.add("dve", lambda: nc.vector.scalar_tensor_tensor(out=out, in0=in0, scalar=scalar, in1=in1,
                                                                   op0=op0, op1=op1), reads, writes)

    def cp(e, out, in_, reads, writes):
        if e == "act":
            return P.add("act", lambda: nc.scalar.copy(out=out, in_=in_), reads, writes)
        return P.add(e, lambda: veng(e).tensor_copy(out=out, in_=in_), reads, writes)

    def mm(out, lhsT, rhs, start, stop, reads, writes):
        return P.add("pe", lambda: nc.tensor.matmul(out, lhsT, rhs, start=start, stop=stop), reads, writes)

    def tr(out, in_, reads, writes):
        return P.add("pe", lambda: nc.tensor.transpose(out, in_, ident[:, :]), reads + ("ident",), writes)

    def rsqrt_cols(c_in, n, scale, eps, e="dve"):
        c_ms = statcol(n)
        c_out = statcol(n)
        ts(e, stat[:, c_ms:c_ms + n], stat[:, c_in:c_in + n], scale, eps, ALU.mult, ALU.add,
           ("stat%d" % c_in,), ("stat%d" % c_ms,))
        tt("pool", stat[:, c_out:c_out + n], stat[:, c_ms:c_ms + n], neghalf[:, 0:n], ALU.pow,
           ("stat%d" % c_ms, "neghalf"), ("stat%d" % c_out,))
        return c_out

    stgA = F32B[0][:, :]
    stgB = xt[:, :, :].rearrange("p a b -> p (a b)")
    def XKp(k):
        return tuple("%s.%d" % (k, i) for i in range(NB))

    EAGER = (BLK_WO, BLK_WO + 1, BLK_WG, BLK_WG + 1) if LAZY_CAST else tuple(range(NBLK))
    for b in EAGER:
        stg, sk = ((stgA, "F32B0"), (stgB, "xt"))[b % 2]
        cb, ck = ((FB[4], "FB4"), (FB[5], "FB5"))[b % 2]
        dma(stg, wblk_d[b], XKp(sk), XKp(sk))
        fold = gpreF if b < 18 else (hgF if b in (BLK_WB, BLK_WB + 1) else None)
        if fold is None:
            cp("dve", cb[:, 0:2048], stg[:, 0:2048], XKp(sk), (ck,))
            cp("act", cb[:, 2048:4096], stg[:, 2048:4096], XKp(sk), (ck,))
        else:
            for kc in range(8):
                if kc % 2 == 0:
                    ts("dve", cb[:, kc * 512:(kc + 1) * 512], stg[:, kc * 512:(kc + 1) * 512], fold[:, kc:kc + 1],
                       None, ALU.mult, None, XKp(sk) + ("small",), (ck,))
                else:
                    act(cb[:, kc * 512:(kc + 1) * 512], stg[:, kc * 512:(kc + 1) * 512], AF.Copy,
                        XKp(sk) + ("small",), (ck,), scale=fold[:, kc:kc + 1])
        dma(wsc_d[b], cb[:, :], (ck,), ("wsc%d" % b,))

    dma(ident[:, :], ident_d, (), ("ident",))
    dma(mbd[:, :], mbd_d, (), ("mbd",))
    dma(rm[:, :], rm_d, (), ("rm",))
    dma(gpost_b[:, :], gpost_d, (), ("gpost_b",))
    dma(gple_b[:, :], gple_d, (), ("gple_b",))
    dma(small[:, C_GPRE:C_GPRE + 8], gpreF_d, (), ("small",))
    dma(small[:, C_HG:C_HG + 8], hgF_d, (), ("small",))
    dma(small[:, C_LNG:C_LNG + 8], lngF_d, (), ("small",))
    P.add("dve", lambda: nc.vector.memset(neghalf[:, :], -0.5), (), ("neghalf",))
    P.add("dve", lambda: nc.vector.memset(QF[:, :, :, :, :], 0.0), (), ("QF",))
    P.add("pool", lambda: nc.gpsimd.memset(ones_row[:, :], 1.0), (), ("ones_row",))
    lbl = ft[0]
    dma(lbl[:, 0:16], lbl_d, (), ("ft0",))
    tt("dve", tmpc[:, 0:8], lbl[:, 0:8], lbl[:, 8:16], ALU.subtract, ("ft0",), ("tmpc",))
    act(tmpc[:, 8:16], tmpc[:, 0:8], AF.Sigmoid, ("tmpc",), ("tmpc",), scale=-1.0)
    ts("dve", small[:, C_LSC:C_LSC + 8], tmpc[:, 8:16], -0.5, None, ALU.mult, None, ("tmpc",), ("small",))
    ts("dve", small[:, C_LBI:C_LBI + 8], tmpc[:, 8:16], -0.5, 1.0, ALU.mult, ALU.add, ("tmpc",), ("small",))
    act(small[:, C_LHO:C_LHO + 8], tmpc[:, 8:16], AF.Ln, ("tmpc",), ("small",), scale=0.5)
    bgf = ft[1]
    dma(bgf[0:1, 0:512], bg_d[:, 0:512], (), ("ft1",))
    cp("dve", bg_row[0:1, 0:512], bgf[0:1, 0:512], ("ft1",), ("bg_row",))
    dma(bgf[0:1, 0:512], bg_d[:, 512:1024], ("ft1",), ("ft1",))
    cp("dve", bg_row[0:1, 512:1024], bgf[0:1, 0:512], ("ft1",), ("bg_row",))
    wpf = gvt[0]
    for h in range(2):
        dma(wpf[:, :], wp_d[:, h * D:(h + 1) * D], ("gvt0",), ("gvt0",))
        cp("dve", wp[:, h, :], wpf[:, :], ("gvt0",), ("wp",))
    wsf = ft[2]
    triu = ft[3]
    dma(wsf[:, :], wsT_d, (), ("ft2",))
    dma(triu[:, 0:128], triu_d, (), ("ft3",))
    tt("dve", wsT[:, :, :], wsf[:, :].rearrange("p (g t) -> p g t", g=4),
       triu[:, 0:128].unsqueeze(1).broadcast_to([128, 4, 128]), ALU.mult, ("ft2", "ft3"), ("wsT",))
    onescol = sb("onescol", [128, 1], BF16)
    P.add("dve", lambda: nc.vector.memset(onescol[:, :], 1.0), (), ("onescol",))
    L2 = gvt[1][0:2, :]
    R2 = ft[5][0:2, :]
    G2 = gvt[0][0:2, :]
    P.add("dve", lambda: nc.vector.memset(L2[:, :], 1.0), (), ("gvt1",))
    dma(L2[0:1, :], lnb_d, ("gvt1",), ("gvt1",))
    dma(G2[0:1, :], lngr_d, (), ("gvt0",))
    dma(G2[1:2, :], lngr_d, (), ("gvt0",))
    P.add("dve", lambda: nc.vector.reciprocal(out=G2[:, :], in_=G2[:, :]), ("gvt0",), ("gvt0",))
    tt("dve", L2p[:, :], L2[:, :], G2[:, :], ALU.mult, ("gvt1", "gvt0"), ("L2p",))
    mm(PB[0][0:1, :], onescol[:, 0:1], wsT[:, :, :].rearrange("p g t -> p (g t)"), True, True,
       ("onescol", "wsT"), ("PB0",))
    cp("dve", R2[0:1, :], PB[0][0:1, :], ("PB0",), ("ft5",))
    dma(R2[1:2, :], bs_d, ("ft5",), ("ft5",))

    ring = {"n": 0}
    lazy = set(range(NBLK)) - set(EAGER)

    def issue_block(blk):
        s = ring["n"] % NSLOT
        ring["n"] += 1
        sk_ = "wslot%d" % s
        if blk in lazy:
            lazy.discard(blk)
            fold = gpreF if blk < 18 else (hgF if blk in (BLK_WB, BLK_WB + 1) else None)
            for hf in range(2):
                stg = xt[:, 2 * hf:2 * hf + 2, :].rearrange("p a b -> p (a b)")
                lk = "xtL%d" % hf
                dma(stg, wblk_d[blk][:, hf * 2048:(hf + 1) * 2048], (),
                    (lk, "xt.%d" % (2 * hf), "xt.%d" % (2 * hf + 1)))
                if fold is None:
                    cp("pool", wslot[s][:, 4 * hf:4 * hf + 4, :].rearrange("p a b -> p (a b)"), stg, (lk,), (sk_,))
                else:
                    for k4 in range(4):
                        kc = 4 * hf + k4
                        ts("pool", wslot[s][:, kc, :], stg[:, k4 * 512:(k4 + 1) * 512], fold[:, kc:kc + 1], 0.0,
                           ALU.mult, ALU.add, (lk, "small"), (sk_,))
            dma(wsc_d[blk], wslot[s][:, :, :].rearrange("p a b -> p (a b)"), (sk_,), ("wsc%d" % blk,))
        else:
            dma(wslot[s][:, :, :].rearrange("p a b -> p (a b)"), wsc_d[blk], ("wsc%d" % blk, sk_), (sk_,))
        return s

    total_tiles = nseq * ntile_seq

    def gb(*gs):
        out = []
        for g in gs:
            out += [2 * g, 2 * g + 1]
        return out

    stream = []
    for g_ in range(total_tiles):
        if g_ == 0:
            stream += gb(G_U, G_ZA, G_AA)
        stream += (gb(G_V) if g_ == 0 else []) + gb(G_Q, G_F) + [BLK_WA, BLK_WA + 1] + gb(G_ZB, G_AB, G_INP)
        if g_ + 1 < total_tiles:
            stream += gb(G_ZA)
        stream += [BLK_WB, BLK_WB + 1, BLK_WO, BLK_WO + 1]
        if g_ + 1 < total_tiles:
            stream += [2 * G_U] + gb(G_AA)
        stream += [BLK_WG, BLK_WG + 1]
    sp_ = {"issued": 0, "used": 0}
    slot_of = {}

    def prefetch():
        while sp_["issued"] < len(stream) and sp_["issued"] - sp_["used"] < NSLOT:
            i = sp_["issued"]
            slot_of[i] = issue_block(stream[i])
            sp_["issued"] += 1

    def take_block(expect, off=0):
        i = sp_["used"] + off
        assert stream[i] == expect, (stream[i], expect)
        assert i < sp_["issued"]
        return i, slot_of[i]

    def release_block(n=1):
        sp_["used"] += n
        prefetch()

    prefetch()

    kchunk = {"k": 0}
    XBUF = [(xt[:, :, :].rearrange("p a b -> p (a b)"), "xt"), (F32B[0][:, :], "F32B0")]

    def XK(k, tb=None):
        if tb is None:
            return tuple("%s.%d" % (k, i) for i in range(NB))
        return ("%s.%d" % (k, tb),)

    HBUF = [(FB[0], "FB0"), (FB[7], "FB7")]
    s1 = {}

    def F3(i):
        return FB[i][:, :].rearrange("p (a b) -> p a b", a=8)

    def T3(i):
        return FB[i][:, :].rearrange("p (a b) -> p a b", a=NB)

    def stage1_load(g):
        X, Xk = XBUF[g % 2]
        r0 = g * TT
        wk = XK(Xk) + (("xtL0", "xtL1") if Xk == "xt" else ())
        dma(X.rearrange("p (a b) -> p a b", a=NB), x_d[r0:r0 + TT, :].rearrange("(b p) d -> p b d", p=128),
            XK(Xk), wk)

    def stage1_stats(g):
        X, Xk = XBUF[g % 2]
        X3 = X.rearrange("p (a b) -> p a b", a=NB)
        c_ss = statcol(NB)
        for tb in range(NB):
            act(junk[:, :], X3[:, tb, :], AF.Square, XK(Xk, tb), ("stat%d" % c_ss,),
                accum_out=stat[:, c_ss + tb:c_ss + tb + 1])
        s1[g] = rsqrt_cols(c_ss, NB, 1.0 / D, EPS)

    def stage1_scale(g, tb):
        X, Xk = XBUF[g % 2]
        X3 = X.rearrange("p (a b) -> p a b", a=NB)
        c_r = s1[g]
        h, hk = hn[tb % 2], "hn%d" % (tb % 2)
        ts("dve", h[:, :], X3[:, tb, :], stat[:, c_r + tb:c_r + tb + 1], None, ALU.mult, None,
           XK(Xk, tb) + ("stat%d" % c_r,), (hk,))

    def stage1_tr(g, tb):
        hTn, hTnk = HBUF[g % 2]
        hTn3 = hTn[:, :].rearrange("p (a b) -> p a b", a=8)
        h, hk = hn[tb % 2], "hn%d" % (tb % 2)
        tbi = next_tb()
        for kc in range(8):
            tr(TB[tbi][:, kc * 128:(kc + 1) * 128], h[:, kc * 128:(kc + 1) * 128], (hk,), ("TB%d" % tbi,))
        cp("act", hTn3[:, :, tb * 128:(tb + 1) * 128], TB[tbi][:, :].rearrange("p (a b) -> p a b", a=8),
           ("TB%d" % tbi,), (hTnk,))

    def tile_body(g):
        r0 = g * TT
        X, Xk = XBUF[g % 2]
        S, Sk = XBUF[(g + 1) % 2]
        X3 = X.rearrange("p (a b) -> p a b", a=NB)
        hT, hTk = HBUF[g % 2]
        hT3 = hT[:, :].rearrange("p (a b) -> p a b", a=8)
        has_next = g + 1 < total_tiles

        if g == 0:
            stage1_load(0)
            stage1_stats(0)
            for tb in range(NB):
                stage1_scale(0, tb)
                stage1_tr(0, tb)
        if g % ntile_seq == 0:
            P.add("pool", lambda: nc.gpsimd.memset(Sst[:, :], 0.0), (), ("Sst0", "Sst1"))
            k0 = kchunk["k"] % 3
            P.add("pool", lambda: nc.gpsimd.memset(Sbf[k0][:, :], 0.0), (), ("Sbf%d" % k0,))

        pt3 = gvt[1][:, :].rearrange("p (a b) -> p a b", a=NB)
        dma(pt3, p_d[r0:r0 + TT, :].rearrange("(b p) d -> p b d", p=128), ("gvt1",), ("gvt1",))
        ptb = hb[0][:, :].rearrange("p (a b) -> p a b", a=NB)
        cp("pool", ptb[:, :, :], pt3, ("gvt1",), ("hb0",))

        def proj_F(gg, evac, src=None, wsrc=None):
            h3_, hk_ = src if src is not None else (hT3, hTk)
            for half in range(2):
                if wsrc is not None and half in wsrc:
                    w3, wk_ = wsrc[half]
                    ring_ = False
                else:
                    _, s = take_block(2 * gg + half)
                    w3, wk_ = wslot[s], "wslot%d" % s
                    ring_ = True
                for cb_ in range(4):
                    b = next_pb()
                    for kc in range(8):
                        mm(PB[b][:, :], w3[:, kc, cb_ * 128:(cb_ + 1) * 128], h3_[:, kc, :], kc == 0, kc == 7,
                           (wk_, hk_), ("PB%d" % b,))
                    evac(half * 4 + cb_, PB[b], "PB%d" % b)
                if ring_:
                    release_block()

        def proj_T(gg, evac, wsrc=None):
            for half in range(2):
                if wsrc is None:
                    _, s = take_block(2 * gg + half)
                    w3, wk_ = wslot[s], "wslot%d" % s
                else:
                    w3, wk_ = wsrc[half]
                for tb in range(NB):
                    b = next_pb()
                    for kc in range(8):
                        mm(PB[b][:, :], hT3[:, kc, tb * 128:(tb + 1) * 128], w3[:, kc, :], kc == 0, kc == 7,
                           (wk_, hTk), ("PB%d" % b,))
                    evac(tb, half, PB[b], "PB%d" % b)
                if wsrc is None:
                    release_block()

        guT, szaT, taaT = F3(5), F3(6), F3(4)

        def proj_uza(which, src=None):
            if which == 0:
                wsrc_ = None
                if src is not None:
                    wsrc_ = {1: (FB[4][:, :].rearrange("p (a b) -> p a b", a=8), "FB4")}
                proj_F(G_U, lambda j, bk, k: act(guT[:, j, :], bk[:, :], AF.Gelu, (k,), ("FB5",)), src, wsrc_)
            elif which == 1:
                proj_F(G_ZA, lambda j, bk, k: act(szaT[:, j, :], bk[:, :], AF.Silu, (k,), ("FB6",)), src)
            else:
                proj_F(G_AA, lambda j, bk, k: act(taaT[:, j, :], bk[:, :], AF.Tanh, (k,), ("FB4",), scale=0.5), src)

        def make_gz():
            for j in range(8):
                tt("pool", szaT[:, j, :], szaT[:, j, :], guT[:, j, :], ALU.mult, ("FB6", "FB5"), ("FB6",))

        if g == 0:
            for w_ in range(3):
                proj_uza(w_)
            make_gz()

        vn = T3(2)
        gv = S.rearrange("p (a b) -> p a b", a=NB)
        vsrc = None
        if g > 0:
            hprev, hprevk = HBUF[(g - 1) % 2]
            vsrc = [(FB[2][:, :].rearrange("p (a b) -> p a b", a=8), "FB2"),
                    (hprev[:, :].rearrange("p (a b) -> p a b", a=8), hprevk)]
        proj_T(G_V, lambda tb, half, bk, k: act(gv[:, tb, half * 512:(half + 1) * 512], bk[:, :], AF.Gelu, (k,),
                                               XK(Sk, tb)), vsrc)
        for tb in range(NB):
            c_bs = statcol(12)
            for hh in range(2):
                P.add("dve", (lambda tb=tb, hh=hh, c=c_bs: nc.vector.bn_stats(
                    out=stat[:, c + 6 * hh:c + 6 * hh + 6], in_=gv[:, tb, hh * 512:(hh + 1) * 512])),
                    XK(Sk, tb), ("stat%d" % c_bs,))
            c_mv = statcol(2)
            P.add("dve", (lambda c=c_bs, m=c_mv: nc.vector.bn_aggr(out=stat[:, m:m + 2], in_=stat[:, c:c + 12])),
                  ("stat%d" % c_bs,), ("stat%d" % c_mv,))
            c_var = statcol(1)
            cp("dve", stat[:, c_var:c_var + 1], stat[:, c_mv + 1:c_mv + 2], ("stat%d" % c_mv,), ("stat%d" % c_var,))
            c_rs = rsqrt_cols(c_var, 1, 1.0, EPS)
            c_nm = statcol(1)
            stt(stat[:, c_nm:c_nm + 1], stat[:, c_mv:c_mv + 1], -1.0, stat[:, c_rs:c_rs + 1], ALU.mult, ALU.mult,
                ("stat%d" % c_mv, "stat%d" % c_rs), ("stat%d" % c_nm,))
            ts("dve", vn[:, tb, :], gv[:, tb, :], stat[:, c_rs:c_rs + 1], stat[:, c_nm:c_nm + 1], ALU.mult, ALU.add,
               XK(Sk, tb) + ("stat%d" % c_rs, "stat%d" % c_nm), ("FB2",))

        sqF = F3(1)
        tfall = S.rearrange("p (a b) -> p a b", a=8)
        proj_F(G_Q, lambda j, bk, k: act(sqF[:, j, :], bk[:, :], AF.Silu, (k,), ("FB1",)))
        proj_F(G_F, lambda j, bk, k: act(tfall[:, j, :], bk[:, :], AF.Tanh, (k,), XK(Sk, j // 2), scale=-0.5))

        KF, KFk = F3(3), "FB3"
        lnsc = math.log(128.0 ** -0.5)

        def f_ln(j):
            lf, lfk = ft[j % 3], "ft%d" % (j % 3)
            act(lf[:, :], tfall[:, j, :], AF.Ln, XK(Sk, j // 2) + ("small",), (lfk,), scale=lscF[:, j:j + 1],
                bias=lbiF[:, j:j + 1])

        def f_scan(j):
            lf, lfk = ft[j % 3], "ft%d" % (j % 3)
            cm, cmk = ft[3 + (j % 2)], "ft%d" % (3 + j % 2)
            P.add("dve", (lambda cm=cm, lf=lf: nc.vector.tensor_tensor_scan(
                out=cm[:, :], data0=rm[:, :], data1=lf[:, :], initial=0.0, op0=ALU.mult, op1=ALU.add)),
                (lfk, "rm"), (cmk,))

        def f_rest(j):
            cm, cmk = ft[3 + (j % 2)], "ft%d" % (3 + j % 2)
            E, Ek = fe[j % 2], "fe%d" % (j % 2)
            Ei, Eik = fe[2 + j % 2], "fe%d" % (2 + j % 2)
            act(E[:, :], cm[:, :], AF.Exp, (cmk,), (Ek,), bias=float(lnsc))
            act(Ei[:, :], cm[:, :], AF.Exp, (cmk, "small"), (Eik,), scale=-1.0, bias=lhoF[:, j:j + 1])
            act(el[:, j, :], cm[:, :].rearrange("p (c i) -> p c i", i=CH)[:, :, CH - 1], AF.Exp, (cmk,), ("el",))
            tt("pool", QF[:, j, :, 0:3:2, :], sqF[:, j, :].rearrange("p (a b c) -> p a b c", a=NB, b=2),
               E[:, :].rearrange("p (a b c) -> p a b c", a=NB, b=2), ALU.mult, ("FB1", Ek), ("QF",))
            stt(KF[:, j, :], tfall[:, j, :], 1.0, Ei[:, :], ALU.add, ALU.mult, XK(Sk, j // 2) + (Eik,), (KFk,))

        tbi = next_tb()
        for tb in range(NB):
            for kc in range(2):
                tr(TB[tbi][:, (tb * 2 + kc) * 128:(tb * 2 + kc + 1) * 128], ptb[:, tb, kc * 128:(kc + 1) * 128],
                   ("hb0",), ("TB%d" % tbi,))
        cp("dve", pT[:, :, :].rearrange("p k (b t) -> p b k t", b=NB),
           TB[tbi][:, :].rearrange("p (b k t) -> p b k t", b=NB, k=2), ("TB%d" % tbi,), ("pT",))

        yapreT = guT

        def spatial(j):
            gi_ = j // 2
            b = 3 + (j % 2)
            for tb in range(NB):
                mm(PB[b][:, tb * 128:(tb + 1) * 128], vn[:, tb, j * 128:(j + 1) * 128], wsT[:, gi_, :], True, False,
                   ("FB2", "wsT"), ("PB%d" % b,))
                mm(PB[b][:, tb * 128:(tb + 1) * 128], L2p[0:2, j * 128:(j + 1) * 128],
                   ft[5][0:2, gi_ * 128:(gi_ + 1) * 128], False, True, ("L2p", "ft5"), ("PB%d" % b,))
            stt(yapreT[:, j, :], PB[b][:, :], lngF[:, j:j + 1], szaT[:, j, :], ALU.mult, ALU.mult,
                ("PB%d" % b, "small", "FB6"), ("FB5",))

        f_ln(0)
        f_ln(1)
        f_scan(0)
        for j in range(8):
            if j + 2 < 8:
                f_ln(j + 2)
            if j + 1 < 8:
                f_scan(j + 1)
            f_rest(j)
            spatial(j)

        if has_next:
            stage1_load(g + 1)

        maT = taaT
        for half in range(2):
            _, s = take_block(BLK_WA + half)
            for cb_ in range(4):
                jo = half * 4 + cb_
                b = next_pb()
                for kc in range(8):
                    mm(PB[b][:, :], wslot[s][:, kc, cb_ * 128:(cb_ + 1) * 128], yapreT[:, kc, :], kc == 0, kc == 7,
                       ("wslot%d" % s, "FB5"), ("PB%d" % b,))
                stt(maT[:, jo, :], taaT[:, jo, :], 1.0, PB[b][:, :], ALU.add, ALU.mult, ("FB4", "PB%d" % b), ("FB4",))
            release_block()

        if has_next:
            stage1_stats(g + 1)
        KT, KTk = T3(1), "FB1"
        for tb in range(NB):
            tbi = next_tb()
            for j in range(8):
                tr(TB[tbi][:, j * 128:(j + 1) * 128], KF[:, j, tb * 128:(tb + 1) * 128], (KFk,), ("TB%d" % tbi,))
            cp("act", KT[:, tb, :], TB[tbi][:, :], ("TB%d" % tbi,), (KTk,))
        szbT, tabT, V_T = T3(2), F3(5), T3(6)
        proj_T(G_ZB, lambda tb, half, bk, k: act(szbT[:, tb, half * 512:(half + 1) * 512], bk[:, :], AF.Silu, (k,),
                                                ("FB2",)))
        proj_F(G_AB, lambda j, bk, k: act(tabT[:, j, :], bk[:, :], AF.Tanh, (k,), ("FB5",), scale=0.5))
        proj_T(G_INP, lambda tb, half, bk, k: cp("dve", V_T[:, tb, half * 512:(half + 1) * 512], bk[:, :], (k,),
                                                 ("FB6",)))

        obT, obk = hT3, hTk
        pend = []

        def T_out(tb):
            on_, onk = hb[tb % 2], "hb%d" % (tb % 2)
            tbi = next_tb()
            for j in range(8):
                tr(TB[tbi][:, j * 128:(j + 1) * 128], on_[:, j * 128:(j + 1) * 128], (onk,), ("TB%d" % tbi,))
            cp("act", obT[:, :, tb * 128:(tb + 1) * 128], TB[tbi][:, :].rearrange("p (a b) -> p a b", a=8),
               ("TB%d" % tbi,), (obk,))

        for tb in range(NB):
            kA = kchunk["k"]
            kB = kA + 1
            kchunk["k"] += 2
            cA, cB = 2 * tb, 2 * tb + 1
            am, amk = Am[tb % 2], "Am%d" % (tb % 2)
            if has_next:
                stage1_scale(g + 1, tb)
            for hb_ in range(2):
                for jj in range(4):
                    j = hb_ * 4 + jj
                    mm(PB[hb_][:, jj * 128:(jj + 1) * 128], KF[:, j, tb * 128:(tb + 1) * 128], QF[:, j, tb, 0:3:2, :],
                       True, True, (KFk, "QF"), ("PB%d" % hb_,))
                tt("dve", am[:, hb_ * 4:(hb_ + 1) * 4, :], PB[hb_][:, :].rearrange("p (a b) -> p a b", a=4),
                   mbd[:, :].unsqueeze(1).broadcast_to([128, 4, 128]), ALU.mult, ("PB%d" % hb_, "mbd"), (amk,))
            for (c, k, lo) in ((cA, kA, 0), (cB, kB, 64)):
                for hb_ in range(2):
                    b = 4 + hb_
                    for jj in range(4):
                        j = hb_ * 4 + jj
                        mm(PB[b][:, jj * 128:(jj + 1) * 128], KT[lo:lo + 64, tb, j * 128:(j + 1) * 128],
                           V_T[lo:lo + 64, tb, j * 128:(j + 1) * 128], True, True, (KTk, "FB6"), ("PB%d" % b,))
                    sl = slice(hb_ * 512, (hb_ + 1) * 512)
                    sk_ = "Sst%d" % hb_
                    tt("dve", Sst[:, sl], PB[b][:, :], Sst[:, sl], ALU.add, ("PB%d" % b, sk_), (sk_,))
                    tt("dve", Sst[:, sl].rearrange("p (a b) -> p a b", a=4),
                       Sst[:, sl].rearrange("p (a b) -> p a b", a=4),
                       el[:, hb_ * 4:(hb_ + 1) * 4, c:c + 1].broadcast_to([128, 4, 128]), ALU.mult,
                       (sk_, "el"), (sk_,))
                kn = (k + 1) % 3
                cp("act", Sbf[kn][:, :], Sst[:, :], ("Sst0", "Sst1"), ("Sbf%d" % kn,))
                if lo == 0 and has_next:
                    stage1_tr(g + 1, tb)
            sA, sB = kA % 3, kB % 3
            for hb_ in range(2):
                b = 2 + hb_
                for jj in range(4):
                    j = hb_ * 4 + jj
                    o_ = PB[b][:, jj * 128:(jj + 1) * 128]
                    mm(o_, am[:, j, :], V_T[:, tb, j * 128:(j + 1) * 128], True, False, (amk, "FB6"), ("PB%d" % b,))
                    mm(o_, QF[:, j, tb, 0:2, :], Sbf[sA][:, j * 128:(j + 1) * 128], False, False,
                       ("QF", "Sbf%d" % sA), ("PB%d" % b,))
                    mm(o_, QF[:, j, tb, 1:3, :], Sbf[sB][:, j * 128:(j + 1) * 128], False, True,
                       ("QF", "Sbf%d" % sB), ("PB%d" % b,))
            oraw, ork = gvt[tb % 2], "gvt%d" % (tb % 2)
            for hb_ in range(2):
                cp("act", oraw[:, hb_ * 512:(hb_ + 1) * 512], PB[2 + hb_][:, :], ("PB%d" % (2 + hb_),), (ork,))
            c_s8 = statcol(8)
            for j in range(8):
                act(junk[:, 0:128], oraw[:, j * 128:(j + 1) * 128], AF.Square, (ork,), ("stat%d" % c_s8,),
                    accum_out=stat[:, c_s8 + j:c_s8 + j + 1])
            c_r8 = rsqrt_cols(c_s8, 8, 1.0 / 128.0, EPS, e="pool")
            tt("pool", oraw[:, :].rearrange("p (a b) -> p a b", a=8), oraw[:, :].rearrange("p (a b) -> p a b", a=8),
               stat[:, c_r8:c_r8 + 8].unsqueeze(2).broadcast_to([128, 8, 128]), ALU.mult,
               (ork, "stat%d" % c_r8), (ork,))
            on_, onk = hb[tb % 2], "hb%d" % (tb % 2)
            tt("pool", on_[:, :], oraw[:, :], szbT[:, tb, :], ALU.mult, (ork, "FB2"), (onk,))
            if tb == NB - 1 and has_next:
                hTn_, hTnk_ = HBUF[(g + 1) % 2]
                nsrc = (hTn_[:, :].rearrange("p (a b) -> p a b", a=8), hTnk_)
                proj_uza(1, nsrc)
            if pend:
                T_out(pend.pop())
            pend.append(tb)
        T_out(pend.pop())

        mgT, mgk = tabT, "FB5"
        for half in range(2):
            _, s = take_block(BLK_WB + half)
            for cb_ in range(4):
                jo = half * 4 + cb_
                b = next_pb()
                for kc in range(8):
                    mm(PB[b][:, :], wslot[s][:, kc, cb_ * 128:(cb_ + 1) * 128], obT[:, kc, :], kc == 0, kc == 7,
                       ("wslot%d" % s, obk), ("PB%d" % b,))
                f_, fk = ft[jo % 2], "ft%d" % (jo % 2)
                stt(f_[:, :], tabT[:, jo, :], 1.0, PB[b][:, :], ALU.add, ALU.mult, ("FB5", "PB%d" % b), (fk,))
                tt("dve", mgT[:, jo, :], f_[:, :], maT[:, jo, :], ALU.add, (fk, "FB4"), (mgk,))
            release_block()

        if has_next:
            dma(FB[2][:, :], wsc_d[2 * G_V], ("wsc%d" % (2 * G_V), "FB2"), ("FB2",))
            dma(hT[:, :], wsc_d[2 * G_V + 1], ("wsc%d" % (2 * G_V + 1), hTk), (hTk,))
            dma(FB[4][:, :], wsc_d[2 * G_U + 1], ("wsc%d" % (2 * G_U + 1), "FB4"), ("FB4",))

        if g == 0 and LAZY_CAST:
            assert not lazy
            stage1_load(0)
        _, s0 = take_block(BLK_WO, 0)
        _, s1_ = take_block(BLK_WO + 1, 1)
        x1b, x1bk = T3(3), "FB3"
        for tb in range(NB):
            banks = []
            c_ss2 = statcol(2)
            for half, s in ((0, s0), (1, s1_)):
                b = next_pb(6)
                banks.append(b)
                for kc in range(8):
                    mm(PB[b][:, :], mgT[:, kc, tb * 128:(tb + 1) * 128], wslot[s][:, kc, :], kc == 0, kc == 7,
                       ("wslot%d" % s, mgk), ("PB%d" % b,))
                act(junk[:, 0:512], PB[b][:, :], AF.Square, ("PB%d" % b,), ("stat%d" % c_ss2,),
                    accum_out=stat[:, c_ss2 + half:c_ss2 + half + 1])
            c_s1 = statcol(1)
            tt("dve", stat[:, c_s1:c_s1 + 1], stat[:, c_ss2:c_ss2 + 1], stat[:, c_ss2 + 1:c_ss2 + 2], ALU.add,
               ("stat%d" % c_ss2,), ("stat%d" % c_s1,))
            c_r1 = rsqrt_cols(c_s1, 1, 1.0 / D, 4.0 * EPS)
            for half in range(2):
                b = banks[half]
                f_, fk = ft[half], "ft%d" % half
                sl = slice(half * 512, (half + 1) * 512)
                stt(f_[:, :], PB[b][:, :], stat[:, c_r1:c_r1 + 1], gpost_b[:, sl], ALU.mult, ALU.mult,
                    ("PB%d" % b, "stat%d" % c_r1, "gpost_b"), (fk,))
                tt(("dve", "pool")[half], X3[:, tb, sl], X3[:, tb, sl], f_[:, :], ALU.add, XK(Xk, tb) + (fk,),
                   XK(Xk, tb))
            cp("act", x1b[:, tb, :], X3[:, tb, :], XK(Xk, tb), (x1bk,))
        release_block(2)
        if has_next:
            proj_uza(0, nsrc)
            make_gz()

        x1T, x1Tk = F3(1), "FB1"
        for tb in range(NB):
            tbi = next_tb()
            for kc in range(8):
                tr(TB[tbi][:, kc * 128:(kc + 1) * 128], x1b[:, tb, kc * 128:(kc + 1) * 128], (x1bk,),
                   ("TB%d" % tbi,))
            cp("act", x1T[:, :, tb * 128:(tb + 1) * 128], TB[tbi][:, :].rearrange("p (a b) -> p a b", a=8),
               ("TB%d" % tbi,), (x1Tk,))

        if has_next:
            proj_uza(2, nsrc)

        _, s0 = take_block(BLK_WG, 0)
        _, s1_ = take_block(BLK_WG + 1, 1)
        ple_c = {}

        def ple_head(tb):
            eg, egk = gvt[tb % 2], "gvt%d" % (tb % 2)
            c_ss2 = statcol(2)
            ple_c[tb] = c_ss2
            bg_, be_ = [], []
            for half, s in ((0, s0), (1, s1_)):
                sl = slice(half * 512, (half + 1) * 512)
                b = next_pb(6)
                bg_.append(b)
                for kc in range(8):
                    mm(PB[b][:, :], x1T[:, kc, tb * 128:(tb + 1) * 128], wslot[s][:, kc, :], kc == 0, False,
                       ("wslot%d" % s, x1Tk), ("PB%d" % b,))
                mm(PB[b][:, :], ones_row[0:1, :], bg_row[0:1, sl], False, True, ("ones_row", "bg_row"),
                   ("PB%d" % b,))
            for half in range(2):
                sl = slice(half * 512, (half + 1) * 512)
                b2 = next_pb(6)
                be_.append(b2)
                for kc in range(2):
                    mm(PB[b2][:, :], pT[:, kc, tb * 128:(tb + 1) * 128], wp[:, kc, sl], kc == 0, kc == 1,
                       ("pT", "wp"), ("PB%d" % b2,))
            for half in range(2):
                f_, fk = ft[half], "ft%d" % half
                act(f_[:, :], PB[bg_[half]][:, :], AF.Tanh, ("PB%d" % bg_[half],), (fk,), scale=0.5)
            for half in range(2):
                sl = slice(half * 512, (half + 1) * 512)
                f_, fk = ft[half], "ft%d" % half
                stt(eg[:, sl], f_[:, :], 1.0, PB[be_[half]][:, :], ALU.add, ALU.mult,
                    (fk, "PB%d" % be_[half], egk), (egk + "h%d" % half,))
            for half in range(2):
                sl = slice(half * 512, (half + 1) * 512)
                act(junk[:, 0:512], eg[:, sl], AF.Square, (egk + "h%d" % half,), ("stat%d" % c_ss2,),
                    accum_out=stat[:, c_ss2 + half:c_ss2 + half + 1])

        def ple_tail(tb):
            eg, egk = gvt[tb % 2], "gvt%d" % (tb % 2)
            c_ss2 = ple_c[tb]
            c_s1 = statcol(1)
            tt("dve", stat[:, c_s1:c_s1 + 1], stat[:, c_ss2:c_ss2 + 1], stat[:, c_ss2 + 1:c_ss2 + 2], ALU.add,
               ("stat%d" % c_ss2,), ("stat%d" % c_s1,))
            c_r1 = rsqrt_cols(c_s1, 1, 1.0 / D, 4.0 * EPS)
            stt(eg[:, :], eg[:, :], stat[:, c_r1:c_r1 + 1], gple_b[:, :], ALU.mult, ALU.mult,
                (egk + "h0", egk + "h1", "stat%d" % c_r1, "gple_b"), (egk + "h0", egk + "h1"))
            tt("dve", eg[:, :], eg[:, :], X3[:, tb, :], ALU.add, (egk + "h0", egk + "h1") + XK(Xk, tb),
               (egk + "h0", egk + "h1", egk))
            dma(y_d[r0 + tb * 128:r0 + (tb + 1) * 128, :], eg[:, :], (egk, egk + "h0", egk + "h1"), ("y",))

        ple_head(0)
        for tb in range(1, NB):
            ple_head(tb)
            ple_tail(tb - 1)
        ple_tail(NB - 1)
        release_block(2)

    for g in range(total_tiles):
        tile_body(g)

    P.emit(es)
    es.close()
    return nc


def _host_inputs(x, p, w_in, gmlp_ln_g, gmlp_ln_b, gmlp_w_s, gmlp_b_s, hgrn_lb_logits, hgrn_norm_g,
                 w_branch_a, w_branch_b, w_out, g_pre, g_post, w_ple, w_ple_gate, b_ple_gate, g_ple):
    f32 = np.float32

    def blocks(w):
        out = []
        for c0 in range(0, w.shape[1], 512):
            out.append(np.ascontiguousarray(w[:, c0:c0 + 512].reshape(8, 128, 512).transpose(1, 0, 2)).reshape(128, 4096))
        return out

    wb = blocks(np.asarray(w_in[0], f32)) + blocks(np.asarray(w_branch_a[0], f32)) + \
        blocks(np.asarray(w_branch_b[0], f32)) + blocks(np.asarray(w_out[0], f32)) + \
        blocks(np.asarray(w_ple_gate[0], f32))
    wblk = np.stack(wb, 0).astype(f32)

    def colF(v):
        return np.ascontiguousarray(np.asarray(v, f32).reshape(8, 128).T)

    tt_ = np.arange(128)
    triu = (tt_[:, None] <= tt_[None, :]).astype(f32)
    mbd = (triu * ((tt_[:, None] // CH) == (tt_[None, :] // CH))).astype(ml_dtypes.bfloat16)
    rm = np.ones((128, TT), f32)
    rm[:, ::CH] = 0.0
    common = {
        "wblk": wblk,
        "wp": np.ascontiguousarray(np.asarray(w_ple[0], f32).reshape(2, 128, D).transpose(1, 0, 2)).reshape(128, 2 * D),
        "gpreF": colF(g_pre[0]),
        "hgF": colF(np.asarray(hgrn_norm_g[0]).reshape(-1)),
        "lngF": colF(gmlp_ln_g[0]),
        "lnb": np.asarray(gmlp_ln_b[0], f32).reshape(1, D),
        "lng_row": np.asarray(gmlp_ln_g[0], f32).reshape(1, D),
        "bs": np.asarray(gmlp_b_s[0], f32).reshape(1, 512),
        "wsT": np.ascontiguousarray(np.asarray(gmlp_w_s[0], f32).transpose(2, 0, 1)).reshape(128, 512),
        "lbl": np.ascontiguousarray(np.asarray(hgrn_lb_logits, f32).reshape(2, 8, 128).transpose(2, 0, 1)).reshape(128, 16),
        "gpost_b": np.ascontiguousarray(np.broadcast_to(np.asarray(g_post[0], f32), (128, D))),
        "gple_b": np.ascontiguousarray(np.broadcast_to(np.asarray(g_ple[0], f32), (128, D))),
        "bg": np.asarray(b_ple_gate[0], f32).reshape(1, D),
        "ident": np.eye(128, dtype=f32).astype(ml_dtypes.bfloat16),
        "triu": triu,
        "mbd": mbd,
        "rm": rm,
    }
    return common


_CACHE = {}


def run(inputs, nseq, T, ncores, x_full, p_full):
    common = _host_inputs(**inputs)
    key = (nseq, T)
    nc = build(nseq, T)
    in_maps = []
    for c in range(ncores):
        m = dict(common)
        m["x"] = np.ascontiguousarray(x_full[c * nseq:(c + 1) * nseq].reshape(nseq * T, D))
        m["p"] = np.ascontiguousarray(p_full[c * nseq:(c + 1) * nseq].reshape(nseq * T, PLE))
        in_maps.append(m)
    res = run_bass_kernel_spmd(nc, in_maps, core_ids=list(range(ncores)))
    outs = [np.asarray(r["y"]).reshape(nseq, T, D) for r in res.results]
    return np.concatenate(outs, 0).astype(np.float32)


def kernel(**inputs):
    x = np.asarray(inputs["x"], np.float32)
    p = np.asarray(inputs["p"], np.float32)[0]
    B, T, _ = x.shape
    nseq = B // NCORES
    return run(inputs, nseq, T, NCORES, x, p)
```

```python
import math
from contextlib import ExitStack

import numpy as np
import ml_dtypes

import concourse.bass as bass
import concourse.mybir as mybir
from concourse.bass_utils import run_bass_kernel_spmd

F32 = mybir.dt.float32
BF16 = mybir.dt.bfloat16
AF = mybir.ActivationFunctionType
ALU = mybir.AluOpType
AX = mybir.AxisListType

D = 1024
PLE = 256
EPS = 1e-6
NCORES = 8
TT = 512
NB = TT // 128
CH = 64
NBLK = 26
NSLOT = 3
LAZY_CAST = True

G_U, G_V, G_ZA, G_Q, G_F, G_INP, G_ZB, G_AA, G_AB = range(9)
BLK_WA, BLK_WB, BLK_WO, BLK_WG = 18, 20, 22, 24


class _Op:
    __slots__ = ("eng", "fn", "deps", "is_dma", "sig", "count", "sem", "val", "gid")

    def __init__(self, eng, fn, is_dma):
        self.eng = eng
        self.fn = fn
        self.deps = set()
        self.is_dma = is_dma
        self.sig = False
        self.count = 0
        self.sem = None
        self.val = 0
        self.gid = 0


class Prog:
    ENGS = ("pe", "act", "dve", "pool", "sp")

    def __init__(self, nc):
        self.nc = nc
        self.ops = []
        self.last_w = {}
        self.readers = {}

    def add(self, eng, fn, reads=(), writes=(), dma=False):
        op = _Op(eng, fn, dma)
        op.gid = len(self.ops)
        deps = set()
        for k in reads:
            w = self.last_w.get(k)
            if w is not None:
                deps.add(w)
        for k in writes:
            w = self.last_w.get(k)
            if w is not None:
                deps.add(w)
            deps |= self.readers.get(k, set())
        deps.discard(op)
        op.deps = deps
        for k in reads:
            self.readers.setdefault(k, set()).add(op)
        for k in writes:
            self.last_w[k] = op
            self.readers[k] = set()
        self.ops.append(op)
        return op

    def emit(self, es, n_dma_sems=24):
        nc = self.nc
        engobj = {"pe": nc.tensor, "act": nc.scalar, "dve": nc.vector, "pool": nc.gpsimd, "sp": nc.sync}
        for op in self.ops:
            best = {}
            keep = set()
            for d in op.deps:
                if d.is_dma:
                    keep.add(d)
                    continue
                if d.eng == "pe" and op.eng == "pe":
                    continue
                if d.eng not in best or best[d.eng].gid < d.gid:
                    best[d.eng] = d
            for d in best.values():
                d.sig = True
                keep.add(d)
            op.deps = keep
        sems = {e: es.enter_context(nc.semaphore("sem_" + e)) for e in ("pe", "act", "dve", "pool")}
        dsems = [es.enter_context(nc.semaphore("dsem%d" % i)) for i in range(n_dma_sems)]
        dcount = [0] * n_dma_sems
        dprev = [None] * n_dma_sems
        cnt = {e: 0 for e in sems}
        ndma = 0
        for op in self.ops:
            if op.is_dma:
                i = ndma % n_dma_sems
                ndma += 1
                op.sem = dsems[i]
                dcount[i] += 16
                op.val = dcount[i]
                if dprev[i] is not None:
                    op.deps.add(dprev[i])
                dprev[i] = op
            elif op.sig:
                cnt[op.eng] += 1
                op.count = cnt[op.eng]
                op.sem = sems[op.eng]
                op.val = op.count
        per_eng = {e: [o for o in self.ops if o.eng == e] for e in self.ENGS}
        block = es.enter_context(nc.Block())

        def run(e):
            eo = engobj[e]
            waited = {}
            for op in per_eng[e]:
                need = {}
                for d in op.deps:
                    if (not d.is_dma) and d.eng == "pe" and e == "pe":
                        continue
                    key = id(d.sem)
                    if waited.get(key, 0) >= d.val:
                        continue
                    if key not in need or need[key][1] < d.val:
                        need[key] = (d.sem, d.val)
                for key, (s, v) in need.items():
                    eo.wait_ge(s, v)
                    waited[key] = v
                inst = op.fn()
                if op.is_dma:
                    inst.then_inc(op.sem, 16)
                elif op.sig:
                    inst.then_inc(op.sem, 1)
            return eo

        @block.tensor
        def _(eng):
            run("pe")

        @block.scalar
        def _(eng):
            run("act")

        @block.vector
        def _(eng):
            run("dve")

        @block.gpsimd
        def _(eng):
            run("pool")

        @block.sync
        def _(eng):
            eo = run("sp")
            for i in range(n_dma_sems):
                if dcount[i]:
                    eo.wait_ge(dsems[i], dcount[i])


def build(nseq, T, debug=False):
    assert T % TT == 0
    ntok = nseq * T
    ntile_seq = T // TT
    nc = bass.Bass("TRN2", target_bir_lowering=False)
    es = ExitStack()
    P = Prog(nc)

    def din(name, shape, dt=F32):
        return nc.dram_tensor(name, list(shape), dt, kind="ExternalInput").ap()

    x_d = din("x", [ntok, D])
    p_d = din("p", [ntok, PLE])
    wblk_d = din("wblk", [NBLK, 128, 8 * 512])
    wp_d = din("wp", [128, 2 * D])
    gpreF_d = din("gpreF", [128, 8])
    hgF_d = din("hgF", [128, 8])
    lngF_d = din("lngF", [128, 8])
    lnb_d = din("lnb", [1, D])
    lngr_d = din("lng_row", [1, D])
    bs_d = din("bs", [1, 512])
    wsT_d = din("wsT", [128, 512])
    lbl_d = din("lbl", [128, 16])
    gpost_d = din("gpost_b", [128, D])
    gple_d = din("gple_b", [128, D])
    bg_d = din("bg", [1, D])
    ident_d = din("ident", [128, 128], BF16)
    triu_d = din("triu", [128, 128])
    mbd_d = din("mbd", [128, 128], BF16)
    rm_d = din("rm", [128, TT])
    y_d = nc.dram_tensor("y", [ntok, D], F32, kind="ExternalOutput").ap()
    wsc_d = nc.dram_tensor("wsc", [NBLK, 128, 8 * 512], BF16, kind="Internal").ap()
    dbg_d = {}

    def sb(name, shape, dt=F32):
        return es.enter_context(nc.sbuf_tensor("s_" + name, list(shape), dt))

    def ps(name, shape, dt=F32):
        return es.enter_context(nc.psum_tensor("ps_" + name, list(shape), dt))

    ident = sb("ident", [128, 128], BF16)
    mbd = sb("mbd", [128, 128], BF16)
    rm = sb("rm", [128, TT])
    wsT = sb("wsT", [128, 4, 128], BF16)
    L2p = sb("L2p", [2, D])
    gpost_b = sb("gpost_b", [128, D])
    gple_b = sb("gple_b", [128, D])
    wp = sb("wp", [128, 2, D], BF16)
    small = sb("small", [128, 128])
    ones_row = sb("ones_row", [1, 128], BF16)
    bg_row = sb("bg_row", [1, D], BF16)
    neghalf = sb("neghalf", [128, 32])
    C_GPRE, C_HG, C_LNG, C_LSC, C_LBI, C_LHO = 0, 8, 16, 24, 32, 40
    gpreF = small[:, C_GPRE:C_GPRE + 8]
    hgF = small[:, C_HG:C_HG + 8]
    lngF = small[:, C_LNG:C_LNG + 8]
    lscF = small[:, C_LSC:C_LSC + 8]
    lbiF = small[:, C_LBI:C_LBI + 8]
    lhoF = small[:, C_LHO:C_LHO + 8]
    tmpc = small[:, 48:80]

    wslot = [sb("wslot%d" % i, [128, 8, 512], BF16) for i in range(NSLOT)]
    FB = [sb("FB%d" % i, [128, 8 * TT], BF16) for i in range(8)]
    F32B = [sb("F32B%d" % i, [128, 8 * TT]) for i in range(1)]
    QF = sb("QF", [128, 8, NB, 3, CH], BF16)
    xt = sb("xt", [128, NB, D])
    hn = [sb("hn%d" % i, [128, D], BF16) for i in range(2)]
    pT = sb("pT", [128, 2, TT], BF16)
    hb = [sb("hb%d" % i, [128, D], BF16) for i in range(2)]
    junk = sb("junk", [128, D], BF16)
    gvt = [sb("gvt%d" % i, [128, D]) for i in range(2)]
    ft = [sb("ft%d" % i, [128, TT]) for i in range(6)]
    fe = [sb("fe%d" % i, [128, TT], BF16) for i in range(4)]
    Sst = sb("Sst", [128, D])
    Sbf = [sb("Sbf%d" % i, [128, D], BF16) for i in range(3)]
    Am = [sb("Am%d" % i, [128, 8, 128], BF16) for i in range(2)]
    el = sb("el", [128, 8, 8])
    stat = sb("stat", [128, 256])

    PB = [ps("PB%d" % i, [128, 512]) for i in range(6)]
    TB = [ps("TB%d" % i, [128, 1024], BF16) for i in range(2)]

    st = {"pb": 0, "tb": 0, "dve_pool": 0, "stat": 0, "gv": 0, "ft": 0}

    def next_pb(n=6):
        i = st["pb"] % n
        st["pb"] += 1
        return i

    def next_tb():
        i = st["tb"] % 2
        st["tb"] += 1
        return i

    def statcol(n):
        c = st["stat"]
        if c + n > 256:
            c = 0
        st["stat"] = c + n
        return c

    def dma(out, in_, reads, writes):
        return P.add("sp", lambda: nc.sync.dma_start(out=out, in_=in_), reads, writes, dma=True)

    def act(out, in_, func, reads, writes, scale=None, bias=None, accum_out=None):
        kw = {}
        if scale is not None:
            kw["scale"] = scale
        if bias is not None:
            kw["bias"] = bias
        if accum_out is not None:
            kw["accum_out"] = accum_out
        return P.add("act", lambda: nc.scalar.activation(out=out, in_=in_, func=func, **kw), reads, writes)

    def veng(e):
        return nc.vector if e == "dve" else nc.gpsimd

    def tt(e, out, in0, in1, op, reads, writes):
        return P.add(e, lambda: veng(e).tensor_tensor(out=out, in0=in0, in1=in1, op=op), reads, writes)

    def ts(e, out, in0, s1, s2, op0, op1, reads, writes):
        if op1 is None:
            return P.add(e, lambda: veng(e).tensor_scalar(out=out, in0=in0, scalar1=s1, scalar2=None, op0=op0),
                         reads, writes)
        return P.add(e, lambda: veng(e).tensor_scalar(out=out, in0=in0, scalar1=s1, scalar2=s2, op0=op0, op1=op1),
                     reads, writes)

    def stt(out, in0, scalar, in1, op0, op1, reads, writes):
        return P.add("dve", lambda: nc.vector.scalar_tensor_tensor(out=out, in0=in0, scalar=scalar, in1=in1,
                                                                   op0=op0, op1=op1), reads, writes)

    def cp(e, out, in_, reads, writes):
        if e == "act":
            return P.add("act", lambda: nc.scalar.copy(out=out, in_=in_), reads, writes)
        return P.add(e, lambda: veng(e).tensor_copy(out=out, in_=in_), reads, writes)

    def mm(out, lhsT, rhs, start, stop, reads, writes):
        return P.add("pe", lambda: nc.tensor.matmul(out, lhsT, rhs, start=start, stop=stop), reads, writes)

    def tr(out, in_, reads, writes):
        return P.add("pe", lambda: nc.tensor.transpose(out, in_, ident[:, :]), reads + ("ident",), writes)

    def rsqrt_cols(c_in, n, scale, eps, e="dve"):
        c_ms = statcol(n)
        c_out = statcol(n)
        ts(e, stat[:, c_ms:c_ms + n], stat[:, c_in:c_in + n], scale, eps, ALU.mult, ALU.add,
           ("stat%d" % c_in,), ("stat%d" % c_ms,))
        tt("pool", stat[:, c_out:c_out + n], stat[:, c_ms:c_ms + n], neghalf[:, 0:n], ALU.pow,
           ("stat%d" % c_ms, "neghalf"), ("stat%d" % c_out,))
        return c_out

    stgA = F32B[0][:, :]
    stgB = xt[:, :, :].rearrange("p a b -> p (a b)")
    def XKp(k):
        return tuple("%s.%d" % (k, i) for i in range(NB))

    EAGER = (BLK_WO, BLK_WO + 1, BLK_WG, BLK_WG + 1) if LAZY_CAST else tuple(range(NBLK))
    for b in EAGER:
        stg, sk = ((stgA, "F32B0"), (stgB, "xt"))[b % 2]
        cb, ck = ((FB[4], "FB4"), (FB[5], "FB5"))[b % 2]
        dma(stg, wblk_d[b], XKp(sk), XKp(sk))
        fold = gpreF if b < 18 else (hgF if b in (BLK_WB, BLK_WB + 1) else None)
        if fold is None:
            cp("dve", cb[:, 0:2048], stg[:, 0:2048], XKp(sk), (ck,))
            cp("act", cb[:, 2048:4096], stg[:, 2048:4096], XKp(sk), (ck,))
        else:
            for kc in range(8):
                if kc % 2 == 0:
                    ts("dve", cb[:, kc * 512:(kc + 1) * 512], stg[:, kc * 512:(kc + 1) * 512], fold[:, kc:kc + 1],
                       None, ALU.mult, None, XKp(sk) + ("small",), (ck,))
                else:
                    act(cb[:, kc * 512:(kc + 1) * 512], stg[:, kc * 512:(kc + 1) * 512], AF.Copy,
                        XKp(sk) + ("small",), (ck,), scale=fold[:, kc:kc + 1])
        dma(wsc_d[b], cb[:, :], (ck,), ("wsc%d" % b,))

    dma(ident[:, :], ident_d, (), ("ident",))
    dma(mbd[:, :], mbd_d, (), ("mbd",))
    dma(rm[:, :], rm_d, (), ("rm",))
    dma(gpost_b[:, :], gpost_d, (), ("gpost_b",))
    dma(gple_b[:, :], gple_d, (), ("gple_b",))
    dma(small[:, C_GPRE:C_GPRE + 8], gpreF_d, (), ("small",))
    dma(small[:, C_HG:C_HG + 8], hgF_d, (), ("small",))
    dma(small[:, C_LNG:C_LNG + 8], lngF_d, (), ("small",))
    P.add("dve", lambda: nc.vector.memset(neghalf[:, :], -0.5), (), ("neghalf",))
    P.add("dve", lambda: nc.vector.memset(QF[:, :, :, :, :], 0.0), (), ("QF",))
    P.add("pool", lambda: nc.gpsimd.memset(ones_row[:, :], 1.0), (), ("ones_row",))
    lbl = ft[0]
    dma(lbl[:, 0:16], lbl_d, (), ("ft0",))
    tt("dve", tmpc[:, 0:8], lbl[:, 0:8], lbl[:, 8:16], ALU.subtract, ("ft0",), ("tmpc",))
    act(tmpc[:, 8:16], tmpc[:, 0:8], AF.Sigmoid, ("tmpc",), ("tmpc",), scale=-1.0)
    ts("dve", small[:, C_LSC:C_LSC + 8], tmpc[:, 8:16], -0.5, None, ALU.mult, None, ("tmpc",), ("small",))
    ts("dve", small[:, C_LBI:C_LBI + 8], tmpc[:, 8:16], -0.5, 1.0, ALU.mult, ALU.add, ("tmpc",), ("small",))
    act(small[:, C_LHO:C_LHO + 8], tmpc[:, 8:16], AF.Ln, ("tmpc",), ("small",), scale=0.5)
    bgf = ft[1]
    dma(bgf[0:1, 0:512], bg_d[:, 0:512], (), ("ft1",))
    cp("dve", bg_row[0:1, 0:512], bgf[0:1, 0:512], ("ft1",), ("bg_row",))
    dma(bgf[0:1, 0:512], bg_d[:, 512:1024], ("ft1",), ("ft1",))
    cp("dve", bg_row[0:1, 512:1024], bgf[0:1, 0:512], ("ft1",), ("bg_row",))
    wpf = gvt[0]
    for h in range(2):
        dma(wpf[:, :], wp_d[:, h * D:(h + 1) * D], ("gvt0",), ("gvt0",))
        cp("dve", wp[:, h, :], wpf[:, :], ("gvt0",), ("wp",))
    wsf = ft[2]
    triu = ft[3]
    dma(wsf[:, :], wsT_d, (), ("ft2",))
    dma(triu[:, 0:128], triu_d, (), ("ft3",))
    tt("dve", wsT[:, :, :], wsf[:, :].rearrange("p (g t) -> p g t", g=4),
       triu[:, 0:128].unsqueeze(1).broadcast_to([128, 4, 128]), ALU.mult, ("ft2", "ft3"), ("wsT",))
    onescol = sb("onescol", [128, 1], BF16)
    P.add("dve", lambda: nc.vector.memset(onescol[:, :], 1.0), (), ("onescol",))
    L2 = gvt[1][0:2, :]
    R2 = ft[5][0:2, :]
    G2 = gvt[0][0:2, :]
    P.add("dve", lambda: nc.vector.memset(L2[:, :], 1.0), (), ("gvt1",))
    dma(L2[0:1, :], lnb_d, ("gvt1",), ("gvt1",))
    dma(G2[0:1, :], lngr_d, (), ("gvt0",))
    dma(G2[1:2, :], lngr_d, (), ("gvt0",))
    P.add("dve", lambda: nc.vector.reciprocal(out=G2[:, :], in_=G2[:, :]), ("gvt0",), ("gvt0",))
    tt("dve", L2p[:, :], L2[:, :], G2[:, :], ALU.mult, ("gvt1", "gvt0"), ("L2p",))
    mm(PB[0][0:1, :], onescol[:, 0:1], wsT[:, :, :].rearrange("p g t -> p (g t)"), True, True,
       ("onescol", "wsT"), ("PB0",))
    cp("dve", R2[0:1, :], PB[0][0:1, :], ("PB0",), ("ft5",))
    dma(R2[1:2, :], bs_d, ("ft5",), ("ft5",))

    ring = {"n": 0}
    lazy = set(range(NBLK)) - set(EAGER)

    def issue_block(blk):
        s = ring["n"] % NSLOT
        ring["n"] += 1
        sk_ = "wslot%d" % s
        if blk in lazy:
            lazy.discard(blk)
            fold = gpreF if blk < 18 else (hgF if blk in (BLK_WB, BLK_WB + 1) else None)
            for hf in range(2):
                stg = xt[:, 2 * hf:2 * hf + 2, :].rearrange("p a b -> p (a b)")
                lk = "xtL%d" % hf
                dma(stg, wblk_d[blk][:, hf * 2048:(hf + 1) * 2048], (),
                    (lk, "xt.%d" % (2 * hf), "xt.%d" % (2 * hf + 1)))
                if fold is None:
                    cp("pool", wslot[s][:, 4 * hf:4 * hf + 4, :].rearrange("p a b -> p (a b)"), stg, (lk,), (sk_,))
                else:
                    for k4 in range(4):
                        kc = 4 * hf + k4
                        ts("pool", wslot[s][:, kc, :], stg[:, k4 * 512:(k4 + 1) * 512], fold[:, kc:kc + 1], 0.0,
                           ALU.mult, ALU.add, (lk, "small"), (sk_,))
            dma(wsc_d[blk], wslot[s][:, :, :].rearrange("p a b -> p (a b)"), (sk_,), ("wsc%d" % blk,))
        else:
            dma(wslot[s][:, :, :].rearrange("p a b -> p (a b)"), wsc_d[blk], ("wsc%d" % blk, sk_), (sk_,))
        return s

    total_tiles = nseq * ntile_seq

    def gb(*gs):
        out = []
        for g in gs:
            out += [2 * g, 2 * g + 1]
        return out

    stream = []
    for g_ in range(total_tiles):
        if g_ == 0:
            stream += gb(G_U, G_ZA, G_AA)
        stream += (gb(G_V) if g_ == 0 else []) + gb(G_Q, G_F) + [BLK_WA, BLK_WA + 1] + gb(G_ZB, G_AB, G_INP)
        if g_ + 1 < total_tiles:
            stream += gb(G_ZA)
        stream += [BLK_WB, BLK_WB + 1, BLK_WO, BLK_WO + 1]
        if g_ + 1 < total_tiles:
            stream += [2 * G_U] + gb(G_AA)
        stream += [BLK_WG, BLK_WG + 1]
    sp_ = {"issued": 0, "used": 0}
    slot_of = {}

    def prefetch():
        while sp_["issued"] < len(stream) and sp_["issued"] - sp_["used"] < NSLOT:
            i = sp_["issued"]
            slot_of[i] = issue_block(stream[i])
            sp_["issued"] += 1

    def take_block(expect, off=0):
        i = sp_["used"] + off
        assert stream[i] == expect, (stream[i], expect)
        assert i < sp_["issued"]
        return i, slot_of[i]

    def release_block(n=1):
        sp_["used"] += n
        prefetch()

    prefetch()

    kchunk = {"k": 0}
    XBUF = [(xt[:, :, :].rearrange("p a b -> p (a b)"), "xt"), (F32B[0][:, :], "F32B0")]

    def XK(k, tb=None):
        if tb is None:
            return tuple("%s.%d" % (k, i) for i in range(NB))
        return ("%s.%d" % (k, tb),)

    HBUF = [(FB[0], "FB0"), (FB[7], "FB7")]
    s1 = {}

    def F3(i):
        return FB[i][:, :].rearrange("p (a b) -> p a b", a=8)

    def T3(i):
        return FB[i][:, :].rearrange("p (a b) -> p a b", a=NB)

    def stage1_load(g):
        X, Xk = XBUF[g % 2]
        r0 = g * TT
        wk = XK(Xk) + (("xtL0", "xtL1") if Xk == "xt" else ())
        dma(X.rearrange("p (a b) -> p a b", a=NB), x_d[r0:r0 + TT, :].rearrange("(b p) d -> p b d", p=128),
            XK(Xk), wk)

    def stage1_stats(g):
        X, Xk = XBUF[g % 2]
        X3 = X.rearrange("p (a b) -> p a b", a=NB)
        c_ss = statcol(NB)
        for tb in range(NB):
            act(junk[:, :], X3[:, tb, :], AF.Square, XK(Xk, tb), ("stat%d" % c_ss,),
                accum_out=stat[:, c_ss + tb:c_ss + tb + 1])
        s1[g] = rsqrt_cols(c_ss, NB, 1.0 / D, EPS)

    def stage1_scale(g, tb):
        X, Xk = XBUF[g % 2]
        X3 = X.rearrange("p (a b) -> p a b", a=NB)
        c_r = s1[g]
        h, hk = hn[tb % 2], "hn%d" % (tb % 2)
        ts("dve", h[:, :], X3[:, tb, :], stat[:, c_r + tb:c_r + tb + 1], None, ALU.mult, None,
           XK(Xk, tb) + ("stat%d" % c_r,), (hk,))

    def stage1_tr(g, tb):
        hTn, hTnk = HBUF[g % 2]
        hTn3 = hTn[:, :].rearrange("p (a b) -> p a b", a=8)
        h, hk = hn[tb % 2], "hn%d" % (tb % 2)
        tbi = next_tb()
        for kc in range(8):
            tr(TB[tbi][:, kc * 128:(kc + 1) * 128], h[:, kc * 128:(kc + 1) * 128], (hk,), ("TB%d" % tbi,))
        cp("act", hTn3[:, :, tb * 128:(tb + 1) * 128], TB[tbi][:, :].rearrange("p (a b) -> p a b", a=8),
           ("TB%d" % tbi,), (hTnk,))

    def tile_body(g):
        r0 = g * TT
        X, Xk = XBUF[g % 2]
        S, Sk = XBUF[(g + 1) % 2]
        X3 = X.rearrange("p (a b) -> p a b", a=NB)
        hT, hTk = HBUF[g % 2]
        hT3 = hT[:, :].rearrange("p (a b) -> p a b", a=8)
        has_next = g + 1 < total_tiles

        if g == 0:
            stage1_load(0)
            stage1_stats(0)
            for tb in range(NB):
                stage1_scale(0, tb)
                stage1_tr(0, tb)
        if g % ntile_seq == 0:
            P.add("pool", lambda: nc.gpsimd.memset(Sst[:, :], 0.0), (), ("Sst0", "Sst1"))
            k0 = kchunk["k"] % 3
            P.add("pool", lambda: nc.gpsimd.memset(Sbf[k0][:, :], 0.0), (), ("Sbf%d" % k0,))

        pt3 = gvt[1][:, :].rearrange("p (a b) -> p a b", a=NB)
        dma(pt3, p_d[r0:r0 + TT, :].rearrange("(b p) d -> p b d", p=128), ("gvt1",), ("gvt1",))
        ptb = hb[0][:, :].rearrange("p (a b) -> p a b", a=NB)
        cp("pool", ptb[:, :, :], pt3, ("gvt1",), ("hb0",))

        def proj_F(gg, evac, src=None, wsrc=None):
            h3_, hk_ = src if src is not None else (hT3, hTk)
            for half in range(2):
                if wsrc is not None and half in wsrc:
                    w3, wk_ = wsrc[half]
                    ring_ = False
                else:
                    _, s = take_block(2 * gg + half)
                    w3, wk_ = wslot[s], "wslot%d" % s
                    ring_ = True
                for cb_ in range(4):
                    b = next_pb()
                    for kc in range(8):
                        mm(PB[b][:, :], w3[:, kc, cb_ * 128:(cb_ + 1) * 128], h3_[:, kc, :], kc == 0, kc == 7,
                           (wk_, hk_), ("PB%d" % b,))
                    evac(half * 4 + cb_, PB[b], "PB%d" % b)
                if ring_:
                    release_block()

        def proj_T(gg, evac, wsrc=None):
            for half in range(2):
                if wsrc is None:
                    _, s = take_block(2 * gg + half)
                    w3, wk_ = wslot[s], "wslot%d" % s
                else:
                    w3, wk_ = wsrc[half]
                for tb in range(NB):
                    b = next_pb()
                    for kc in range(8):
                        mm(PB[b][:, :], hT3[:, kc, tb * 128:(tb + 1) * 128], w3[:, kc, :], kc == 0, kc == 7,
                           (wk_, hTk), ("PB%d" % b,))
                    evac(tb, half, PB[b], "PB%d" % b)
                if wsrc is None:
                    release_block()

        guT, szaT, taaT = F3(5), F3(6), F3(4)

        def proj_uza(which, src=None):
            if which == 0:
                wsrc_ = None
                if src is not None:
                    wsrc_ = {1: (FB[4][:, :].rearrange("p (a b) -> p a b", a=8), "FB4")}
                proj_F(G_U, lambda j, bk, k: act(guT[:, j, :], bk[:, :], AF.Gelu, (k,), ("FB5",)), src, wsrc_)
            elif which == 1:
                proj_F(G_ZA, lambda j, bk, k: act(szaT[:, j, :], bk[:, :], AF.Silu, (k,), ("FB6",)), src)
            else:
                proj_F(G_AA, lambda j, bk, k: act(taaT[:, j, :], bk[:, :], AF.Tanh, (k,), ("FB4",), scale=0.5), src)

        def make_gz():
            for j in range(8):
                tt("pool", szaT[:, j, :], szaT[:, j, :], guT[:, j, :], ALU.mult, ("FB6", "FB5"), ("FB6",))

        if g == 0:
            for w_ in range(3):
                proj_uza(w_)
            make_gz()

        vn = T3(2)
        gv = S.rearrange("p (a b) -> p a b", a=NB)
        vsrc = None
        if g > 0:
            hprev, hprevk = HBUF[(g - 1) % 2]
            vsrc = [(FB[2][:, :].rearrange("p (a b) -> p a b", a=8), "FB2"),
                    (hprev[:, :].rearrange("p (a b) -> p a b", a=8), hprevk)]
        proj_T(G_V, lambda tb, half, bk, k: act(gv[:, tb, half * 512:(half + 1) * 512], bk[:, :], AF.Gelu, (k,),
                                               XK(Sk, tb)), vsrc)
        for tb in range(NB):
            c_bs = statcol(12)
            for hh in range(2):
                P.add("dve", (lambda tb=tb, hh=hh, c=c_bs: nc.vector.bn_stats(
                    out=stat[:, c + 6 * hh:c + 6 * hh + 6], in_=gv[:, tb, hh * 512:(hh + 1) * 512])),
                    XK(Sk, tb), ("stat%d" % c_bs,))
            c_mv = statcol(2)
            P.add("dve", (lambda c=c_bs, m=c_mv: nc.vector.bn_aggr(out=stat[:, m:m + 2], in_=stat[:, c:c + 12])),
                  ("stat%d" % c_bs,), ("stat%d" % c_mv,))
            c_var = statcol(1)
            cp("dve", stat[:, c_var:c_var + 1], stat[:, c_mv + 1:c_mv + 2], ("stat%d" % c_mv,), ("stat%d" % c_var,))
            c_rs = rsqrt_cols(c_var, 1, 1.0, EPS)
            c_nm = statcol(1)
            stt(stat[:, c_nm:c_nm + 1], stat[:, c_mv:c_mv + 1], -1.0, stat[:, c_rs:c_rs + 1], ALU.mult, ALU.mult,
                ("stat%d" % c_mv, "stat%d" % c_rs), ("stat%d" % c_nm,))
            ts("dve", vn[:, tb, :], gv[:, tb, :], stat[:, c_rs:c_rs + 1], stat[:, c_nm:c_nm + 1], ALU.mult, ALU.add,
               XK(Sk, tb) + ("stat%d" % c_rs, "stat%d" % c_nm), ("FB2",))

        sqF = F3(1)
        tfall = S.rearrange("p (a b) -> p a b", a=8)
        proj_F(G_Q, lambda j, bk, k: act(sqF[:, j, :], bk[:, :], AF.Silu, (k,), ("FB1",)))
        proj_F(G_F, lambda j, bk, k: act(tfall[:, j, :], bk[:, :], AF.Tanh, (k,), XK(Sk, j // 2), scale=-0.5))

        KF, KFk = F3(3), "FB3"
        lnsc = math.log(128.0 ** -0.5)

        def f_ln(j):
            lf, lfk = ft[j % 3], "ft%d" % (j % 3)
            act(lf[:, :], tfall[:, j, :], AF.Ln, XK(Sk, j // 2) + ("small",), (lfk,), scale=lscF[:, j:j + 1],
                bias=lbiF[:, j:j + 1])

        def f_scan(j):
            lf, lfk = ft[j % 3], "ft%d" % (j % 3)
            cm, cmk = ft[3 + (j % 2)], "ft%d" % (3 + j % 2)
            P.add("dve", (lambda cm=cm, lf=lf: nc.vector.tensor_tensor_scan(
                out=cm[:, :], data0=rm[:, :], data1=lf[:, :], initial=0.0, op0=ALU.mult, op1=ALU.add)),
                (lfk, "rm"), (cmk,))

        def f_rest(j):
            cm, cmk = ft[3 + (j % 2)], "ft%d" % (3 + j % 2)
            E, Ek = fe[j % 2], "fe%d" % (j % 2)
            Ei, Eik = fe[2 + j % 2], "fe%d" % (2 + j % 2)
            act(E[:, :], cm[:, :], AF.Exp, (cmk,), (Ek,), bias=float(lnsc))
            act(Ei[:, :], cm[:, :], AF.Exp, (cmk, "small"), (Eik,), scale=-1.0, bias=lhoF[:, j:j + 1])
            act(el[:, j, :], cm[:, :].rearrange("p (c i) -> p c i", i=CH)[:, :, CH - 1], AF.Exp, (cmk,), ("el",))
            tt("pool", QF[:, j, :, 0:3:2, :], sqF[:, j, :].rearrange("p (a b c) -> p a b c", a=NB, b=2),
               E[:, :].rearrange("p (a b c) -> p a b c", a=NB, b=2), ALU.mult, ("FB1", Ek), ("QF",))
            stt(KF[:, j, :], tfall[:, j, :], 1.0, Ei[:, :], ALU.add, ALU.mult, XK(Sk, j // 2) + (Eik,), (KFk,))

        tbi = next_tb()
        for tb in range(NB):
            for kc in range(2):
                tr(TB[tbi][:, (tb * 2 + kc) * 128:(tb * 2 + kc + 1) * 128], ptb[:, tb, kc * 128:(kc + 1) * 128],
                   ("hb0",), ("TB%d" % tbi,))
        cp("dve", pT[:, :, :].rearrange("p k (b t) -> p b k t", b=NB),
           TB[tbi][:, :].rearrange("p (b k t) -> p b k t", b=NB, k=2), ("TB%d" % tbi,), ("pT",))

        yapreT = guT

        def spatial(j):
            gi_ = j // 2
            b = 3 + (j % 2)
            for tb in range(NB):
                mm(PB[b][:, tb * 128:(tb + 1) * 128], vn[:, tb, j * 128:(j + 1) * 128], wsT[:, gi_, :], True, False,
                   ("FB2", "wsT"), ("PB%d" % b,))
                mm(PB[b][:, tb * 128:(tb + 1) * 128], L2p[0:2, j * 128:(j + 1) * 128],
                   ft[5][0:2, gi_ * 128:(gi_ + 1) * 128], False, True, ("L2p", "ft5"), ("PB%d" % b,))
            stt(yapreT[:, j, :], PB[b][:, :], lngF[:, j:j + 1], szaT[:, j, :], ALU.mult, ALU.mult,
                ("PB%d" % b, "small", "FB6"), ("FB5",))

        f_ln(0)
        f_ln(1)
        f_scan(0)
        for j in range(8):
            if j + 2 < 8:
                f_ln(j + 2)
            if j + 1 < 8:
                f_scan(j + 1)
            f_rest(j)
            spatial(j)

        if has_next:
            stage1_load(g + 1)

        maT = taaT
        for half in range(2):
            _, s = take_block(BLK_WA + half)
            for cb_ in range(4):
                jo = half * 4 + cb_
                b = next_pb()
                for kc in range(8):
                    mm(PB[b][:, :], wslot[s][:, kc, cb_ * 128:(cb_ + 1) * 128], yapreT[:, kc, :], kc == 0, kc == 7,
                       ("wslot%d" % s, "FB5"), ("PB%d" % b,))
                stt(maT[:, jo, :], taaT[:, jo, :], 1.0, PB[b][:, :], ALU.add, ALU.mult, ("FB4", "PB%d" % b), ("FB4",))
            release_block()

        if has_next:
            stage1_stats(g + 1)
        KT, KTk = T3(1), "FB1"
        for tb in range(NB):
            tbi = next_tb()
            for j in range(8):
                tr(TB[tbi][:, j * 128:(j + 1) * 128], KF[:, j, tb * 128:(tb + 1) * 128], (KFk,), ("TB%d" % tbi,))
            cp("act", KT[:, tb, :], TB[tbi][:, :], ("TB%d" % tbi,), (KTk,))
        szbT, tabT, V_T = T3(2), F3(5), T3(6)
        proj_T(G_ZB, lambda tb, half, bk, k: act(szbT[:, tb, half * 512:(half + 1) * 512], bk[:, :], AF.Silu, (k,),
                                                ("FB2",)))
        proj_F(G_AB, lambda j, bk, k: act(tabT[:, j, :], bk[:, :], AF.Tanh, (k,), ("FB5",), scale=0.5))
        proj_T(G_INP, lambda tb, half, bk, k: cp("dve", V_T[:, tb, half * 512:(half + 1) * 512], bk[:, :], (k,),
                                                 ("FB6",)))

        obT, obk = hT3, hTk
        pend = []

        def T_out(tb):
            on_, onk = hb[tb % 2], "hb%d" % (tb % 2)
            tbi = next_tb()
            for j in range(8):
                tr(TB[tbi][:, j * 128:(j + 1) * 128], on_[:, j * 128:(j + 1) * 128], (onk,), ("TB%d" % tbi,))
            cp("act", obT[:, :, tb * 128:(tb + 1) * 128], TB[tbi][:, :].rearrange("p (a b) -> p a b", a=8),
               ("TB%d" % tbi,), (obk,))

        for tb in range(NB):
            kA = kchunk["k"]
            kB = kA + 1
            kchunk["k"] += 2
            cA, cB = 2 * tb, 2 * tb + 1
            am, amk = Am[tb % 2], "Am%d" % (tb % 2)
            if has_next:
                stage1_scale(g + 1, tb)
            for hb_ in range(2):
                for jj in range(4):
                    j = hb_ * 4 + jj
                    mm(PB[hb_][:, jj * 128:(jj + 1) * 128], KF[:, j, tb * 128:(tb + 1) * 128], QF[:, j, tb, 0:3:2, :],
                       True, True, (KFk, "QF"), ("PB%d" % hb_,))
                tt("dve", am[:, hb_ * 4:(hb_ + 1) * 4, :], PB[hb_][:, :].rearrange("p (a b) -> p a b", a=4),
                   mbd[:, :].unsqueeze(1).broadcast_to([128, 4, 128]), ALU.mult, ("PB%d" % hb_, "mbd"), (amk,))
            for (c, k, lo) in ((cA, kA, 0), (cB, kB, 64)):
                for hb_ in range(2):
                    b = 4 + hb_
                    for jj in range(4):
                        j = hb_ * 4 + jj
                        mm(PB[b][:, jj * 128:(jj + 1) * 128], KT[lo:lo + 64, tb, j * 128:(j + 1) * 128],
                           V_T[lo:lo + 64, tb, j * 128:(j + 1) * 128], True, True, (KTk, "FB6"), ("PB%d" % b,))
                    sl = slice(hb_ * 512, (hb_ + 1) * 512)
                    sk_ = "Sst%d" % hb_
                    tt("dve", Sst[:, sl], PB[b][:, :], Sst[:, sl], ALU.add, ("PB%d" % b, sk_), (sk_,))
                    tt("dve", Sst[:, sl].rearrange("p (a b) -> p a b", a=4),
                       Sst[:, sl].rearrange("p (a b) -> p a b", a=4),
                       el[:, hb_ * 4:(hb_ + 1) * 4, c:c + 1].broadcast_to([128, 4, 128]), ALU.mult,
                       (sk_, "el"), (sk_,))
                kn = (k + 1) % 3
                cp("act", Sbf[kn][:, :], Sst[:, :], ("Sst0", "Sst1"), ("Sbf%d" % kn,))
                if lo == 0 and has_next:
                    stage1_tr(g + 1, tb)
            sA, sB = kA % 3, kB % 3
            for hb_ in range(2):
                b = 2 + hb_
                for jj in range(4):
                    j = hb_ * 4 + jj
                    o_ = PB[b][:, jj * 128:(jj + 1) * 128]
                    mm(o_, am[:, j, :], V_T[:, tb, j * 128:(j + 1) * 128], True, False, (amk, "FB6"), ("PB%d" % b,))
                    mm(o_, QF[:, j, tb, 0:2, :], Sbf[sA][:, j * 128:(j + 1) * 128], False, False,
                       ("QF", "Sbf%d" % sA), ("PB%d" % b,))
                    mm(o_, QF[:, j, tb, 1:3, :], Sbf[sB][:, j * 128:(j + 1) * 128], False, True,
                       ("QF", "Sbf%d" % sB), ("PB%d" % b,))
            oraw, ork = gvt[tb % 2], "gvt%d" % (tb % 2)
            for hb_ in range(2):
                cp("act", oraw[:, hb_ * 512:(hb_ + 1) * 512], PB[2 + hb_][:, :], ("PB%d" % (2 + hb_),), (ork,))
            c_s8 = statcol(8)
            for j in range(8):
                act(junk[:, 0:128], oraw[:, j * 128:(j + 1) * 128], AF.Square, (ork,), ("stat%d" % c_s8,),
                    accum_out=stat[:, c_s8 + j:c_s8 + j + 1])
            c_r8 = rsqrt_cols(c_s8, 8, 1.0 / 128.0, EPS, e="pool")
            tt("pool", oraw[:, :].rearrange("p (a b) -> p a b", a=8), oraw[:, :].rearrange("p (a b) -> p a b", a=8),
               stat[:, c_r8:c_r8 + 8].unsqueeze(2).broadcast_to([128, 8, 128]), ALU.mult,
               (ork, "stat%d" % c_r8), (ork,))
            on_, onk = hb[tb % 2], "hb%d" % (tb % 2)
            tt("pool", on_[:, :], oraw[:, :], szbT[:, tb, :], ALU.mult, (ork, "FB2"), (onk,))
            if tb == NB - 1 and has_next:
                hTn_, hTnk_ = HBUF[(g + 1) % 2]
                nsrc = (hTn_[:, :].rearrange("p (a b) -> p a b", a=8), hTnk_)
                proj_uza(1, nsrc)
            if pend:
                T_out(pend.pop())
            pend.append(tb)
        T_out(pend.pop())

        mgT, mgk = tabT, "FB5"
        for half in range(2):
            _, s = take_block(BLK_WB + half)
            for cb_ in range(4):
                jo = half * 4 + cb_
                b = next_pb()
                for kc in range(8):
                    mm(PB[b][:, :], wslot[s][:, kc, cb_ * 128:(cb_ + 1) * 128], obT[:, kc, :], kc == 0, kc == 7,
                       ("wslot%d" % s, obk), ("PB%d" % b,))
                f_, fk = ft[jo % 2], "ft%d" % (jo % 2)
                stt(f_[:, :], tabT[:, jo, :], 1.0, PB[b][:, :], ALU.add, ALU.mult, ("FB5", "PB%d" % b), (fk,))
                tt("dve", mgT[:, jo, :], f_[:, :], maT[:, jo, :], ALU.add, (fk, "FB4"), (mgk,))
            release_block()

        if has_next:
            dma(FB[2][:, :], wsc_d[2 * G_V], ("wsc%d" % (2 * G_V), "FB2"), ("FB2",))
            dma(hT[:, :], wsc_d[2 * G_V + 1], ("wsc%d" % (2 * G_V + 1), hTk), (hTk,))
            dma(FB[4][:, :], wsc_d[2 * G_U + 1], ("wsc%d" % (2 * G_U + 1), "FB4"), ("FB4",))

        if g == 0 and LAZY_CAST:
            assert not lazy
            stage1_load(0)
        _, s0 = take_block(BLK_WO, 0)
        _, s1_ = take_block(BLK_WO + 1, 1)
        x1b, x1bk = T3(3), "FB3"
        for tb in range(NB):
            banks = []
            c_ss2 = statcol(2)
            for half, s in ((0, s0), (1, s1_)):
                b = next_pb(6)
                banks.append(b)
                for kc in range(8):
                    mm(PB[b][:, :], mgT[:, kc, tb * 128:(tb + 1) * 128], wslot[s][:, kc, :], kc == 0, kc == 7,
                       ("wslot%d" % s, mgk), ("PB%d" % b,))
                act(junk[:, 0:512], PB[b][:, :], AF.Square, ("PB%d" % b,), ("stat%d" % c_ss2,),
                    accum_out=stat[:, c_ss2 + half:c_ss2 + half + 1])
            c_s1 = statcol(1)
            tt("dve", stat[:, c_s1:c_s1 + 1], stat[:, c_ss2:c_ss2 + 1], stat[:, c_ss2 + 1:c_ss2 + 2], ALU.add,
               ("stat%d" % c_ss2,), ("stat%d" % c_s1,))
            c_r1 = rsqrt_cols(c_s1, 1, 1.0 / D, 4.0 * EPS)
            for half in range(2):
                b = banks[half]
                f_, fk = ft[half], "ft%d" % half
                sl = slice(half * 512, (half + 1) * 512)
                stt(f_[:, :], PB[b][:, :], stat[:, c_r1:c_r1 + 1], gpost_b[:, sl], ALU.mult, ALU.mult,
                    ("PB%d" % b, "stat%d" % c_r1, "gpost_b"), (fk,))
                tt("dve", X3[:, tb, sl], X3[:, tb, sl], f_[:, :], ALU.add, XK(Xk, tb) + (fk,),
                   XK(Xk, tb))
            cp("act", x1b[:, tb, :], X3[:, tb, :], XK(Xk, tb), (x1bk,))
        release_block(2)
        if has_next:
            proj_uza(0, nsrc)
            make_gz()

        x1T, x1Tk = F3(1), "FB1"
        for tb in range(NB):
            tbi = next_tb()
            for kc in range(8):
                tr(TB[tbi][:, kc * 128:(kc + 1) * 128], x1b[:, tb, kc * 128:(kc + 1) * 128], (x1bk,),
                   ("TB%d" % tbi,))
            cp("act", x1T[:, :, tb * 128:(tb + 1) * 128], TB[tbi][:, :].rearrange("p (a b) -> p a b", a=8),
               ("TB%d" % tbi,), (x1Tk,))

        if has_next:
            proj_uza(2, nsrc)

        _, s0 = take_block(BLK_WG, 0)
        _, s1_ = take_block(BLK_WG + 1, 1)
        ple_c = {}

        def ple_head(tb):
            eg, egk = gvt[tb % 2], "gvt%d" % (tb % 2)
            c_ss2 = statcol(2)
            ple_c[tb] = c_ss2
            bg_, be_ = [], []
            for half, s in ((0, s0), (1, s1_)):
                sl = slice(half * 512, (half + 1) * 512)
                b = next_pb(6)
                bg_.append(b)
                for kc in range(8):
                    mm(PB[b][:, :], x1T[:, kc, tb * 128:(tb + 1) * 128], wslot[s][:, kc, :], kc == 0, False,
                       ("wslot%d" % s, x1Tk), ("PB%d" % b,))
                mm(PB[b][:, :], ones_row[0:1, :], bg_row[0:1, sl], False, True, ("ones_row", "bg_row"),
                   ("PB%d" % b,))
            for half in range(2):
                sl = slice(half * 512, (half + 1) * 512)
                b2 = next_pb(6)
                be_.append(b2)
                for kc in range(2):
                    mm(PB[b2][:, :], pT[:, kc, tb * 128:(tb + 1) * 128], wp[:, kc, sl], kc == 0, kc == 1,
                       ("pT", "wp"), ("PB%d" % b2,))
            for half in range(2):
                f_, fk = ft[half], "ft%d" % half
                act(f_[:, :], PB[bg_[half]][:, :], AF.Tanh, ("PB%d" % bg_[half],), (fk,), scale=0.5)
            for half in range(2):
                sl = slice(half * 512, (half + 1) * 512)
                f_, fk = ft[half], "ft%d" % half
                stt(eg[:, sl], f_[:, :], 1.0, PB[be_[half]][:, :], ALU.add, ALU.mult,
                    (fk, "PB%d" % be_[half], egk), (egk + "h%d" % half,))
            for half in range(2):
                sl = slice(half * 512, (half + 1) * 512)
                act(junk[:, 0:512], eg[:, sl], AF.Square, (egk + "h%d" % half,), ("stat%d" % c_ss2,),
                    accum_out=stat[:, c_ss2 + half:c_ss2 + half + 1])

        def ple_tail(tb):
            eg, egk = gvt[tb % 2], "gvt%d" % (tb % 2)
            c_ss2 = ple_c[tb]
            c_s1 = statcol(1)
            tt("dve", stat[:, c_s1:c_s1 + 1], stat[:, c_ss2:c_ss2 + 1], stat[:, c_ss2 + 1:c_ss2 + 2], ALU.add,
               ("stat%d" % c_ss2,), ("stat%d" % c_s1,))
            c_r1 = rsqrt_cols(c_s1, 1, 1.0 / D, 4.0 * EPS)
            stt(eg[:, :], eg[:, :], stat[:, c_r1:c_r1 + 1], gple_b[:, :], ALU.mult, ALU.mult,
                (egk + "h0", egk + "h1", "stat%d" % c_r1, "gple_b"), (egk + "h0", egk + "h1"))
            tt("dve", eg[:, :], eg[:, :], X3[:, tb, :], ALU.add, (egk + "h0", egk + "h1") + XK(Xk, tb),
               (egk + "h0", egk + "h1", egk))
            dma(y_d[r0 + tb * 128:r0 + (tb + 1) * 128, :], eg[:, :], (egk, egk + "h0", egk + "h1"), ("y",))

        ple_head(0)
        for tb in range(1, NB):
            ple_head(tb)
            ple_tail(tb - 1)
        ple_tail(NB - 1)
        release_block(2)

    for g in range(total_tiles):
        tile_body(g)

    P.emit(es)
    es.close()
    return nc


def _host_inputs(x, p, w_in, gmlp_ln_g, gmlp_ln_b, gmlp_w_s, gmlp_b_s, hgrn_lb_logits, hgrn_norm_g,
                 w_branch_a, w_branch_b, w_out, g_pre, g_post, w_ple, w_ple_gate, b_ple_gate, g_ple):
    f32 = np.float32

    def blocks(w):
        out = []
        for c0 in range(0, w.shape[1], 512):
            out.append(np.ascontiguousarray(w[:, c0:c0 + 512].reshape(8, 128, 512).transpose(1, 0, 2)).reshape(128, 4096))
        return out

    wb = blocks(np.asarray(w_in[0], f32)) + blocks(np.asarray(w_branch_a[0], f32)) + \
        blocks(np.asarray(w_branch_b[0], f32)) + blocks(np.asarray(w_out[0], f32)) + \
        blocks(np.asarray(w_ple_gate[0], f32))
    wblk = np.stack(wb, 0).astype(f32)

    def colF(v):
        return np.ascontiguousarray(np.asarray(v, f32).reshape(8, 128).T)

    tt_ = np.arange(128)
    triu = (tt_[:, None] <= tt_[None, :]).astype(f32)
    mbd = (triu * ((tt_[:, None] // CH) == (tt_[None, :] // CH))).astype(ml_dtypes.bfloat16)
    rm = np.ones((128, TT), f32)
    rm[:, ::CH] = 0.0
    common = {
        "wblk": wblk,
        "wp": np.ascontiguousarray(np.asarray(w_ple[0], f32).reshape(2, 128, D).transpose(1, 0, 2)).reshape(128, 2 * D),
        "gpreF": colF(g_pre[0]),
        "hgF": colF(np.asarray(hgrn_norm_g[0]).reshape(-1)),
        "lngF": colF(gmlp_ln_g[0]),
        "lnb": np.asarray(gmlp_ln_b[0], f32).reshape(1, D),
        "lng_row": np.asarray(gmlp_ln_g[0], f32).reshape(1, D),
        "bs": np.asarray(gmlp_b_s[0], f32).reshape(1, 512),
        "wsT": np.ascontiguousarray(np.asarray(gmlp_w_s[0], f32).transpose(2, 0, 1)).reshape(128, 512),
        "lbl": np.ascontiguousarray(np.asarray(hgrn_lb_logits, f32).reshape(2, 8, 128).transpose(2, 0, 1)).reshape(128, 16),
        "gpost_b": np.ascontiguousarray(np.broadcast_to(np.asarray(g_post[0], f32), (128, D))),
        "gple_b": np.ascontiguousarray(np.broadcast_to(np.asarray(g_ple[0], f32), (128, D))),
        "bg": np.asarray(b_ple_gate[0], f32).reshape(1, D),
        "ident": np.eye(128, dtype=f32).astype(ml_dtypes.bfloat16),
        "triu": triu,
        "mbd": mbd,
        "rm": rm,
    }
    return common


_CACHE = {}


def run(inputs, nseq, T, ncores, x_full, p_full):
    common = _host_inputs(**inputs)
    key = (nseq, T)
    nc = build(nseq, T)
    in_maps = []
    for c in range(ncores):
        m = dict(common)
        m["x"] = np.ascontiguousarray(x_full[c * nseq:(c + 1) * nseq].reshape(nseq * T, D))
        m["p"] = np.ascontiguousarray(p_full[c * nseq:(c + 1) * nseq].reshape(nseq * T, PLE))
        in_maps.append(m)
    res = run_bass_kernel_spmd(nc, in_maps, core_ids=list(range(ncores)))
    outs = [np.asarray(r["y"]).reshape(nseq, T, D) for r in res.results]
    return np.concatenate(outs, 0).astype(np.float32)


def kernel(**inputs):
    x = np.asarray(inputs["x"], np.float32)
    p = np.asarray(inputs["p"], np.float32)[0]
    B, T, _ = x.shape
    nseq = B // NCORES
    return run(inputs, nseq, T, NCORES, x, p)
```
